# Optimizing a Trainium2 kernel written in Bass

```python
import math
import jax, jax.numpy as jnp
from jax import lax
import numpy as np

D_MODEL = 1024
BATCH = 4
SEQ = 4096
DEPTH = 2
DEC_BATCH = 8
DEC_SEQ = 8192
PAST_LEN = 128

N_MEM = 256
N_MIXERS = 2
N_MLA_LAYERS = (DEPTH + 1) // 2
N_DIFF_LAYERS = DEPTH // 2

MLA_HEADS = 8
MLA_NOPE = 128
MLA_ROPE = 64
MLA_V = 128
Q_LORA = 256
KV_LORA = 256
ROPE_THETA = 10000.0

DIFF_HEADS = 8
DIFF_HD = D_MODEL // DIFF_HEADS // 2
DIFF_VD = 2 * DIFF_HD

XATTN_HEADS = 4
XATTN_HD = D_MODEL // XATTN_HEADS

D_FF = 4 * D_MODEL
N_BUCKETS = 32
MAX_DISTANCE = 128
Q_BLOCK = 128
EPS = 1e-6

kernel_name = 'hybrid_mla_diffattn_encoder'


def rms_norm(x, g):
    xf = x.astype(jnp.float32)
    y = xf * lax.rsqrt(jnp.mean(xf * xf, axis=-1, keepdims=True) + EPS)
    return (y * g.astype(jnp.float32)).astype(x.dtype)


def rope(x, pos):
    half = x.shape[-1] // 2
    freqs = ROPE_THETA ** (-jnp.arange(half, dtype=jnp.float32) / half)
    ang = pos.astype(jnp.float32)[:, None] * freqs[None, :]
    cos = jnp.cos(ang)[:, None, :]
    sin = jnp.sin(ang)[:, None, :]
    xf = x.astype(jnp.float32)
    x1, x2 = xf[..., :half], xf[..., half:]
    return jnp.concatenate([x1 * cos - x2 * sin, x1 * sin + x2 * cos], axis=-1).astype(x.dtype)


def t5_bucket(rel):
    nb = N_BUCKETS // 2
    max_exact = nb // 2
    ret = (rel > 0).astype(jnp.int32) * nb
    n = jnp.abs(rel)
    large = max_exact + (jnp.log(jnp.maximum(n, 1).astype(jnp.float32) / max_exact)
                         / math.log(MAX_DISTANCE / max_exact) * (nb - max_exact)).astype(jnp.int32)
    large = jnp.minimum(large, nb - 1)
    return ret + jnp.where(n < max_exact, n, large)


def sweep_query_blocks(block_fn, *qs):
    b, s = qs[0].shape[:2]
    nb = s // Q_BLOCK
    blocks = tuple(jnp.moveaxis(q.reshape(b, nb, Q_BLOCK, *q.shape[2:]), 1, 0) for q in qs)
    starts = jnp.arange(nb, dtype=jnp.int32) * Q_BLOCK
    out = lax.map(lambda a: block_fn(*a), (starts,) + blocks)
    out = jnp.moveaxis(out, 0, 1)
    return out.reshape(b, s, *out.shape[3:])


def mla_mixer(h, w_in, q_norm, kv_norm, w_uq, w_ukv, w_o):
    b, s, _ = h.shape
    proj = h @ w_in
    c_q, c_kv, k_r = jnp.split(proj, [Q_LORA, Q_LORA + KV_LORA], axis=-1)
    q = (rms_norm(c_q, q_norm) @ w_uq).reshape(b, s, MLA_HEADS, MLA_NOPE + MLA_ROPE)
    kv = (rms_norm(c_kv, kv_norm) @ w_ukv).reshape(b, s, MLA_HEADS, MLA_NOPE + MLA_V)
    q_nope, q_rope = q[..., :MLA_NOPE], q[..., MLA_NOPE:]
    k_nope, v = kv[..., :MLA_NOPE], kv[..., MLA_NOPE:]
    pos = jnp.arange(s, dtype=jnp.int32)
    q_rope = rope(q_rope, pos)
    k_rope = rope(k_r[:, :, None, :], pos)[:, :, 0, :]
    scale = (MLA_NOPE + MLA_ROPE) ** -0.5

    def block(start, qn, qr):
        sc = (jnp.einsum('bqhd,bkhd->bhqk', qn, k_nope)
              + jnp.einsum('bqhr,bkr->bhqk', qr, k_rope)).astype(jnp.float32) * scale
        p = jax.nn.softmax(sc, axis=-1).astype(v.dtype)
        return jnp.einsum('bhqk,bkhd->bqhd', p, v)

    o = sweep_query_blocks(block, q_nope, q_rope)
    return o.reshape(b, s, MLA_HEADS * MLA_V) @ w_o


def diff_mixer(h, layer_idx, rel_bias_table, w_in, lam, subln, w_o):
    b, s, _ = h.shape
    q, k, v = jnp.split(h @ w_in, 3, axis=-1)
    q = q.reshape(b, s, DIFF_HEADS, 2, DIFF_HD)
    k = k.reshape(b, s, DIFF_HEADS, 2, DIFF_HD)
    v = v.reshape(b, s, DIFF_HEADS, DIFF_VD)
    lambda_init = 0.8 - 0.6 * math.exp(-0.3 * layer_idx)
    lf = lam.astype(jnp.float32)
    lam_full = jnp.exp(jnp.sum(lf[0] * lf[1])) - jnp.exp(jnp.sum(lf[2] * lf[3])) + lambda_init
    scale = DIFF_HD ** -0.5
    kpos = jnp.arange(s, dtype=jnp.int32)

    def block(start, qb):
        sc = jnp.einsum('bqhcd,bkhcd->bchqk', qb, k).astype(jnp.float32) * scale
        rel = kpos[None, :] - (start + jnp.arange(Q_BLOCK, dtype=jnp.int32))[:, None]
        bias = jnp.moveaxis(rel_bias_table[t5_bucket(rel)], -1, 0).astype(jnp.float32)
        p = jax.nn.softmax(sc + bias, axis=-1)
        a = (p[:, 0] - lam_full * p[:, 1]).astype(v.dtype)
        return jnp.einsum('bhqk,bkhd->bqhd', a, v)

    o = sweep_query_blocks(block, q)
    o = (rms_norm(o, subln) * (1.0 - lambda_init)).astype(h.dtype)
    return o.reshape(b, s, DIFF_HEADS * DIFF_VD) @ w_o


def memory_cross_attention(h, mem, mem_norm, w_q, w_kv, w_o):
    b, s, _ = h.shape
    m = rms_norm(mem, mem_norm)
    q = (h @ w_q).reshape(b, s, XATTN_HEADS, XATTN_HD)
    k, v = jnp.split(m @ w_kv, 2, axis=-1)
    k = k.reshape(b, N_MEM, XATTN_HEADS, XATTN_HD)
    v = v.reshape(b, N_MEM, XATTN_HEADS, XATTN_HD)
    sc = jnp.einsum('bqhd,bkhd->bhqk', q, k).astype(jnp.float32) * (XATTN_HD ** -0.5)
    p = jax.nn.softmax(sc, axis=-1).astype(v.dtype)
    o = jnp.einsum('bhqk,bkhd->bqhd', p, v).reshape(b, s, D_MODEL)
    return o @ w_o


def squared_relu_mlp(h, w1, w2):
    return jnp.square(jax.nn.relu(h @ w1)) @ w2


def trunk(x, mem, norm_gains, rel_bias_table,
          mla_w_in, mla_q_norm, mla_kv_norm, mla_w_uq, mla_w_ukv, mla_w_o,
          diff_w_in, diff_lambda, diff_subln, diff_w_o,
          xattn_mem_norm, xattn_w_q, xattn_w_kv, xattn_w_o,
          mlp_w1, mlp_w2):
    for i in range(DEPTH):
        g = norm_gains[i]
        j = i // N_MIXERS
        h = rms_norm(x, g[0])
        if i % N_MIXERS == 0:
            h = mla_mixer(h, mla_w_in[j], mla_q_norm[j], mla_kv_norm[j],
                          mla_w_uq[j], mla_w_ukv[j], mla_w_o[j])
        else:
            h = diff_mixer(h, i, rel_bias_table, diff_w_in[j], diff_lambda[j],
                           diff_subln[j], diff_w_o[j])
        x = x + rms_norm(h, g[1])
        h = memory_cross_attention(rms_norm(x, g[2]), mem, xattn_mem_norm[i],
                                   xattn_w_q[i], xattn_w_kv[i], xattn_w_o[i])
        x = x + rms_norm(h, g[3])
        h = squared_relu_mlp(rms_norm(x, g[4]), mlp_w1[i], mlp_w2[i])
        x = x + rms_norm(h, g[5])
    return x


def setup_inputs(seed: int = 0) -> dict:
    key = jax.random.key(seed)
    ks = jax.random.split(key, 24)
    f32 = jnp.float32

    def w(k, shape, fan_in):
        return jax.random.normal(k, shape, f32) * (fan_in ** -0.5)

    def gain(k, shape):
        return 1.0 + 0.01 * jax.random.normal(k, shape, f32)

    na, nd = N_MLA_LAYERS, N_DIFF_LAYERS
    return {
        'x_prompt': jax.random.normal(ks[0], (BATCH, SEQ, D_MODEL), f32),
        'x_sample': jax.random.normal(ks[1], (DEC_BATCH, DEC_SEQ, D_MODEL), f32),
        'mem_prompt': jax.random.normal(ks[2], (BATCH, N_MEM, D_MODEL), f32),
        'mem_sample': jax.random.normal(ks[3], (DEC_BATCH, N_MEM, D_MODEL), f32),
        'norm_gains': gain(ks[4], (DEPTH, 6, D_MODEL)),
        'rel_bias_table': 0.1 * jax.random.normal(ks[5], (N_BUCKETS, DIFF_HEADS), f32),
        'mla_w_in': w(ks[6], (na, D_MODEL, Q_LORA + KV_LORA + MLA_ROPE), D_MODEL),
        'mla_q_norm': gain(ks[7], (na, Q_LORA)),
        'mla_kv_norm': gain(ks[8], (na, KV_LORA)),
        'mla_w_uq': w(ks[9], (na, Q_LORA, MLA_HEADS * (MLA_NOPE + MLA_ROPE)), Q_LORA),
        'mla_w_ukv': w(ks[10], (na, KV_LORA, MLA_HEADS * (MLA_NOPE + MLA_V)), KV_LORA),
        'mla_w_o': w(ks[11], (na, MLA_HEADS * MLA_V, D_MODEL), MLA_HEADS * MLA_V),
        'diff_w_in': w(ks[12], (nd, D_MODEL, 3 * D_MODEL), D_MODEL),
        'diff_lambda': 0.1 * jax.random.normal(ks[13], (nd, 4, DIFF_HD), f32),
        'diff_subln': gain(ks[14], (nd, DIFF_VD)),
        'diff_w_o': w(ks[15], (nd, DIFF_HEADS * DIFF_VD, D_MODEL), DIFF_HEADS * DIFF_VD),
        'xattn_mem_norm': gain(ks[16], (DEPTH, D_MODEL)),
        'xattn_w_q': w(ks[17], (DEPTH, D_MODEL, D_MODEL), D_MODEL),
        'xattn_w_kv': w(ks[18], (DEPTH, D_MODEL, 2 * D_MODEL), D_MODEL),
        'xattn_w_o': w(ks[19], (DEPTH, D_MODEL, D_MODEL), D_MODEL),
        'mlp_w1': w(ks[20], (DEPTH, D_MODEL, D_FF), D_MODEL),
        'mlp_w2': w(ks[21], (DEPTH, D_FF, D_MODEL), D_FF),
    }


def reference(x_prompt, x_sample, mem_prompt, mem_sample, norm_gains, rel_bias_table,
              mla_w_in, mla_q_norm, mla_kv_norm, mla_w_uq, mla_w_ukv, mla_w_o,
              diff_w_in, diff_lambda, diff_subln, diff_w_o,
              xattn_mem_norm, xattn_w_q, xattn_w_kv, xattn_w_o,
              mlp_w1, mlp_w2):
    y_prompt = trunk(x_prompt, mem_prompt, norm_gains, rel_bias_table,
                     mla_w_in, mla_q_norm, mla_kv_norm, mla_w_uq, mla_w_ukv, mla_w_o,
                     diff_w_in, diff_lambda, diff_subln, diff_w_o,
                     xattn_mem_norm, xattn_w_q, xattn_w_kv, xattn_w_o,
                     mlp_w1, mlp_w2)
    y_sample = trunk(x_sample, mem_sample, norm_gains, rel_bias_table,
                     mla_w_in, mla_q_norm, mla_kv_norm, mla_w_uq, mla_w_ukv, mla_w_o,
                     diff_w_in, diff_lambda, diff_subln, diff_w_o,
                     xattn_mem_norm, xattn_w_q, xattn_w_kv, xattn_w_o,
                     mlp_w1, mlp_w2)
    return (y_prompt, y_sample)
```

```python
import math
from contextlib import ExitStack

import numpy as np
import concourse.bass as bass
import concourse.mybir as mybir
from concourse.bass_utils import run_bass_kernel_spmd

F32 = mybir.dt.float32
BF16 = mybir.dt.bfloat16
AF = mybir.ActivationFunctionType
ALU = mybir.AluOpType

import os as _os
SAME_ENGINE_SYNC = set(_os.environ.get('SES', 'act,dve,pool').split(','))
ENGS = ("pe", "act", "dve", "pool", "sp")

D = 1024
NC8 = 8
TT = 512
NMEM = 256
EPS = 1e-6
GW = 1152
GL = 1280


class Ins:
    __slots__ = ("eng", "fn", "deps", "needed", "val", "dsem", "flushed")

    def __init__(self, eng, fn, dsem=None):
        self.eng = eng
        self.fn = fn
        self.deps = ()
        self.needed = False
        self.val = 0
        self.dsem = dsem
        self.flushed = False


class DSem:
    def __init__(self, handle, name):
        self.handle = handle
        self.name = name
        self.count = 0
        self.last = None
        self.queue = None


class BufState:
    __slots__ = ("w", "r")

    def __init__(self):
        self.w = None
        self.r = {}


class Prog:
    def __init__(self, nc, stack):
        self.nc = nc
        self.stack = stack
        self.esem = {}
        self.ecnt = {e: 0 for e in ENGS}
        for e in ("pe", "act", "dve", "pool"):
            self.esem[e] = stack.enter_context(nc.semaphore("es_" + e))
        self.bufs = {}
        self.streams = {e: [] for e in ENGS}
        self.waited = {e: {} for e in ENGS}
        self.dsems = []
        self.nins = 0
        self.nwait = 0

    def dsem(self, name):
        h = self.stack.enter_context(self.nc.semaphore("ds_" + name))
        d = DSem(h, name)
        self.dsems.append(d)
        return d

    def _record(self, ins, reads, writes):
        deps = {}
        for b in reads:
            st = self.bufs.get(b)
            if st is not None and st.w is not None:
                deps[id(st.w)] = st.w
        for b in writes:
            st = self.bufs.get(b)
            if st is not None:
                if st.w is not None:
                    deps[id(st.w)] = st.w
                for r in st.r.values():
                    deps[id(r)] = r
        if ins.dsem is not None and ins.dsem.last is not None:
            deps[id(ins.dsem.last)] = ins.dsem.last
        out = []
        for d in deps.values():
            if d is ins:
                continue
            if d.flushed and d.dsem is None:
                continue
            if d.dsem is None and ins.dsem is None and d.eng == ins.eng:
                if d.eng == "pe" or d.eng not in SAME_ENGINE_SYNC:
                    continue
            d.needed = True
            out.append(d)
        ins.deps = out
        rk = ins.eng if ins.dsem is None else ("d", id(ins.dsem))
        for b in reads:
            st = self.bufs.get(b)
            if st is None:
                st = self.bufs[b] = BufState()
            st.r[rk] = ins
        for b in writes:
            st = self.bufs.get(b)
            if st is None:
                st = self.bufs[b] = BufState()
            st.w = ins
            st.r = {}
        if ins.dsem is not None:
            ins.dsem.last = ins
        self.streams[ins.eng].append(ins)
        return ins

    def op(self, eng, fn, reads=(), writes=()):
        return self._record(Ins(eng, fn), reads, writes)

    def dma(self, queue, dsem, out, in_, reads=(), writes=(), **kw):
        assert dsem.queue in (None, queue)
        dsem.queue = queue
        fn = lambda e: e.dma_start(out=out, in_=in_, **kw)
        return self._record(Ins(queue, fn, dsem=dsem), reads, writes)

    def wait_all(self, eng, inss):
        ins = Ins(eng, None)
        out = []
        for d in inss:
            if d is None:
                continue
            d.needed = True
            out.append(d)
        ins.deps = out
        self.streams[eng].append(ins)
        return ins

    def flush(self):
        nc = self.nc
        for e in ENGS:
            for ins in self.streams[e]:
                if ins.dsem is not None:
                    ins.dsem.count += 1
                    ins.val = 16 * ins.dsem.count
                elif ins.needed and ins.fn is not None:
                    self.ecnt[e] += 1
                    ins.val = self.ecnt[e]

        def replay(ename, eng):
            waited = self.waited[ename]
            for ins in self.streams[ename]:
                for d in ins.deps:
                    if d.dsem is not None:
                        key = ("d", id(d.dsem))
                        h = d.dsem.handle
                    else:
                        key = d.eng
                        h = self.esem[d.eng]
                    assert d.val > 0, (ename, d.eng)
                    if waited.get(key, 0) < d.val:
                        eng.wait_ge(h, d.val)
                        waited[key] = d.val
                        self.nwait += 1
                if ins.fn is None:
                    continue
                i = ins.fn(eng)
                self.nins += 1
                if ins.dsem is not None:
                    i.then_inc(ins.dsem.handle, 16)
                elif ins.needed:
                    i.then_inc(self.esem[ename], 1)

        with nc.Block() as block:
            @block.tensor
            def _(t):
                replay("pe", t)

            @block.scalar
            def _(s):
                replay("act", s)

            @block.vector
            def _(v):
                replay("dve", v)

            @block.gpsimd
            def _(g):
                replay("pool", g)

            @block.sync
            def _(s):
                replay("sp", s)

        for e in ENGS:
            for ins in self.streams[e]:
                ins.flushed = True
        self.streams = {e: [] for e in ENGS}


class Builder:
    def __init__(self, SA, SB):
        self.SA, self.SB = SA, SB
        self.HB = SB // 2
        self.nc = bass.Bass("TRN2", target_bir_lowering=False)
        self.rr = 0

    def un(self, name):
        self.uid = getattr(self, "uid", 0) + 1
        return "%s_u%d" % (name, self.uid)

    def din(self, name, shape, dt=F32):
        return self.nc.dram_tensor(name, list(shape), dt, kind="ExternalInput").ap()

    def dout(self, name, shape, dt=F32):
        return self.nc.dram_tensor(name, list(shape), dt, kind="ExternalOutput").ap()

    def dscr(self, name, shape, dt):
        import os
        kind = "ExternalOutput" if os.environ.get("KDEBUG") else "Internal"
        return self.nc.dram_tensor(name, list(shape), dt, kind=kind).ap()

    def build(self):
        nc = self.nc
        SA, SB, HB = self.SA, self.SB, self.HB
        I = self.I = {}
        I["xA"] = self.din("xA", [SA, D])
        I["xB"] = self.din("xB", [SB, D])
        I["memA"] = self.din("memA", [NMEM, D])
        I["memB"] = self.din("memB", [NMEM, D])
        I["gcols"] = self.din("gcols", [128, 96])
        I["memg"] = self.din("memg", [128, 16])
        I["qkvg"] = self.din("qkvg", [128, 4])
        I["subg"] = self.din("subg", [128, 1])
        I["lam"] = self.din("lam", [1, 256])
        I["table"] = self.din("table", [32, 8])
        I["ohA"] = self.din("ohA", [32, GL])
        I["ohB"] = self.din("ohB", [32, GL])
        I["ident"] = self.din("ident", [128, 128])
        I["anti"] = self.din("anti", [128, 128])
        I["cosA"] = self.din("cosA", [64, SA])
        I["sinA"] = self.din("sinA", [64, SA])
        I["cosB"] = self.din("cosB", [64, SB])
        I["sinB"] = self.din("sinB", [64, SB])
        I["mla_w_in"] = self.din("mla_w_in", [D, 576])
        I["mla_w_uq"] = self.din("mla_w_uq", [256, 1536])
        I["mla_w_ukv"] = self.din("mla_w_ukv", [256, 2048])
        I["mla_w_o"] = self.din("mla_w_o", [D, D])
        I["diff_w_in"] = self.din("diff_w_in", [D, 3 * D])
        I["diff_w_o"] = self.din("diff_w_o", [D, D])
        I["xattn_w_q"] = self.din("xattn_w_q", [2, D, D])
        I["xattn_w_kv"] = self.din("xattn_w_kv", [2, D, 2 * D])
        I["xattn_w_o"] = self.din("xattn_w_o", [2, D, D])
        I["mlp_w1"] = self.din("mlp_w1", [2, D, 4 * D])
        I["mlp_w2"] = self.din("mlp_w2", [2, 4 * D, D])
        self.yA = self.dout("yA", [SA, D])
        self.yB = self.dout("yB", [HB, D])
        S = self.S = {}
        for s, L in (("A", SA), ("B", SB)):
            S["XT" + s] = self.dscr("XT" + s, [D, L], F32)
            S["Q" + s] = self.dscr("Q" + s, [D, L], BF16)
            S["K" + s] = self.dscr("K" + s, [D, L], BF16)
            S["QR" + s] = self.dscr("QR" + s, [8, 64, L], BF16)
            S["KR" + s] = self.dscr("KR" + s, [64, L], BF16)
            S["V" + s] = self.dscr("V" + s, [L, D], BF16)
            S["O" + s] = self.dscr("O" + s, [L, D], BF16)
            S["strip" + s] = self.dscr("strip" + s, [128, 8, GW], BF16)
            S["gv" + s] = self.dscr("gv" + s, [8, GL], F32)
            for l in range(2):
                S["mK%d%s" % (l, s)] = self.dscr("mK%d%s" % (l, s), [128, 8, NMEM], BF16)
                S["mV%d%s" % (l, s)] = self.dscr("mV%d%s" % (l, s), [128, 2, D], BF16)
        S["w_mla_o"] = self.dscr("w_mla_o", [D, D], BF16)
        S["w_diff_o"] = self.dscr("w_diff_o", [D, D], BF16)
        S["w_diff_in"] = self.dscr("w_diff_in", [D, 3 * D], BF16)
        S["w_xq"] = self.dscr("w_xq", [2, D, D], BF16)
        S["w_xo"] = self.dscr("w_xo", [2, D, D], BF16)
        S["w_1"] = self.dscr("w_1", [2, D, 4 * D], BF16)
        S["w_2"] = self.dscr("w_2", [2, 4 * D, D], BF16)

        with ExitStack() as top:
            self.P = P = Prog(nc, top)
            sbt = lambda name, shape, dt: top.enter_context(nc.sbuf_tensor(self.un(name), list(shape), dt))
            C = self.C = {}
            C["identf"] = sbt("identf", [128, 128], F32)
            C["identb"] = sbt("identb", [128, 128], BF16)
            C["antif"] = sbt("antif", [128, 128], F32)
            C["ones1"] = sbt("ones1", [128, 128], BF16)
            C["onesD"] = sbt("onesD", [128, 128], BF16)
            C["ones256"] = sbt("ones256", [128, 128], BF16)
            C["onesf"] = sbt("onesf", [128, 128], F32)
            C["gcols"] = sbt("gcols_s", [128, 96], F32)
            C["memg"] = sbt("memg_s", [128, 16], F32)
            C["qkvg"] = sbt("qkvg_s", [128, 4], F32)
            C["subs"] = sbt("subs_s", [128, 1], F32)
            C["nlam"] = sbt("nlam_s", [128, 1], F32)
            C["epsc"] = sbt("epsc", [128, 1], F32)
            C["cst"] = sbt("cst_s", [128, 2, 8, 2], F32)
            self.nds = 0
            self.dspool = {}
            self.dsrr = {}

            import os
            stop = int(os.environ.get("STOP_AFTER", "99"))
            phases = [self.phase_prologue, self.phase_l0_p1, lambda: self.phase_attn(0), lambda: self.phase_tail(0),
                      lambda: self.phase_attn(1), lambda: self.phase_tail(1)]
            for i, ph in enumerate(phases):
                if i > stop:
                    break
                ph()
            print("program: instrs", self.P.nins, "waits", self.P.nwait, "dsems", len(self.P.dsems), flush=True)
        return nc

    def newds(self, name):
        self.nds += 1
        if name in ("c", "wc"):
            pool = self.dspool.setdefault(name, [])
            if len(pool) < 4:
                pool.append(self.P.dsem("%s%d" % (name, self.nds)))
                return pool[-1]
            self.dsrr[name] = self.dsrr.get(name, 0) + 1
            return pool[self.dsrr[name] % 4]
        return self.P.dsem("%s%d" % (name, self.nds))

    def evac_eng(self):
        self.rr += 1
        return "act" if (self.rr & 1) else "dve"

    def copy(self, eng, out, in_, reads, writes):
        if eng == "act":
            return self.P.op("act", lambda e: e.copy(out=out, in_=in_), reads, writes)
        return self.P.op(eng, lambda e: e.tensor_copy(out=out, in_=in_), reads, writes)

    def mm(self, out, pairs, reads, writes):
        n = len(pairs)
        last = None
        for i, (l, r) in enumerate(pairs):
            last = self.P.op("pe", (lambda e, l=l, r=r, a=(i == 0), b=(i == n - 1):
                                    e.matmul(out, lhsT=l, rhs=r, start=a, stop=b)), reads, writes)
        return last

    def rstd_from_ms(self, ms_ps, ms_key, lnv, rstd, key, width=TT):
        P, C = self.P, self.C
        P.op("act", lambda e: e.activation(out=lnv[:, 0:width], in_=ms_ps, func=AF.Ln, bias=C["epsc"][:], scale=1.0),
             [ms_key, "epsc"], [key + "_ln"])
        P.op("act", lambda e: e.activation(out=rstd[:, 0:width], in_=lnv[:, 0:width], func=AF.Exp, scale=-0.5),
             [key + "_ln"], [key])

    def phase_prologue(self):
        nc, P, C, I, S = self.nc, self.P, self.C, self.I, self.S
        with ExitStack() as st:
            sbt = lambda name, shape, dt: st.enter_context(nc.sbuf_tensor(self.un(name), list(shape), dt))
            pst = lambda name, shape, dt: st.enter_context(nc.psum_tensor(self.un(name), list(shape), dt))
            ld = self.newds("c")
            P.dma("sp", ld, C["identf"][:], I["ident"][:, :], writes=["identf"])
            ld2 = self.newds("c")
            P.dma("sp", ld2, C["antif"][:], I["anti"][:, :], writes=["antif"])
            for nm in ("gcols", "memg", "qkvg"):
                P.dma("sp", self.newds("c"), C[nm][:], I[nm][:, :], writes=[nm])
            P.op("act", lambda e: e.copy(out=C["identb"][:], in_=C["identf"][:]), ["identf"], ["identb"])
            P.op("dve", lambda e: e.memset(C["ones1"][:], 1.0), [], ["ones1"])
            P.op("dve", lambda e: e.memset(C["onesD"][:], 1.0 / D), [], ["onesD"])
            P.op("dve", lambda e: e.memset(C["ones256"][:], 1.0 / 256), [], ["ones256"])
            P.op("dve", lambda e: e.memset(C["onesf"][:], 1.0), [], ["onesf"])
            P.op("dve", lambda e: e.memset(C["epsc"][:], EPS), [], ["epsc"])
            lam_init = 0.8 - 0.6 * math.exp(-0.3 * 1)
            subg = sbt("subg", [128, 1], F32)
            P.dma("sp", self.newds("c"), subg[:], I["subg"][:, :], writes=["subg"])
            P.op("dve", lambda e: e.tensor_scalar(out=C["subs"][:], in0=subg[:], scalar1=float(1.0 - lam_init),
                                                   scalar2=None, op0=ALU.mult), ["subg"], ["subs"])
            lam = sbt("lam", [1, 256], F32)
            lj = sbt("lamj", [1, 64], F32)
            ls = sbt("lams", [1, 4], F32)
            P.dma("sp", self.newds("c"), lam[:], I["lam"][:, :], writes=["lam"])
            P.op("dve", lambda e: e.memset(ls[:], 0.0), [], ["ls"])
            P.op("dve", lambda e: e.scalar_tensor_tensor(out=lj[:], in0=lam[:, 0:64], scalar=1.0, in1=lam[:, 64:128],
                                                          op0=ALU.mult, op1=ALU.mult, accum_out=ls[:, 0:1]),
                 ["lam", "ls"], ["lj", "ls"])
            P.op("dve", lambda e: e.scalar_tensor_tensor(out=lj[:], in0=lam[:, 128:192], scalar=1.0, in1=lam[:, 192:256],
                                                          op0=ALU.mult, op1=ALU.mult, accum_out=ls[:, 1:2]),
                 ["lam", "ls", "lj"], ["lj", "ls"])
            P.op("act", lambda e: e.activation(out=ls[:, 2:4], in_=ls[:, 0:2], func=AF.Exp), ["ls"], ["ls"])
            P.op("dve", lambda e: e.tensor_tensor(out=ls[:, 0:1], in0=ls[:, 3:4], in1=ls[:, 2:3], op=ALU.subtract),
                 ["ls"], ["ls"])
            P.op("dve", lambda e: e.tensor_scalar(out=ls[:, 1:2], in0=ls[:, 0:1], scalar1=float(-lam_init),
                                                   scalar2=None, op0=ALU.add), ["ls"], ["ls"])
            pb = pst("pb", [128, 512], F32)
            P.op("pe", lambda e: e.matmul(pb[:, 0:1], lhsT=C["onesf"][0:1, :], rhs=ls[0:1, 1:2], start=True, stop=True),
                 ["onesf", "ls"], ["pb"])
            P.op("dve", lambda e: e.tensor_copy(out=C["nlam"][:], in_=pb[:, 0:1]), ["pb"], ["nlam"])

            wdo = sbt("wdo", [128, 8, D], F32)
            wdb = sbt("wdb", [128, 8, D], BF16)
            P.dma("sp", self.newds("c"), wdo[:], I["diff_w_o"].rearrange("(c p) n -> p c n", p=128), writes=["wdo"])
            for c in range(8):
                P.op("dve", lambda e, c=c: e.tensor_scalar(out=wdb[:, c, :], in0=wdo[:, c, :], scalar1=C["subs"][:],
                                                            scalar2=None, op0=ALU.mult), ["wdo", "subs"], ["wdb"])
            P.dma("sp", self.newds("c"), S["w_diff_o"].rearrange("(c p) n -> p c n", p=128), wdb[:],
                  reads=["wdb"], writes=[("scr", "w_diff_o")])

            tab = sbt("tab", [32, 8], F32)
            P.dma("sp", self.newds("c"), tab[:], I["table"][:, :], writes=["tab"])
            oh = sbt("oh", [32, GL], F32)
            gvs = sbt("gvs", [8, GL], F32)
            hk = sbt("hk", [128, 8, GW], F32)
            stp = sbt("stp", [128, 8, GW], BF16)
            pg = [pst("pg%d" % i, [128, 512], F32) for i in range(3)]
            for si, s in enumerate(("A", "B")):
                P.dma("sp", self.newds("c"), oh[:], I["oh" + s][:, :], writes=["oh"])
                for j, (a, b) in enumerate(((0, 512), (512, 1024), (1024, GL))):
                    P.op("pe", lambda e, j=j, a=a, b=b: e.matmul(pg[j][0:8, 0:b - a], lhsT=tab[:, :], rhs=oh[:, a:b],
                                                                 start=True, stop=True), ["tab", "oh"], [("pg", j)])
                    P.op("dve", lambda e, j=j, a=a, b=b: e.tensor_copy(out=gvs[:, a:b], in_=pg[j][0:8, 0:b - a]),
                         [("pg", j)], ["gvs"])
                P.dma("sp", self.newds("c"), S["gv" + s][:, :], gvs[:], reads=["gvs"], writes=[("scr", "gv" + s)])
                src = bass.AP(S["gv" + s].tensor, 0, [[1, 128], [GL, 8], [1, GW]])
                P.dma("sp", self.newds("c"), hk[:], src, reads=[("scr", "gv" + s)], writes=["hk"])
                for h in range(8):
                    for j, (a, b) in enumerate(((0, 512), (512, 1024), (1024, GW))):
                        P.op("pe", lambda e, h=h, j=j, a=a, b=b: e.matmul(pg[j][:, 0:b - a], lhsT=C["antif"][:],
                                                                          rhs=hk[:, h, a:b], start=True, stop=True),
                             ["antif", "hk"], [("pg", j)])
                        P.op("dve", lambda e, h=h, j=j, a=a, b=b: e.tensor_scalar(out=stp[:, h, a:b], in0=pg[j][:, 0:b - a],
                                                                                  scalar1=8.0, scalar2=None, op0=ALU.mult),
                             [("pg", j)], ["stp"])
                        if j == 0:
                            P.op("dve", lambda e, h=h, si=si: e.tensor_copy(out=C["cst"][:, si, h, 1:2], in_=pg[0][:, 0:1]),
                                 [("pg", 0)], ["cst"])
                        if j == 2:
                            P.op("dve", lambda e, h=h, si=si: e.tensor_copy(out=C["cst"][:, si, h, 0:1],
                                                                            in_=pg[2][:, GW - 1025:GW - 1024]),
                                 [("pg", 2)], ["cst"])
                P.dma("sp", self.newds("c"), S["strip" + s][:, :, :], stp[:], reads=["stp"],
                      writes=[("scr", "strip" + s)])

            self.end_phase()
        with ExitStack() as st:
            sbt = lambda name, shape, dt: st.enter_context(nc.sbuf_tensor(self.un(name), list(shape), dt))
            pst = lambda name, shape, dt: st.enter_context(nc.psum_tensor(self.un(name), list(shape), dt))
            wkv = sbt("wkv", [128, 8, 2 * D], BF16)
            mem = sbt("mem", [128, 2, D], F32)
            memT = sbt("memT", [128, 8, NMEM], F32)
            msq = sbt("msq", [128, 8, NMEM], BF16)
            mn = sbt("mn", [128, 8, NMEM], BF16)
            lnv = sbt("plnv", [128, 512], F32)
            rstd = sbt("prstd", [128, 512], F32)
            mko = sbt("mko", [128, 8, NMEM], BF16)
            mvo = sbt("mvo", [128, 2, D], BF16)
            pm = [pst("pm%d" % i, [128, 512], F32) for i in range(3)]
            pms = pst("pms", [128, 512], F32)
            k = 0
            for l in range(2):
                P.dma("pool", self.newds("wkv"), wkv[:], I["xattn_w_kv"][l].rearrange("(c p) n -> p c n", p=128),
                      writes=["wkv"])
                for s in ("A", "B"):
                    P.dma("sp", self.newds("c"), mem[:], I["mem" + s].rearrange("(j p) n -> p j n", p=128), writes=["mem"])
                    for c in range(8):
                        b = pm[k % 3]; bk = ("pm", k % 3); k += 1
                        for j in range(2):
                            P.op("pe", lambda e, b=b, c=c, j=j: e.transpose(b[:, j * 128:(j + 1) * 128],
                                                                            mem[:, j, c * 128:(c + 1) * 128], C["identf"][:]),
                                 ["mem", "identf"], [bk])
                        P.op("act", lambda e, b=b, c=c: e.copy(out=memT[:, c, :], in_=b[:, 0:NMEM]), [bk], [("memT", c)])
                        P.op("dve", lambda e, c=c: e.tensor_tensor(out=msq[:, c, :], in0=memT[:, c, :], in1=memT[:, c, :],
                                                                   op=ALU.mult), [("memT", c)], [("msq", c)])
                    self.mm(pms[:, 0:NMEM], [(C["onesD"][:], msq[:, c, :]) for c in range(8)],
                            ["onesD"] + [("msq", c) for c in range(8)], ["pms"])
                    self.rstd_from_ms(pms[:, 0:NMEM], "pms", lnv, rstd, "prstd", width=NMEM)
                    for c in range(8):
                        P.op("dve", lambda e, c=c, l=l: e.scalar_tensor_tensor(
                            out=mn[:, c, :], in0=memT[:, c, :], scalar=C["memg"][:, l * 8 + c:l * 8 + c + 1],
                            in1=rstd[:, 0:NMEM], op0=ALU.mult, op1=ALU.mult),
                            [("memT", c), "memg", "prstd"], [("mn", c)])
                    for mc in range(8):
                        b = pm[k % 3]; bk = ("pm", k % 3); k += 1
                        self.mm(b[:, 0:NMEM], [(wkv[:, c, mc * 128:(mc + 1) * 128], mn[:, c, :]) for c in range(8)],
                                ["wkv"] + [("mn", c) for c in range(8)], [bk])
                        self.copy(self.evac_eng(), mko[:, mc, :], b[:, 0:NMEM], [bk], ["mko"])
                    for tcn in range(2):
                        for hf in range(2):
                            b = pm[k % 3]; bk = ("pm", k % 3); k += 1
                            self.mm(b[:, :], [(mn[:, c, tcn * 128:(tcn + 1) * 128],
                                               wkv[:, c, D + hf * 512:D + (hf + 1) * 512]) for c in range(8)],
                                    ["wkv"] + [("mn", c) for c in range(8)], [bk])
                            self.copy(self.evac_eng(), mvo[:, tcn, hf * 512:(hf + 1) * 512], b[:, :], [bk], ["mvo"])
                    P.dma("sp", self.newds("c"), S["mK%d%s" % (l, s)][:, :, :], mko[:], reads=["mko"],
                          writes=[("scr", "mK%d%s" % (l, s))])
                    P.dma("sp", self.newds("c"), S["mV%d%s" % (l, s)][:, :, :], mvo[:], reads=["mvo"],
                          writes=[("scr", "mV%d%s" % (l, s))])
            self.end_phase()

    def emit_casts(self):
        P, I, S = self.P, self.I, self.S
        def cast(dst, src, rows):
            ncol = src.shape[1]
            for r0 in range(0, rows, 256):
                for c0 in range(0, ncol, 1024):
                    P.dma("pool", self.newds("wc"), dst[r0:r0 + 256, c0:c0 + 1024], src[r0:r0 + 256, c0:c0 + 1024],
                          writes=[("scrw", dst.tensor.name, self.nds)])
        cast(S["w_mla_o"], I["mla_w_o"], D)
        cast(S["w_diff_in"], I["diff_w_in"], D)
        for l in range(2):
            cast(S["w_xq"][l], I["xattn_w_q"][l], D)
            cast(S["w_xo"][l], I["xattn_w_o"][l], D)
            cast(S["w_1"][l], I["mlp_w1"][l], D)
            cast(S["w_2"][l], I["mlp_w2"][l], 4 * D)

    def end_phase(self, skip_casts=False):
        P = self.P
        casts = set(id(d) for d in self.dspool.get("wc", [])) if skip_casts else set()
        P.wait_all("sp", [d.last for d in P.dsems if d.queue == "sp" and d.last is not None and not d.last.flushed])
        P.wait_all("pool", [d.last for d in P.dsems if d.queue == "pool" and d.last is not None and not d.last.flushed
                            and id(d) not in casts])
        P.flush()

    def phase_l0_p1(self):
        nc, P, C, I, S = self.nc, self.P, self.C, self.I, self.S
        with ExitStack() as st:
            sbt = lambda name, shape, dt: st.enter_context(nc.sbuf_tensor(self.un(name), list(shape), dt))
            pst = lambda name, shape, dt: st.enter_context(nc.psum_tensor(self.un(name), list(shape), dt))
            w_in = sbt("w_in", [128, 8, 640], BF16)
            w_uq = sbt("w_uq", [128, 2, 1536], BF16)
            w_qrh = sbt("w_qrh", [128, 2, 512], BF16)
            w_ukv = sbt("w_ukv", [128, 2, 2048], BF16)
            P.dma("pool", self.newds("w"), w_in[:, :, 0:576], I["mla_w_in"].rearrange("(c p) n -> p c n", p=128),
                  writes=["w_in"])
            P.dma("pool", self.newds("w"), w_uq[:], I["mla_w_uq"].rearrange("(c p) n -> p c n", p=128), writes=["w_uq"])
            P.dma("pool", self.newds("w"), w_ukv[:], I["mla_w_ukv"].rearrange("(c p) n -> p c n", p=128), writes=["w_ukv"])
            self.emit_casts()
            P.op("act", lambda e: e.mul(out=w_in[:, :, 576:608], in_=w_in[:, :, 544:576], mul=-1.0),
                 ["w_in"], ["w_in"])
            P.op("act", lambda e: e.copy(out=w_in[:, :, 608:640], in_=w_in[:, :, 512:544]), ["w_in"], ["w_in"])
            uq4 = w_uq[:].rearrange("p c (h f) -> p c h f", h=8)
            rh4 = w_qrh[:].rearrange("p c (h f) -> p c h f", h=8)
            for c in range(2):
                P.op("act", lambda e, c=c: e.mul(out=rh4[:, c, :, 0:32], in_=uq4[:, c, :, 160:192], mul=-1.0),
                     ["w_uq"], ["w_qrh"])
                P.op("act", lambda e, c=c: e.copy(out=rh4[:, c, :, 32:64], in_=uq4[:, c, :, 128:160]), ["w_uq"], ["w_qrh"])
            xin = sbt("xin", [128, 4, D], F32)
            xT = sbt("xT", [128, 8, TT], F32)
            sq = [sbt("sq%d" % i, [128, TT], BF16) for i in range(2)]
            hn = sbt("hn", [128, 8, TT], BF16)
            lnv = sbt("lnv", [128, TT], F32)
            rstd = sbt("rstd", [128, TT], F32)
            cst = sbt("cstage", [128, 4, TT], F32)
            cn = sbt("cn", [128, 4, TT], BF16)
            cs = sbt("cosb", [64, TT], F32)
            sn = sbt("sinb", [64, TT], F32)
            tmp = [sbt("rt%d" % i, [64, TT], F32) for i in range(2)]
            qst = sbt("qst", [128, 8, TT], BF16)
            kst = sbt("kst", [128, 8, TT], BF16)
            qrs = sbt("qrs", [64, 8, TT], BF16)
            krs = sbt("krs", [64, TT], BF16)
            vst = sbt("vst", [128, 4, D], BF16)
            pm = [pst("p1m%d" % i, [128, 512], F32) for i in range(6)]
            pms = pst("p1s", [128, 512], F32)
            ds_x = self.newds("x"); ds_cs = self.newds("cs"); ds_sn = self.newds("sn")
            ds_o = [self.newds("o") for _ in range(7)]
            k = 0

            def bank():
                nonlocal k
                b = pm[k % 6]; bk = ("p1m", k % 6); k += 1
                return b, bk

            deferred = []

            def defer(dsem, out, in_, reads=(), writes=()):
                deferred.append((dsem, out, in_, reads, writes))

            def flush_stores():
                for (dsem, out, in_, reads, writes) in deferred:
                    P.dma("sp", dsem, out, in_, reads=reads, writes=writes)
                del deferred[:]

            for s, L, x in (("A", self.SA, I["xA"]), ("B", self.SB, I["xB"])):
                for t in range(L // TT):
                    t0 = t * TT
                    P.dma("sp", ds_x, xin[:], x[t0:t0 + TT, :].rearrange("(j p) n -> p j n", p=128), writes=["xin"])
                    P.dma("sp", ds_cs, cs[:], I["cos" + s][:, t0:t0 + TT], writes=["cs"])
                    P.dma("sp", ds_sn, sn[:], I["sin" + s][:, t0:t0 + TT], writes=["sn"])
                    flush_stores()
                    for c in range(8):
                        b, bk = bank()
                        for j in range(4):
                            P.op("pe", lambda e, b=b, c=c, j=j: e.transpose(b[:, j * 128:(j + 1) * 128],
                                                                            xin[:, j, c * 128:(c + 1) * 128], C["identf"][:]),
                                 ["xin", "identf"], [bk])
                        self.copy(self.evac_eng(), xT[:, c, :], b[:, :], [bk], [("xT", c)])
                        P.op("act", lambda e, c=c: e.activation(out=sq[c % 2][:], in_=xT[:, c, :], func=AF.Square),
                             [("xT", c)], [("sq", c % 2)])
                        P.op("pe", lambda e, c=c: e.matmul(pms[:, :], lhsT=C["onesD"][:], rhs=sq[c % 2][:], start=(c == 0),
                                                           stop=(c == 7)), ["onesD", ("sq", c % 2)], ["pms"])
                    defer(ds_o[0], S["XT" + s][:, t0:t0 + TT].rearrange("(c p) n -> p c n", p=128), xT[:],
                          reads=[("xT", c) for c in range(8)], writes=[("scr", "XT" + s)])
                    self.rstd_from_ms(pms[:, :], "pms", lnv, rstd, "rstd")
                    for c in range(8):
                        P.op("dve", lambda e, c=c: e.scalar_tensor_tensor(
                            out=hn[:, c, :], in0=xT[:, c, :], scalar=C["gcols"][:, c:c + 1], in1=rstd[:],
                            op0=ALU.mult, op1=ALU.mult), [("xT", c), "gcols", "rstd"], [("hn", c)])
                    hnr = [("hn", c) for c in range(8)]
                    for grp in range(2):
                        for m in range(2):
                            mc = grp * 2 + m
                            b, bk = bank()
                            self.mm(b[:, :], [(w_in[:, c, mc * 128:(mc + 1) * 128], hn[:, c, :]) for c in range(8)],
                                    ["w_in"] + hnr, [bk])
                            P.op("act", lambda e, b=b, mc=mc: e.copy(out=cst[:, mc, :], in_=b[:, :]), [bk], [("cst", mc)])
                            P.op("dve", lambda e, mc=mc: e.tensor_tensor(out=sq[mc % 2][:], in0=cst[:, mc, :],
                                                                         in1=cst[:, mc, :], op=ALU.mult),
                                 [("cst", mc)], [("sq", mc % 2)])
                            P.op("pe", lambda e, mc=mc, m=m: e.matmul(pms[:, :], lhsT=C["ones256"][:], rhs=sq[mc % 2][:],
                                                                      start=(m == 0), stop=(m == 1)),
                                 ["ones256", ("sq", mc % 2)], ["pms"])
                        self.rstd_from_ms(pms[:, :], "pms", lnv, rstd, "rstd")
                        for m in range(2):
                            mc = grp * 2 + m
                            P.op("dve", lambda e, mc=mc: e.scalar_tensor_tensor(
                                out=cn[:, mc, :], in0=cst[:, mc, :], scalar=C["qkvg"][:, mc:mc + 1], in1=rstd[:],
                                op0=ALU.mult, op1=ALU.mult), [("cst", mc), "qkvg", "rstd"], [("cn", mc)])

                    def rope(psr, psh, kr_, kh_, out_ap, outkey):
                        P.op("dve", lambda e: e.tensor_tensor(out=tmp[0][:], in0=psr, in1=cs[:], op=ALU.mult),
                             [kr_, "cs"], ["rt0"])
                        P.op("dve", lambda e: e.tensor_tensor(out=tmp[1][:], in0=psh, in1=sn[:], op=ALU.mult),
                             [kh_, "sn"], ["rt1"])
                        P.op("dve", lambda e: e.tensor_tensor(out=out_ap, in0=tmp[0][:], in1=tmp[1][:], op=ALU.add),
                             ["rt0", "rt1"], [outkey])

                    b1, bk1 = bank()
                    self.mm(b1[0:64, :], [(w_in[:, c, 512:576], hn[:, c, :]) for c in range(8)], ["w_in"] + hnr, [bk1])
                    b2, bk2 = bank()
                    self.mm(b2[0:64, :], [(w_in[:, c, 576:640], hn[:, c, :]) for c in range(8)], ["w_in"] + hnr, [bk2])
                    rope(b1[0:64, :], b2[0:64, :], bk1, bk2, krs[:], "krs")
                    defer(ds_o[1], S["KR" + s][:, t0:t0 + TT], krs[:], reads=["krs"], writes=[("scr", "KR" + s)])
                    cq = [("cn", 0), ("cn", 1)]
                    ckv = [("cn", 2), ("cn", 3)]
                    for h in range(8):
                        b, bk = bank()
                        self.mm(b[:, :], [(w_uq[:, c, h * 192:h * 192 + 128], cn[:, c, :]) for c in range(2)],
                                ["w_uq"] + cq, [bk])
                        self.copy(self.evac_eng(), qst[:, h, :], b[:, :], [bk], [("qst", h)])
                        b1, bk1 = bank()
                        self.mm(b1[0:64, :], [(w_uq[:, c, h * 192 + 128:h * 192 + 192], cn[:, c, :]) for c in range(2)],
                                ["w_uq"] + cq, [bk1])
                        b2, bk2 = bank()
                        self.mm(b2[0:64, :], [(w_qrh[:, c, h * 64:(h + 1) * 64], cn[:, c, :]) for c in range(2)],
                                ["w_qrh"] + cq, [bk2])
                        rope(b1[0:64, :], b2[0:64, :], bk1, bk2, qrs[:, h, :], ("qrs", h))
                        b, bk = bank()
                        self.mm(b[:, :], [(w_ukv[:, c, h * 256:h * 256 + 128], cn[:, 2 + c, :]) for c in range(2)],
                                ["w_ukv"] + ckv, [bk])
                        self.copy(self.evac_eng(), kst[:, h, :], b[:, :], [bk], [("kst", h)])
                    defer(ds_o[2], S["Q" + s][:, t0:t0 + TT].rearrange("(h p) n -> p h n", p=128), qst[:],
                          reads=[("qst", h) for h in range(8)], writes=[("scr", "Q" + s)])
                    defer(ds_o[3], S["K" + s][:, t0:t0 + TT].rearrange("(h p) n -> p h n", p=128), kst[:],
                          reads=[("kst", h) for h in range(8)], writes=[("scr", "K" + s)])
                    defer(ds_o[4], S["QR" + s][:, :, t0:t0 + TT].rearrange("h p n -> p h n"), qrs[:],
                          reads=[("qrs", h) for h in range(8)], writes=[("scr", "QR" + s)])
                    wv4 = w_ukv[:].rearrange("p c (h f) -> p c h f", h=8)
                    for j in range(4):
                        for hf in range(2):
                            b, bk = bank()
                            self.mm(b[:, :].rearrange("p (h f) -> p h f", h=4),
                                    [(cn[:, 2 + c, j * 128:(j + 1) * 128], wv4[:, c, hf * 4:(hf + 1) * 4, 128:256])
                                     for c in range(2)], ["w_ukv"] + ckv, [bk])
                            self.copy(self.evac_eng(), vst[:, j, hf * 512:(hf + 1) * 512], b[:, :], [bk], [("vst", j)])
                    defer(ds_o[5], S["V" + s][t0:t0 + TT, :].rearrange("(j p) n -> p j n", p=128), vst[:],
                          reads=[("vst", j) for j in range(4)], writes=[("scr", "V" + s)])
            flush_stores()
            self.end_phase(skip_casts=True)

    def phase_attn(self, layer):
        nc, P, C, I, S = self.nc, self.P, self.C, self.I, self.S
        mla = (layer == 0)
        nbr = 1 if mla else 2
        with ExitStack() as st:
            sbt = lambda name, shape, dt: st.enter_context(nc.sbuf_tensor(self.un(name), list(shape), dt))
            pst = lambda name, shape, dt: st.enter_context(nc.psum_tensor(self.un(name), list(shape), dt))
            Lmax = self.SA
            nkmax = Lmax // 128
            Kt = [sbt("Kt%d" % i, [128, Lmax], BF16) for i in range(2)]
            Qt = [sbt("Qt%d" % i, [128, Lmax], BF16) for i in range(2)]
            Vt = [sbt("Vt%d" % i, [128, nkmax, 129], BF16) for i in range(2)]
            if mla:
                QRt = [sbt("QRt%d" % i, [128, Lmax], BF16) for i in range(2)]
                KRt = sbt("KRt", [128, Lmax], BF16)
                nsc, nacc = 0, 2
                scp = [pst("scp%d" % i, [128, 1024], F32) for i in range(2)]
                etp = [sbt("etp%d" % i, [128, 1024], BF16) for i in range(3)]
            else:
                stp = sbt("stp", [128, 8, GW], BF16)
                nsc, nacc = 0, 1
                scd = [pst("scd%d" % i, [128, 1024], F32) for i in range(2)]
                etd = [sbt("etd%d" % i, [128, 1024], BF16) for i in range(3)]
            for i in range(2):
                P.op("pool", lambda e, i=i: e.memset(Vt[i][:, :, 128:129], 1.0), [], [("Vt1", i)])
            ne = 4
            ost = [sbt("ost%d" % i, [128, 4, 128], BF16) for i in range(2)]
            rinv = [sbt("rinv%d" % i, [128, 8], F32) for i in range(2)]
            if not mla:
                tq = [sbt("tq%d" % i, [128, 128], F32) for i in range(2)]
                accs = [[sbt("accs%d_%d" % (br, hb), [128, 512], F32) for hb in range(2)] for br in range(2)]
                od = [sbt("od%d" % i, [128, 4, 128], F32) for i in range(2)]
                junk = sbt("junk", [128, 128], F32)
                ss = [sbt("ss%d" % i, [128, 4], F32) for i in range(2)]
                ssl = [sbt("ssl%d" % i, [128, 4], F32) for i in range(2)]
                eps128 = sbt("eps128", [128, 1], F32)
                P.op("dve", lambda e: e.memset(eps128[:], EPS), [], ["eps128"])
            sc = [[pst("sc%d_%d" % (br, i), [128, 512], F32) for i in range(nsc)] for br in range(nbr)]
            acc = [[[pst("acc%d_%d_%d" % (br, a, i), [128, 512], F32) for i in range(2)] for a in range(nacc)]
                   for br in range(nbr)]
            dsl = [[self.newds("al") for _ in range(4)] for _ in range(2)]
            dso = [self.newds("ao") for _ in range(2)]
            dkr = self.newds("akr")
            dkr2 = self.newds("akr2")
            dsl2 = [self.newds("al2") for _ in range(2)]
            scale = (192 ** -0.5) if mla else (64 ** -0.5)
            hidx = 0
            qbi = 0
            for si, (s, L) in enumerate((("A", self.SA), ("B", self.SB))):
                nk = L // 128
                Lq = L if (mla or s == "A") else self.HB
                nq = Lq // 512
                if mla:
                    P.dma("sp", dkr, KRt[0:64, 0:L], S["KR" + s][:, :], reads=[("scr", "KR" + s)], writes=["KRt"])
                    P.dma("sp", dkr2, KRt[64:128, 0:L], S["KR" + s][:, :], reads=[("scr", "KR" + s)], writes=["KRt2"])
                else:
                    P.dma("sp", dkr, stp[:], S["strip" + s][:, :, :], reads=[("scr", "strip" + s)], writes=["stp"])

                def load_head(h, hp):
                    P.dma("sp", dsl[hp][0], Kt[hp][:, 0:L], S["K" + s][h * 128:(h + 1) * 128, :],
                          reads=[("scr", "K" + s)], writes=[("Kt", hp)])
                    P.dma("sp", dsl[hp][1], Qt[hp][:, 0:Lq], S["Q" + s][h * 128:(h + 1) * 128, 0:Lq],
                          reads=[("scr", "Q" + s)], writes=[("Qt", hp)])
                    P.dma("sp", dsl[hp][2], Vt[hp][:, 0:nk, 0:128],
                          S["V" + s][:, h * 128:(h + 1) * 128].rearrange("(k p) d -> p k d", p=128),
                          reads=[("scr", "V" + s)], writes=[("Vt", hp)])
                    if mla:
                        P.dma("sp", dsl[hp][3], QRt[hp][0:64, 0:L], S["QR" + s][h, :, :],
                              reads=[("scr", "QR" + s)], writes=[("QRt", hp)])
                        P.dma("sp", dsl2[hp], QRt[hp][64:128, 0:L], S["QR" + s][h, :, :],
                              reads=[("scr", "QR" + s)], writes=[("QRt2", hp)])

                load_head(0, hidx % 2)
                for h in range(8):
                    hp = hidx % 2
                    if h + 1 < 8:
                        load_head(h + 1, (hidx + 1) % 2)
                    rd_k = [("Kt", hp)] + (["KRt"] if mla else [])
                    rd_q = [("Qt", hp)] + ([("QRt", hp)] if mla else [])
                    tiles = [(qb, kc) for qb in range(nq) for kc in range(nk)]
                    pend = None
                    ti = 0

                    def emit_pv(qb, kc, slot, aset):
                        for br in range(nbr):
                            for j in range(4):
                                a = acc[br][aset][j // 2]
                                P.op("pe", lambda e, a=a, j=j, br=br, slot=slot, kc=kc, hp=hp,
                                     st_=(kc == 0 and j % 2 == 0), sp_=(kc == nk - 1 and j % 2 == 1): e.matmul(
                                    a[:, (j % 2) * 256:(j % 2) * 256 + 129],
                                    lhsT=etd[slot][:, br * 512 + j * 128:br * 512 + (j + 1) * 128], rhs=Vt[hp][:, kc, :],
                                    start=st_, stop=sp_, skip_group_check=True),
                                    [("ed", slot), ("Vt", hp), ("Vt1", hp)], [("acc", br, aset, j // 2)])

                    def emit_epilogue(qb, aset):
                        nonlocal qbi
                        op_ = qbi % 2
                        qbi += 1
                        o = ost[op_]
                        if mla:
                            for j in range(4):
                                a = acc[0][aset][j // 2]
                                c0 = (j % 2) * 256
                                P.op("dve", lambda e, a=a, c0=c0, j=j: e.reciprocal(out=rinv[op_][:, j:j + 1],
                                                                                   in_=a[:, c0 + 128:c0 + 129]),
                                     [("acc", 0, aset, j // 2)], [("rinv", op_, j)])
                                P.op("dve", lambda e, a=a, c0=c0, j=j: e.tensor_scalar(
                                    out=o[:, j, :], in0=a[:, c0:c0 + 128], scalar1=rinv[op_][:, j:j + 1], scalar2=None,
                                    op0=ALU.mult), [("acc", 0, aset, j // 2), ("rinv", op_, j)], [("ost", op_)])
                        else:
                            for br in range(2):
                                for hb in range(2):
                                    P.op("dve", lambda e, br=br, hb=hb: e.tensor_copy(out=accs[br][hb][:, 0:385],
                                                                                      in_=acc[br][aset][hb][:, 0:385]),
                                         [("acc", br, aset, hb)], [("accs", br, hb)])
                            P.op("dve", lambda e: e.memset(ss[op_][:], 0.0), [], [("ss", op_, j) for j in range(4)])
                            for j in range(4):
                                a0 = accs[0][j // 2]
                                a1 = accs[1][j // 2]
                                c0 = (j % 2) * 256
                                P.op("dve", lambda e, a0=a0, c0=c0, j=j: e.reciprocal(out=rinv[op_][:, j:j + 1],
                                                                                     in_=a0[:, c0 + 128:c0 + 129]),
                                     [("accs", 0, j // 2)], [("rinv", op_, j)])
                                P.op("dve", lambda e, a1=a1, c0=c0, j=j: e.reciprocal(out=rinv[op_][:, 4 + j:5 + j],
                                                                                     in_=a1[:, c0 + 128:c0 + 129]),
                                     [("accs", 1, j // 2)], [("rinv", op_, 4 + j)])
                                P.op("dve", lambda e, j=j: e.tensor_tensor(out=rinv[op_][:, 4 + j:5 + j],
                                                                           in0=rinv[op_][:, 4 + j:5 + j], in1=C["nlam"][:],
                                                                           op=ALU.mult),
                                     [("rinv", op_, 4 + j), "nlam"], [("rinv", op_, 4 + j)])
                                P.op("dve", lambda e, a0=a0, c0=c0, j=j: e.tensor_scalar(
                                    out=tq[j % 2][:], in0=a0[:, c0:c0 + 128], scalar1=rinv[op_][:, j:j + 1], scalar2=None,
                                    op0=ALU.mult), [("accs", 0, j // 2), ("rinv", op_, j)], [("tq", j % 2)])
                                P.op("dve", lambda e, a1=a1, c0=c0, j=j: e.scalar_tensor_tensor(
                                    out=od[op_][:, j, :], in0=a1[:, c0:c0 + 128], scalar=rinv[op_][:, 4 + j:5 + j],
                                    in1=tq[j % 2][:], op0=ALU.mult, op1=ALU.add),
                                    [("accs", 1, j // 2), ("rinv", op_, 4 + j), ("tq", j % 2)], [("od", op_, j)])
                                P.op("dve", lambda e, j=j: e.scalar_tensor_tensor(
                                    out=junk[:], in0=od[op_][:, j, :], scalar=1.0 / 128, in1=od[op_][:, j, :],
                                    op0=ALU.mult, op1=ALU.mult, accum_out=ss[op_][:, j:j + 1]),
                                    [("od", op_, j), ("ss", op_, j)], ["junk", ("ss", op_, j)])
                            ssk = [("ss", op_, j) for j in range(4)]
                            P.op("act", lambda e: e.activation(out=ssl[op_][:], in_=ss[op_][:], func=AF.Ln, bias=eps128[:],
                                                               scale=1.0), ssk + ["eps128"], [("ssl", op_)])
                            P.op("act", lambda e: e.activation(out=ssl[op_][:], in_=ssl[op_][:], func=AF.Exp, scale=-0.5),
                                 [("ssl", op_)], [("ssl", op_)])
                            for j in range(4):
                                P.op("dve", lambda e, j=j: e.tensor_scalar(out=o[:, j, :], in0=od[op_][:, j, :],
                                                                           scalar1=ssl[op_][:, j:j + 1], scalar2=None,
                                                                           op0=ALU.mult),
                                     [("od", op_, j), ("ssl", op_)], [("ost", op_)])
                        q0 = qb * 512
                        P.dma("pool", dso[op_],
                              S["O" + s][q0:q0 + 512, h * 128:(h + 1) * 128].rearrange("(j p) d -> p j d", p=128), o[:],
                              reads=[("ost", op_)], writes=[("scr", "O" + s)])

                    if mla:
                        pairs = [(qb, kp) for qb in range(nq) for kp in range(nk // 2)]
                        ppend = None

                        def emit_pv_pair(qb, kp, slot, aset):
                            for half in range(2):
                                kc = 2 * kp + half
                                for j in range(4):
                                    a = acc[0][aset][j // 2]
                                    P.op("pe", lambda e, a=a, j=j, slot=slot, kc=kc, hp=hp, half=half,
                                         st_=(kc == 0 and j % 2 == 0), sp_=(kc == nk - 1 and j % 2 == 1): e.matmul(
                                        a[:, (j % 2) * 256:(j % 2) * 256 + 129],
                                        lhsT=etp[slot][:, half * 512 + j * 128:half * 512 + (j + 1) * 128],
                                        rhs=Vt[hp][:, kc, :], start=st_, stop=sp_, skip_group_check=True),
                                        [("ep", slot), ("Vt", hp), ("Vt1", hp)], [("acc", 0, aset, j // 2)])

                        for pi_, (qb, kp) in enumerate(pairs):
                            slot_s = pi_ % 2
                            slot_e = pi_ % 3
                            aset = qb % nacc
                            q0 = qb * 512
                            scb = scp[slot_s]
                            for half in range(2):
                                kc = 2 * kp + half
                                P.op("pe", lambda e, scb=scb, kc=kc, q0=q0, hp=hp, half=half: e.matmul(
                                    scb[:, half * 512:(half + 1) * 512], lhsT=Kt[hp][:, kc * 128:(kc + 1) * 128],
                                    rhs=Qt[hp][:, q0:q0 + 512], start=True, stop=False),
                                    [("Kt", hp), ("Qt", hp)], [("scp", slot_s)])
                            for half in range(2):
                                kc = 2 * kp + half
                                r0 = half * 64
                                P.op("pe", lambda e, scb=scb, kc=kc, q0=q0, hp=hp, half=half, r0=r0: e.matmul(
                                    scb[:, half * 512:(half + 1) * 512], lhsT=KRt[r0:r0 + 64, kc * 128:(kc + 1) * 128],
                                    rhs=QRt[hp][r0:r0 + 64, q0:q0 + 512], start=False, stop=True),
                                    ["KRt", "KRt2", ("QRt", hp), ("QRt2", hp)], [("scp", slot_s)])
                            eb = etp[slot_e]
                            P.op("act", lambda e, eb=eb, scb=scb: e.activation(out=eb[:], in_=scb[:, :], func=AF.Exp,
                                                                               scale=float(scale)),
                                 [("scp", slot_s)], [("ep", slot_e)])
                            if ppend is not None:
                                emit_pv_pair(*ppend)
                                if ppend[1] == nk // 2 - 1:
                                    emit_epilogue(ppend[0], ppend[3])
                            ppend = (qb, kp, slot_e, aset)
                        emit_pv_pair(*ppend)
                        emit_epilogue(ppend[0], ppend[3])
                        hidx += 1
                        continue
                    for (qb, kc) in tiles:
                        slot_s = ti % 2
                        slot_e = ti % 3
                        aset = 0
                        q0 = qb * 512
                        m = kc - 4 * qb
                        scb = scd[slot_s]
                        near = (-1 <= m <= 4)
                        for br in range(2):
                            r0 = br * 64
                            P.op("pe", lambda e, scb=scb, kc=kc, q0=q0, r0=r0, near=near, hp=hp, br=br: e.matmul(
                                scb[:, br * 512:(br + 1) * 512], lhsT=Kt[hp][r0:r0 + 64, kc * 128:(kc + 1) * 128],
                                rhs=Qt[hp][r0:r0 + 64, q0:q0 + 512], start=True, stop=(not near)),
                                rd_k + rd_q, [("scd", slot_s)])
                        if near:
                            off = 512 - 128 * m
                            for br in range(2):
                                P.op("pe", lambda e, scb=scb, off=off, h=h, br=br: e.matmul(
                                    scb[:, br * 512:(br + 1) * 512], lhsT=C["identb"][:], rhs=stp[:, h, off:off + 512],
                                    start=False, stop=True), ["identb", "stp"], [("scd", slot_s)])
                            bias_ap = None
                        else:
                            bias_ap = C["cst"][:, si, h, 0:1] if m < -1 else C["cst"][:, si, h, 1:2]
                        eb = etd[slot_e]
                        if bias_ap is None:
                            P.op("act", lambda e, eb=eb, scb=scb: e.activation(out=eb[:], in_=scb[:, :], func=AF.Exp,
                                                                               scale=float(scale)),
                                 [("scd", slot_s)], [("ed", slot_e)])
                        else:
                            P.op("act", lambda e, eb=eb, scb=scb, bias_ap=bias_ap: e.activation(
                                out=eb[:], in_=scb[:, :], func=AF.Exp, bias=bias_ap, scale=float(scale)),
                                [("scd", slot_s), "cst"], [("ed", slot_e)])
                        if pend is not None:
                            emit_pv(*pend)
                            if pend[1] == nk - 1:
                                emit_epilogue(pend[0], pend[3])
                        pend = (qb, kc, slot_e, aset)
                        ti += 1
                    emit_pv(*pend)
                    emit_epilogue(pend[0], pend[3])
                    hidx += 1
            self.end_phase()

    def phase_tail(self, layer):
        nc, P, C, I, S = self.nc, self.P, self.C, self.I, self.S
        last = (layer == 1)
        with ExitStack() as st:
            sbt = lambda name, shape, dt: st.enter_context(nc.sbuf_tensor(self.un(name), list(shape), dt))
            pst = lambda name, shape, dt: st.enter_context(nc.psum_tensor(self.un(name), list(shape), dt))
            NSLOT = 3
            ring = [sbt("ring%d" % i, [128, 8, D], BF16) for i in range(NSLOT)]
            dsr = [self.newds("r") for _ in range(NSLOT)]
            xT = [sbt("xT%d" % i, [128, 8, TT], F32) for i in range(2)]
            oin = sbt("oin", [128, 4, D], BF16)
            hn = sbt("hn", [128, 8, TT], BF16)
            ys = sbt("ys", [128, 8, TT], F32)
            hid = sbt("hid", [128, 32, TT], BF16)
            sq8 = sbt("sq8", [128, 8, TT], BF16)
            rl = [sbt("rl%d" % i, [128, TT], F32) for i in range(4)]
            lnv = sbt("lnv", [128, TT], F32)
            lnx = sbt("lnx", [128, TT], F32)
            rstd = sbt("rstd", [128, TT], F32)
            rinv = sbt("rinv", [128, TT], F32)
            ex = [sbt("ex%d" % i, [128, 2, TT], BF16) for i in range(2)]
            mK = sbt("mK", [128, 8, NMEM], BF16)
            mV = sbt("mV", [128, 2, D], BF16)
            yout = ys[:].rearrange("p (j a) n -> p j (a n)", j=4)
            pm = [pst("ptm%d" % i, [128, 512], F32) for i in range(5)]
            pms = pst("pts", [128, 512], F32)
            ptrs = [pst("ptr%d" % i, [128, 1024], BF16) for i in range(2)]
            pmx = pms
            ds_x = [self.newds("tx") for _ in range(2)]
            ds_oi = self.newds("toi")
            ds_m = [self.newds("tm") for _ in range(2)]
            ds_st = [self.newds("ts") for _ in range(5)]
            k = 0

            def bank():
                nonlocal k
                b = pm[k % 5]; bk = ("ptm", k % 5); k += 1
                return b, bk

            g0 = layer * 48
            slices = []
            w_o = S["w_mla_o"] if layer == 0 else S["w_diff_o"]
            slices.append(("wo", w_o))
            slices.append(("xq", S["w_xq"][layer]))
            slices.append(("xo", S["w_xo"][layer]))
            for i in range(4):
                slices.append(("w1_%d" % i, S["w_1"][layer][:, i * D:(i + 1) * D]))
            for i in range(4):
                slices.append(("w2_%d" % i, S["w_2"][layer][i * D:(i + 1) * D, :]))
            if not last:
                for i in range(3):
                    slices.append(("win_%d" % i, S["w_diff_in"][:, i * D:(i + 1) * D]))
            wname = {"wo": ("scr", w_o.tensor.name), "xq": ("scr", "w_xq"), "xo": ("scr", "w_xo")}
            self._ring_seq = []
            state = {"issued": 0, "used": 0}
            plan = []

            for s, L in (("A", self.SA), ("B", self.SB)):
                Lq = L if (not last or s == "A") else self.HB
                for t in range(Lq // TT):
                    own = (s == "A") or (t * TT < self.HB)
                    sl = list(slices)
                    if (not last) and (not own):
                        sl = [x for x in sl if x[0] != "win_0"]
                    plan.append((s, t, own, sl))
            flat = [(pi, sname, ap) for pi, (_, _, _, sl) in enumerate(plan) for (sname, ap) in sl]

            def issue_next():
                i = state["issued"]
                if i >= len(flat):
                    return
                _, sname, ap = flat[i]
                slot = i % NSLOT
                P.dma("sp", dsr[slot], ring[slot][:], ap.rearrange("(c p) n -> p c n", p=128),
                      reads=[("scr", ap.tensor.name)], writes=[("ring", slot)])
                state["issued"] += 1

            def next_slice(expect):
                i = state["used"]
                assert flat[i][1] == expect, (flat[i][1], expect)
                while state["issued"] < min(len(flat), i + NSLOT):
                    issue_next()
                state["used"] += 1
                slot = i % NSLOT
                return ring[slot], ("ring", slot)

            def load_tile_inputs(pi):
                s, t, own, _ = plan[pi]
                par = pi % 2
                t0 = t * TT
                P.dma("sp", ds_x[par], xT[par][:], S["XT" + s][:, t0:t0 + TT].rearrange("(c p) n -> p c n", p=128),
                      reads=[("scr", "XT" + s, t)], writes=[("xT", par, c) for c in range(8)])

            def sq_op(c, src_ap, srckey):
                if c % 2 == 0:
                    P.op("act", lambda e, c=c: e.activation(out=sq8[:, c, :], in_=src_ap, func=AF.Square),
                         [srckey], [("sq8", c)])
                else:
                    P.op("dve", lambda e, c=c: e.tensor_tensor(out=sq8[:, c, :], in0=src_ap, in1=src_ap, op=ALU.mult),
                         [srckey], [("sq8", c)])

            def ones_mm(c):
                P.op("pe", lambda e, c=c: e.matmul(pms[:, :], lhsT=C["onesD"][:], rhs=sq8[:, c, :], start=(c == 0),
                                                   stop=(c == 7)), ["onesD", ("sq8", c)], ["pms"])

            def stats_mm():
                for c in range(8):
                    ones_mm(c)
                self.rstd_from_ms(pms[:, :], "pms", lnv, rstd, "rstd")

            def pre_norm(par, gbase):
                for c in range(8):
                    sq_op(c, xT[par][:, c, :], ("xT", par, c))
                stats_mm()
                for c in range(8):
                    P.op("dve", lambda e, c=c: e.scalar_tensor_tensor(
                        out=hn[:, c, :], in0=xT[par][:, c, :], scalar=C["gcols"][:, gbase + c:gbase + c + 1], in1=rstd[:],
                        op0=ALU.mult, op1=ALU.mult), [("xT", par, c), "gcols", "rstd"], [("hn", c)])

            def pre_scale(par, gbase):
                for c in range(8):
                    P.op("dve", lambda e, c=c: e.tensor_scalar(
                        out=hn[:, c, :], in0=xT[par][:, c, :], scalar1=C["gcols"][:, gbase + c:gbase + c + 1],
                        scalar2=None, op0=ALU.mult), [("xT", par, c), "gcols"], [("hn", c)])
                    P.op("act", lambda e, c=c: e.activation(out=sq8[:, c, :], in_=xT[par][:, c, :], func=AF.Square),
                         [("xT", par, c)], [("sq8", c)])

            def out_proj_post(par, w, wkey, src, srckeys, gbase):
                for mc in range(8):
                    b, bk = bank()
                    self.mm(b[:, :], [(w[:, c, mc * 128:(mc + 1) * 128], src[:, c, :]) for c in range(8)],
                            [wkey] + srckeys, [bk])
                    if mc >= 1:
                        ones_mm(mc - 1)
                    P.op("act", lambda e, b=b, mc=mc: e.copy(out=ys[:, mc, :], in_=b[:, :]), [bk], [("ys", mc)])
                    P.op("dve", lambda e, mc=mc: e.tensor_tensor(out=sq8[:, mc, :], in0=ys[:, mc, :], in1=ys[:, mc, :],
                                                                 op=ALU.mult), [("ys", mc)], [("sq8", mc)])
                ones_mm(7)
                post(par, gbase)

            def post(par, gbase):
                self.rstd_from_ms(pms[:, :], "pms", lnv, rstd, "rstd")
                for c in range(8):
                    P.op("dve", lambda e, c=c: e.scalar_tensor_tensor(
                        out=ys[:, c, :], in0=ys[:, c, :], scalar=C["gcols"][:, gbase + c:gbase + c + 1], in1=rstd[:],
                        op0=ALU.mult, op1=ALU.mult), [("ys", c), "gcols", "rstd"], [("ys", c)])
                    P.op("pool", lambda e, c=c: e.tensor_tensor(out=xT[par][:, c, :], in0=xT[par][:, c, :],
                                                                in1=ys[:, c, :], op=ALU.add),
                         [("xT", par, c), ("ys", c)], [("xT", par, c)])

            if layer == 0:
                P.wait_all("sp", [d.last for d in self.dspool.get("wc", []) if d.last is not None])
            cur_seq = None
            load_tile_inputs(0)
            for pi, (s, t, own, sl) in enumerate(plan):
                par = pi % 2
                t0 = t * TT
                if s != cur_seq:
                    cur_seq = s
                    P.dma("sp", ds_m[0], mK[:], S["mK%d%s" % (layer, s)][:, :, :],
                          reads=[("scr", "mK%d%s" % (layer, s))], writes=["mK"])
                    P.dma("sp", ds_m[1], mV[:], S["mV%d%s" % (layer, s)][:, :, :],
                          reads=[("scr", "mV%d%s" % (layer, s))], writes=["mV"])
                P.dma("sp", ds_oi, oin[:], S["O" + s][t0:t0 + TT, :].rearrange("(j p) n -> p j n", p=128),
                      reads=[("scr", "O" + s)], writes=["oin"])
                if pi + 1 < len(plan):
                    load_tile_inputs(pi + 1)
                for c in range(8):
                    for j in range(4):
                        P.op("pe", lambda e, c=c, j=j: e.transpose(ptrs[c % 2][:, j * 128:(j + 1) * 128],
                                                                   oin[:, j, c * 128:(c + 1) * 128], C["identb"][:]),
                             ["oin", "identb"], [("ptr", c % 2)])
                    self.copy(self.evac_eng(), hn[:, c, :], ptrs[c % 2][:, 0:512], [("ptr", c % 2)],
                              [("hn", c)])
                hnk = [("hn", c) for c in range(8)]
                w, wk = next_slice("wo")
                out_proj_post(par, w, wk, hn, hnk, g0 + 8)
                pre_scale(par, g0 + 16)
                w, wk = next_slice("xq")
                for mc in range(8):
                    b, bk = bank()
                    self.mm(b[:, :], [(w[:, c, mc * 128:(mc + 1) * 128], hn[:, c, :]) for c in range(8)], [wk] + hnk, [bk])
                    if mc == 0:
                        stats_mm()
                    P.op("dve", lambda e, b=b, mc=mc: e.tensor_tensor(out=hid[:, mc, :], in0=b[:, :], in1=rstd[:],
                                                                      op=ALU.mult), [bk, "rstd"], [("hid", mc)])
                xs = 256 ** -0.5

                def xscores(h):
                    xe = ex[h % 2]
                    for mcn in range(2):
                        b, bk = bank()
                        self.mm(b[:, :], [(mK[:, 2 * h + fc, mcn * 128:(mcn + 1) * 128], hid[:, 2 * h + fc, :])
                                          for fc in range(2)], ["mK", ("hid", 2 * h), ("hid", 2 * h + 1)], [bk])
                        P.op("act", lambda e, b=b, xe=xe, mcn=mcn: e.activation(out=xe[:, mcn, :], in_=b[:, :], func=AF.Exp,
                                                                                scale=float(xs)),
                             [bk], [("ex", h % 2, mcn)])

                def xfinish(h):
                    xe = ex[h % 2]
                    exk = [("ex", h % 2, 0), ("ex", h % 2, 1)]
                    self.mm(pmx[:, :], [(C["ones1"][:], xe[:, mcn, :]) for mcn in range(2)], ["ones1"] + exk, ["pms"])
                    P.op("act", lambda e: e.activation(out=lnx[:], in_=pmx[:, :], func=AF.Ln), ["pms"], ["lnx"])
                    P.op("act", lambda e: e.activation(out=rinv[:], in_=lnx[:], func=AF.Exp, scale=-1.0),
                         ["lnx"], ["rinv"])
                    for dc in range(2):
                        b, bk = bank()
                        self.mm(b[:, :], [(mV[:, mcn, h * 256 + dc * 128:h * 256 + (dc + 1) * 128], xe[:, mcn, :])
                                          for mcn in range(2)], ["mV"] + exk, [bk])
                        P.op("dve", lambda e, b=b, h=h, dc=dc: e.tensor_tensor(out=hn[:, 2 * h + dc, :], in0=b[:, :],
                                                                               in1=rinv[:], op=ALU.mult),
                             [bk, "rinv"], [("hn", 2 * h + dc)])

                for h in range(5):
                    if h < 4:
                        xscores(h)
                    if h >= 1:
                        xfinish(h - 1)
                w, wk = next_slice("xo")
                out_proj_post(par, w, wk, hn, hnk, g0 + 24)
                pre_scale(par, g0 + 32)
                for i in range(4):
                    w, wk = next_slice("w1_%d" % i)
                    for mc in range(8):
                        b, bk = bank()
                        self.mm(b[:, :], [(w[:, c, mc * 128:(mc + 1) * 128], hn[:, c, :]) for c in range(8)], [wk] + hnk,
                                [bk])
                        if i == 0 and mc == 0:
                            stats_mm()
                        ri = (i * 8 + mc) % 4
                        r = rl[ri]
                        P.op("dve", lambda e, b=b, r=r: e.scalar_tensor_tensor(out=r[:], in0=b[:, :], scalar=0.0, in1=rstd[:],
                                                                               op0=ALU.max, op1=ALU.mult),
                             [bk, "rstd"], [("rl", ri)])
                        P.op("pool", lambda e, r=r, i=i, mc=mc: e.tensor_tensor(out=hid[:, i * 8 + mc, :], in0=r[:], in1=r[:],
                                                                                 op=ALU.mult),
                             [("rl", ri)], [("hid", i * 8 + mc)])
                for i in range(4):
                    w, wk = next_slice("w2_%d" % i)
                    for mc in range(8):
                        b, bk = bank()
                        self.mm(b[:, :], [(w[:, c, mc * 128:(mc + 1) * 128], hid[:, i * 8 + c, :]) for c in range(8)],
                                [wk] + [("hid", i * 8 + c) for c in range(8)], [bk])
                        if i == 0:
                            P.op("act", lambda e, b=b, mc=mc: e.copy(out=ys[:, mc, :], in_=b[:, :]), [bk], [("ys", mc)])
                        else:
                            P.op("dve", lambda e, b=b, mc=mc: e.tensor_tensor(out=ys[:, mc, :], in0=ys[:, mc, :], in1=b[:, :],
                                                                              op=ALU.add), [bk, ("ys", mc)], [("ys", mc)])
                        if i == 3:
                            sq_op(mc, ys[:, mc, :], ("ys", mc))
                            if mc >= 1:
                                ones_mm(mc - 1)
                ones_mm(7)
                post(par, g0 + 40)
                xk = [("xT", par, c) for c in range(8)]
                if not last:
                    if own:
                        P.dma("pool", ds_st[0], S["XT" + s][:, t0:t0 + TT].rearrange("(c p) n -> p c n", p=128), xT[par][:],
                              reads=xk, writes=[("scr", "XT" + s, t)])
                    pre_norm(par, 48)
                    for i in range(3):
                        if i == 0 and not own:
                            continue
                        w, wk = next_slice("win_%d" % i)
                        if i < 2:
                            for mc in range(8):
                                b, bk = bank()
                                self.mm(b[:, :], [(w[:, c, mc * 128:(mc + 1) * 128], hn[:, c, :]) for c in range(8)],
                                        [wk] + hnk, [bk])
                                self.copy(self.evac_eng(), hid[:, i * 8 + mc, :], b[:, :], [bk], [("hid", i * 8 + mc)])
                            dst = S[("Q" if i == 0 else "K") + s]
                            P.dma("pool", ds_st[1 + i], dst[:, t0:t0 + TT].rearrange("(c p) n -> p c n", p=128),
                                  hid[:, i * 8:(i + 1) * 8, :], reads=[("hid", i * 8 + c) for c in range(8)],
                                  writes=[("scr", dst.tensor.name)])
                        else:
                            vv = hid[:, 16:24, :].rearrange("p (j a) n -> p j (a n)", j=4)
                            for j in range(4):
                                for hf in range(2):
                                    b, bk = bank()
                                    self.mm(b[:, :], [(hn[:, c, j * 128:(j + 1) * 128], w[:, c, hf * 512:(hf + 1) * 512])
                                                      for c in range(8)], [wk] + hnk, [bk])
                                    self.copy(self.evac_eng(), vv[:, j, hf * 512:(hf + 1) * 512], b[:, :], [bk],
                                              [("hid", 16 + 2 * j + hf)])
                            P.dma("pool", ds_st[3], S["V" + s][t0:t0 + TT, :].rearrange("(j p) n -> p j n", p=128), vv,
                                  reads=[("hid", 16 + c) for c in range(8)], writes=[("scr", "V" + s)])
                else:
                    for j in range(4):
                        for hf in range(2):
                            b, bk = bank()
                            for cc in range(4):
                                c = hf * 4 + cc
                                P.op("pe", lambda e, b=b, c=c, cc=cc, j=j, par=par: e.transpose(
                                    b[:, cc * 128:(cc + 1) * 128], xT[par][:, c, j * 128:(j + 1) * 128], C["identf"][:]),
                                    [("xT", par, c), "identf"], [bk])
                            self.copy(self.evac_eng(), yout[:, j, hf * 512:(hf + 1) * 512], b[:, :], [bk], [("ys", 2 * j + hf)])
                    ydst = self.yA if s == "A" else self.yB
                    P.dma("pool", ds_st[4], ydst[t0:t0 + TT, :].rearrange("(j p) n -> p j n", p=128), yout,
                          reads=[("ys", c) for c in range(8)], writes=[("out", s)])
            assert state["used"] == len(flat)
            self.end_phase()


def _t5_bucket(rel):
    nb = 16
    max_exact = 8
    rel = np.asarray(rel, dtype=np.int64)
    ret = (rel > 0).astype(np.int64) * nb
    n = np.abs(rel)
    nf = np.maximum(n, 1).astype(np.float32)
    val = np.log(nf / np.float32(max_exact)) / np.float32(math.log(128 / max_exact)) * np.float32(nb - max_exact)
    large = max_exact + val.astype(np.float32).astype(np.int32).astype(np.int64)
    large = np.minimum(large, nb - 1)
    return ret + np.where(n < max_exact, n, large)


def _onehot(sign):
    m = np.arange(GL)
    rel = sign * (639 - m)
    b = _t5_bucket(rel)
    oh = np.zeros((32, GL), np.float32)
    oh[b, m] = 1.0
    return oh


def _rope_tables(pos):
    half = 32
    freqs = (np.float32(10000.0) ** (-np.arange(half, dtype=np.float32) / np.float32(half))).astype(np.float32)
    ang = (pos.astype(np.float32)[None, :] * freqs[:, None]).astype(np.float32)
    c = np.cos(ang).astype(np.float32)
    s = np.sin(ang).astype(np.float32)
    return np.ascontiguousarray(np.concatenate([c, c], 0)), np.ascontiguousarray(np.concatenate([s, s], 0))


_CACHE = {}
_DBG = None


def kernel(x_prompt, x_sample, mem_prompt, mem_sample, norm_gains, rel_bias_table,
           mla_w_in, mla_q_norm, mla_kv_norm, mla_w_uq, mla_w_ukv, mla_w_o,
           diff_w_in, diff_lambda, diff_subln, diff_w_o,
           xattn_mem_norm, xattn_w_q, xattn_w_kv, xattn_w_o, mlp_w1, mlp_w2):
    f = lambda a: np.ascontiguousarray(np.asarray(a, dtype=np.float32))
    x_prompt, x_sample, mem_prompt, mem_sample = f(x_prompt), f(x_sample), f(mem_prompt), f(mem_sample)
    NB, SB, _ = x_prompt.shape
    NA, SA, _ = x_sample.shape
    assert NA == 8 and NB == 4
    HB = SB // 2
    key = (SA, SB)
    if key not in _CACHE:
        _CACHE[key] = Builder(SA, SB).build()
    nc = _CACHE[key]

    def cols(v):
        v = f(v).reshape(-1, 128)
        return np.ascontiguousarray(v.T)

    shared = {
        "gcols": cols(f(norm_gains).reshape(-1)),
        "memg": cols(f(xattn_mem_norm).reshape(-1)),
        "qkvg": np.ascontiguousarray(np.concatenate([cols(f(mla_q_norm)[0]), cols(f(mla_kv_norm)[0])], 1)),
        "subg": cols(f(diff_subln)[0]),
        "lam": f(diff_lambda)[0].reshape(1, 256),
        "table": f(rel_bias_table),
        "ohA": _onehot(1),
        "ident": np.eye(128, dtype=np.float32),
        "anti": np.ascontiguousarray(np.eye(128, dtype=np.float32)[::-1]),
        "mla_w_in": f(mla_w_in)[0], "mla_w_uq": f(mla_w_uq)[0], "mla_w_ukv": f(mla_w_ukv)[0], "mla_w_o": f(mla_w_o)[0],
        "diff_w_in": f(diff_w_in)[0], "diff_w_o": f(diff_w_o)[0],
        "xattn_w_q": f(xattn_w_q), "xattn_w_kv": f(xattn_w_kv), "xattn_w_o": f(xattn_w_o),
        "mlp_w1": f(mlp_w1), "mlp_w2": f(mlp_w2),
    }
    cosA, sinA = _rope_tables(np.arange(SA))
    shared["cosA"], shared["sinA"] = cosA, sinA
    tabs = {}
    for p in range(2):
        pos = np.arange(SB) if p == 0 else (SB - 1 - np.arange(SB))
        tabs[p] = _rope_tables(pos) + (_onehot(1 if p == 0 else -1),)
    in_maps = []
    for c in range(8):
        b, p = c // 2, c % 2
        m = dict(shared)
        m["xA"] = x_sample[c]
        m["xB"] = x_prompt[b] if p == 0 else np.ascontiguousarray(x_prompt[b][::-1])
        m["memA"] = mem_sample[c]
        m["memB"] = mem_prompt[b]
        m["cosB"], m["sinB"], m["ohB"] = tabs[p]
        in_maps.append(m)
    res = run_bass_kernel_spmd(nc, in_maps, core_ids=list(range(8)))
    global _DBG
    _DBG = res.results
    y_sample = np.stack([np.asarray(res.results[c]["yA"], dtype=np.float32) for c in range(8)], 0)
    y_prompt = np.empty((NB, SB, D), np.float32)
    for c in range(8):
        b, p = c // 2, c % 2
        yb = np.asarray(res.results[c]["yB"], dtype=np.float32)
        if p == 0:
            y_prompt[b, :HB] = yb
        else:
            y_prompt[b, HB:] = yb[::-1]
    return (y_prompt, y_sample)
```

```python
import math
from contextlib import ExitStack

import numpy as np
import concourse.bass as bass
import concourse.mybir as mybir
from concourse.bass_utils import run_bass_kernel_spmd

F32 = mybir.dt.float32
BF16 = mybir.dt.bfloat16
AF = mybir.ActivationFunctionType
ALU = mybir.AluOpType

import os as _os
SAME_ENGINE_SYNC = set(_os.environ.get('SES', 'act,dve,pool').split(','))
ENGS = ("pe", "act", "dve", "pool", "sp")

D = 1024
NC8 = 8
TT = 512
NMEM = 256
EPS = 1e-6
GW = 1152
GL = 1280


class Ins:
    __slots__ = ("eng", "fn", "deps", "needed", "val", "dsem", "flushed")

    def __init__(self, eng, fn, dsem=None):
        self.eng = eng
        self.fn = fn
        self.deps = ()
        self.needed = False
        self.val = 0
        self.dsem = dsem
        self.flushed = False


class DSem:
    def __init__(self, handle, name):
        self.handle = handle
        self.name = name
        self.count = 0
        self.last = None
        self.queue = None


class BufState:
    __slots__ = ("w", "r")

    def __init__(self):
        self.w = None
        self.r = {}


class Prog:
    def __init__(self, nc, stack):
        self.nc = nc
        self.stack = stack
        self.esem = {}
        self.ecnt = {e: 0 for e in ENGS}
        for e in ("pe", "act", "dve", "pool"):
            self.esem[e] = stack.enter_context(nc.semaphore("es_" + e))
        self.bufs = {}
        self.streams = {e: [] for e in ENGS}
        self.waited = {e: {} for e in ENGS}
        self.dsems = []
        self.nins = 0
        self.nwait = 0

    def dsem(self, name):
        h = self.stack.enter_context(self.nc.semaphore("ds_" + name))
        d = DSem(h, name)
        self.dsems.append(d)
        return d

    def _record(self, ins, reads, writes):
        deps = {}
        for b in reads:
            st = self.bufs.get(b)
            if st is not None and st.w is not None:
                deps[id(st.w)] = st.w
        for b in writes:
            st = self.bufs.get(b)
            if st is not None:
                if st.w is not None:
                    deps[id(st.w)] = st.w
                for r in st.r.values():
                    deps[id(r)] = r
        if ins.dsem is not None and ins.dsem.last is not None:
            deps[id(ins.dsem.last)] = ins.dsem.last
        out = []
        for d in deps.values():
            if d is ins:
                continue
            if d.flushed and d.dsem is None:
                continue
            if d.dsem is None and ins.dsem is None and d.eng == ins.eng:
                if d.eng == "pe" or d.eng not in SAME_ENGINE_SYNC:
                    continue
            d.needed = True
            out.append(d)
        ins.deps = out
        rk = ins.eng if ins.dsem is None else ("d", id(ins.dsem))
        for b in reads:
            st = self.bufs.get(b)
            if st is None:
                st = self.bufs[b] = BufState()
            st.r[rk] = ins
        for b in writes:
            st = self.bufs.get(b)
            if st is None:
                st = self.bufs[b] = BufState()
            st.w = ins
            st.r = {}
        if ins.dsem is not None:
            ins.dsem.last = ins
        self.streams[ins.eng].append(ins)
        return ins

    def op(self, eng, fn, reads=(), writes=()):
        return self._record(Ins(eng, fn), reads, writes)

    def dma(self, queue, dsem, out, in_, reads=(), writes=(), **kw):
        assert dsem.queue in (None, queue)
        dsem.queue = queue
        fn = lambda e: e.dma_start(out=out, in_=in_, **kw)
        return self._record(Ins(queue, fn, dsem=dsem), reads, writes)

    def wait_all(self, eng, inss):
        ins = Ins(eng, None)
        out = []
        for d in inss:
            if d is None:
                continue
            d.needed = True
            out.append(d)
        ins.deps = out
        self.streams[eng].append(ins)
        return ins

    def flush(self):
        nc = self.nc
        for e in ENGS:
            for ins in self.streams[e]:
                if ins.dsem is not None:
                    ins.dsem.count += 1
                    ins.val = 16 * ins.dsem.count
                elif ins.needed and ins.fn is not None:
                    self.ecnt[e] += 1
                    ins.val = self.ecnt[e]

        def replay(ename, eng):
            waited = self.waited[ename]
            for ins in self.streams[ename]:
                for d in ins.deps:
                    if d.dsem is not None:
                        key = ("d", id(d.dsem))
                        h = d.dsem.handle
                    else:
                        key = d.eng
                        h = self.esem[d.eng]
                    assert d.val > 0, (ename, d.eng)
                    if waited.get(key, 0) < d.val:
                        eng.wait_ge(h, d.val)
                        waited[key] = d.val
                        self.nwait += 1
                if ins.fn is None:
                    continue
                i = ins.fn(eng)
                self.nins += 1
                if ins.dsem is not None:
                    i.then_inc(ins.dsem.handle, 16)
                elif ins.needed:
                    i.then_inc(self.esem[ename], 1)

        with nc.Block() as block:
            @block.tensor
            def _(t):
                replay("pe", t)

            @block.scalar
            def _(s):
                replay("act", s)

            @block.vector
            def _(v):
                replay("dve", v)

            @block.gpsimd
            def _(g):
                replay("pool", g)

            @block.sync
            def _(s):
                replay("sp", s)

        for e in ENGS:
            for ins in self.streams[e]:
                ins.flushed = True
        self.streams = {e: [] for e in ENGS}


class Builder:
    def __init__(self, SA, SB):
        self.SA, self.SB = SA, SB
        self.HB = SB // 2
        self.nc = bass.Bass("TRN2", target_bir_lowering=False)
        self.rr = 0

    def un(self, name):
        self.uid = getattr(self, "uid", 0) + 1
        return "%s_u%d" % (name, self.uid)

    def din(self, name, shape, dt=F32):
        return self.nc.dram_tensor(name, list(shape), dt, kind="ExternalInput").ap()

    def dout(self, name, shape, dt=F32):
        return self.nc.dram_tensor(name, list(shape), dt, kind="ExternalOutput").ap()

    def dscr(self, name, shape, dt):
        import os
        kind = "ExternalOutput" if os.environ.get("KDEBUG") else "Internal"
        return self.nc.dram_tensor(name, list(shape), dt, kind=kind).ap()

    def build(self):
        nc = self.nc
        SA, SB, HB = self.SA, self.SB, self.HB
        I = self.I = {}
        I["xA"] = self.din("xA", [SA, D])
        I["xB"] = self.din("xB", [SB, D])
        I["memA"] = self.din("memA", [NMEM, D])
        I["memB"] = self.din("memB", [NMEM, D])
        I["gcols"] = self.din("gcols", [128, 96])
        I["memg"] = self.din("memg", [128, 16])
        I["qkvg"] = self.din("qkvg", [128, 4])
        I["subg"] = self.din("subg", [128, 1])
        I["lam"] = self.din("lam", [1, 256])
        I["table"] = self.din("table", [32, 8])
        I["ohA"] = self.din("ohA", [32, GL])
        I["ohB"] = self.din("ohB", [32, GL])
        I["ident"] = self.din("ident", [128, 128])
        I["anti"] = self.din("anti", [128, 128])
        I["cosA"] = self.din("cosA", [64, SA])
        I["sinA"] = self.din("sinA", [64, SA])
        I["cosB"] = self.din("cosB", [64, SB])
        I["sinB"] = self.din("sinB", [64, SB])
        I["mla_w_in"] = self.din("mla_w_in", [D, 576])
        I["mla_w_uq"] = self.din("mla_w_uq", [256, 1536])
        I["mla_w_ukv"] = self.din("mla_w_ukv", [256, 2048])
        I["mla_w_o"] = self.din("mla_w_o", [D, D])
        I["diff_w_in"] = self.din("diff_w_in", [D, 3 * D])
        I["diff_w_o"] = self.din("diff_w_o", [D, D])
        I["xattn_w_q"] = self.din("xattn_w_q", [2, D, D])
        I["xattn_w_kv"] = self.din("xattn_w_kv", [2, D, 2 * D])
        I["xattn_w_o"] = self.din("xattn_w_o", [2, D, D])
        I["mlp_w1"] = self.din("mlp_w1", [2, D, 4 * D])
        I["mlp_w2"] = self.din("mlp_w2", [2, 4 * D, D])
        self.yA = self.dout("yA", [SA, D])
        self.yB = self.dout("yB", [HB, D])
        S = self.S = {}
        for s, L in (("A", SA), ("B", SB)):
            S["XT" + s] = self.dscr("XT" + s, [D, L], F32)
            S["Q" + s] = self.dscr("Q" + s, [D, L], BF16)
            S["K" + s] = self.dscr("K" + s, [D, L], BF16)
            S["QR" + s] = self.dscr("QR" + s, [8, 64, L], BF16)
            S["KR" + s] = self.dscr("KR" + s, [64, L], BF16)
            S["V" + s] = self.dscr("V" + s, [L, D], BF16)
            S["O" + s] = self.dscr("O" + s, [L, D], BF16)
            S["strip" + s] = self.dscr("strip" + s, [128, 8, GW], BF16)
            S["gv" + s] = self.dscr("gv" + s, [8, GL], F32)
            for l in range(2):
                S["mK%d%s" % (l, s)] = self.dscr("mK%d%s" % (l, s), [128, 8, NMEM], BF16)
                S["mV%d%s" % (l, s)] = self.dscr("mV%d%s" % (l, s), [128, 2, D], BF16)
        S["w_mla_o"] = self.dscr("w_mla_o", [D, D], BF16)
        S["w_diff_o"] = self.dscr("w_diff_o", [D, D], BF16)
        S["w_diff_in"] = self.dscr("w_diff_in", [D, 3 * D], BF16)
        S["w_xq"] = self.dscr("w_xq", [2, D, D], BF16)
        S["w_xo"] = self.dscr("w_xo", [2, D, D], BF16)
        S["w_1"] = self.dscr("w_1", [2, D, 4 * D], BF16)
        S["w_2"] = self.dscr("w_2", [2, 4 * D, D], BF16)

        with ExitStack() as top:
            self.P = P = Prog(nc, top)
            sbt = lambda name, shape, dt: top.enter_context(nc.sbuf_tensor(self.un(name), list(shape), dt))
            C = self.C = {}
            C["identf"] = sbt("identf", [128, 128], F32)
            C["identb"] = sbt("identb", [128, 128], BF16)
            C["antif"] = sbt("antif", [128, 128], F32)
            C["ones1"] = sbt("ones1", [128, 128], BF16)
            C["onesD"] = sbt("onesD", [128, 128], BF16)
            C["ones256"] = sbt("ones256", [128, 128], BF16)
            C["onesf"] = sbt("onesf", [128, 128], F32)
            C["gcols"] = sbt("gcols_s", [128, 96], F32)
            C["memg"] = sbt("memg_s", [128, 16], F32)
            C["qkvg"] = sbt("qkvg_s", [128, 4], F32)
            C["subs"] = sbt("subs_s", [128, 1], F32)
            C["nlam"] = sbt("nlam_s", [128, 1], F32)
            C["epsc"] = sbt("epsc", [128, 1], F32)
            C["cst"] = sbt("cst_s", [128, 2, 8, 2], F32)
            self.nds = 0
            self.dspool = {}
            self.dsrr = {}

            import os
            stop = int(os.environ.get("STOP_AFTER", "99"))
            phases = [self.phase_prologue, self.phase_l0_p1, lambda: self.phase_attn(0), lambda: self.phase_tail(0),
                      lambda: self.phase_attn(1), lambda: self.phase_tail(1)]
            for i, ph in enumerate(phases):
                if i > stop:
                    break
                ph()
            print("program: instrs", self.P.nins, "waits", self.P.nwait, "dsems", len(self.P.dsems), flush=True)
        return nc

    def newds(self, name):
        self.nds += 1
        if name in ("c", "wc"):
            pool = self.dspool.setdefault(name, [])
            if len(pool) < 4:
                pool.append(self.P.dsem("%s%d" % (name, self.nds)))
                return pool[-1]
            self.dsrr[name] = self.dsrr.get(name, 0) + 1
            return pool[self.dsrr[name] % 4]
        return self.P.dsem("%s%d" % (name, self.nds))

    def evac_eng(self):
        self.rr += 1
        return "act" if (self.rr & 1) else "dve"

    def copy(self, eng, out, in_, reads, writes):
        if eng == "act":
            return self.P.op("act", lambda e: e.copy(out=out, in_=in_), reads, writes)
        return self.P.op(eng, lambda e: e.tensor_copy(out=out, in_=in_), reads, writes)

    def mm(self, out, pairs, reads, writes):
        n = len(pairs)
        last = None
        for i, (l, r) in enumerate(pairs):
            last = self.P.op("pe", (lambda e, l=l, r=r, a=(i == 0), b=(i == n - 1):
                                    e.matmul(out, lhsT=l, rhs=r, start=a, stop=b)), reads, writes)
        return last

    def rstd_from_ms(self, ms_ps, ms_key, lnv, rstd, key, width=TT):
        P, C = self.P, self.C
        P.op("act", lambda e: e.activation(out=lnv[:, 0:width], in_=ms_ps, func=AF.Ln, bias=C["epsc"][:], scale=1.0),
             [ms_key, "epsc"], [key + "_ln"])
        P.op("act", lambda e: e.activation(out=rstd[:, 0:width], in_=lnv[:, 0:width], func=AF.Exp, scale=-0.5),
             [key + "_ln"], [key])

    def phase_prologue(self):
        nc, P, C, I, S = self.nc, self.P, self.C, self.I, self.S
        with ExitStack() as st:
            sbt = lambda name, shape, dt: st.enter_context(nc.sbuf_tensor(self.un(name), list(shape), dt))
            pst = lambda name, shape, dt: st.enter_context(nc.psum_tensor(self.un(name), list(shape), dt))
            ld = self.newds("c")
            P.dma("sp", ld, C["identf"][:], I["ident"][:, :], writes=["identf"])
            ld2 = self.newds("c")
            P.dma("sp", ld2, C["antif"][:], I["anti"][:, :], writes=["antif"])
            for nm in ("gcols", "memg", "qkvg"):
                P.dma("sp", self.newds("c"), C[nm][:], I[nm][:, :], writes=[nm])
            P.op("act", lambda e: e.copy(out=C["identb"][:], in_=C["identf"][:]), ["identf"], ["identb"])
            P.op("dve", lambda e: e.memset(C["ones1"][:], 1.0), [], ["ones1"])
            P.op("dve", lambda e: e.memset(C["onesD"][:], 1.0 / D), [], ["onesD"])
            P.op("dve", lambda e: e.memset(C["ones256"][:], 1.0 / 256), [], ["ones256"])
            P.op("dve", lambda e: e.memset(C["onesf"][:], 1.0), [], ["onesf"])
            P.op("dve", lambda e: e.memset(C["epsc"][:], EPS), [], ["epsc"])
            lam_init = 0.8 - 0.6 * math.exp(-0.3 * 1)
            subg = sbt("subg", [128, 1], F32)
            P.dma("sp", self.newds("c"), subg[:], I["subg"][:, :], writes=["subg"])
            P.op("dve", lambda e: e.tensor_scalar(out=C["subs"][:], in0=subg[:], scalar1=float(1.0 - lam_init),
                                                   scalar2=None, op0=ALU.mult), ["subg"], ["subs"])
            lam = sbt("lam", [1, 256], F32)
            lj = sbt("lamj", [1, 64], F32)
            ls = sbt("lams", [1, 4], F32)
            P.dma("sp", self.newds("c"), lam[:], I["lam"][:, :], writes=["lam"])
            P.op("dve", lambda e: e.memset(ls[:], 0.0), [], ["ls"])
            P.op("dve", lambda e: e.scalar_tensor_tensor(out=lj[:], in0=lam[:, 0:64], scalar=1.0, in1=lam[:, 64:128],
                                                          op0=ALU.mult, op1=ALU.mult, accum_out=ls[:, 0:1]),
                 ["lam", "ls"], ["lj", "ls"])
            P.op("dve", lambda e: e.scalar_tensor_tensor(out=lj[:], in0=lam[:, 128:192], scalar=1.0, in1=lam[:, 192:256],
                                                          op0=ALU.mult, op1=ALU.mult, accum_out=ls[:, 1:2]),
                 ["lam", "ls", "lj"], ["lj", "ls"])
            P.op("act", lambda e: e.activation(out=ls[:, 2:4], in_=ls[:, 0:2], func=AF.Exp), ["ls"], ["ls"])
            P.op("dve", lambda e: e.tensor_tensor(out=ls[:, 0:1], in0=ls[:, 3:4], in1=ls[:, 2:3], op=ALU.subtract),
                 ["ls"], ["ls"])
            P.op("dve", lambda e: e.tensor_scalar(out=ls[:, 1:2], in0=ls[:, 0:1], scalar1=float(-lam_init),
                                                   scalar2=None, op0=ALU.add), ["ls"], ["ls"])
            pb = pst("pb", [128, 512], F32)
            P.op("pe", lambda e: e.matmul(pb[:, 0:1], lhsT=C["onesf"][0:1, :], rhs=ls[0:1, 1:2], start=True, stop=True),
                 ["onesf", "ls"], ["pb"])
            P.op("dve", lambda e: e.tensor_copy(out=C["nlam"][:], in_=pb[:, 0:1]), ["pb"], ["nlam"])

            wdo = sbt("wdo", [128, 8, D], F32)
            wdb = sbt("wdb", [128, 8, D], BF16)
            P.dma("sp", self.newds("c"), wdo[:], I["diff_w_o"].rearrange("(c p) n -> p c n", p=128), writes=["wdo"])
            for c in range(8):
                P.op("dve", lambda e, c=c: e.tensor_scalar(out=wdb[:, c, :], in0=wdo[:, c, :], scalar1=C["subs"][:],
                                                            scalar2=None, op0=ALU.mult), ["wdo", "subs"], ["wdb"])
            P.dma("sp", self.newds("c"), S["w_diff_o"].rearrange("(c p) n -> p c n", p=128), wdb[:],
                  reads=["wdb"], writes=[("scr", "w_diff_o")])

            tab = sbt("tab", [32, 8], F32)
            P.dma("sp", self.newds("c"), tab[:], I["table"][:, :], writes=["tab"])
            oh = sbt("oh", [32, GL], F32)
            gvs = sbt("gvs", [8, GL], F32)
            hk = sbt("hk", [128, 8, GW], F32)
            stp = sbt("stp", [128, 8, GW], BF16)
            pg = [pst("pg%d" % i, [128, 512], F32) for i in range(3)]
            for si, s in enumerate(("A", "B")):
                P.dma("sp", self.newds("c"), oh[:], I["oh" + s][:, :], writes=["oh"])
                for j, (a, b) in enumerate(((0, 512), (512, 1024), (1024, GL))):
                    P.op("pe", lambda e, j=j, a=a, b=b: e.matmul(pg[j][0:8, 0:b - a], lhsT=tab[:, :], rhs=oh[:, a:b],
                                                                 start=True, stop=True), ["tab", "oh"], [("pg", j)])
                    P.op("dve", lambda e, j=j, a=a, b=b: e.tensor_copy(out=gvs[:, a:b], in_=pg[j][0:8, 0:b - a]),
                         [("pg", j)], ["gvs"])
                P.dma("sp", self.newds("c"), S["gv" + s][:, :], gvs[:], reads=["gvs"], writes=[("scr", "gv" + s)])
                src = bass.AP(S["gv" + s].tensor, 0, [[1, 128], [GL, 8], [1, GW]])
                P.dma("sp", self.newds("c"), hk[:], src, reads=[("scr", "gv" + s)], writes=["hk"])
                for h in range(8):
                    for j, (a, b) in enumerate(((0, 512), (512, 1024), (1024, GW))):
                        P.op("pe", lambda e, h=h, j=j, a=a, b=b: e.matmul(pg[j][:, 0:b - a], lhsT=C["antif"][:],
                                                                          rhs=hk[:, h, a:b], start=True, stop=True),
                             ["antif", "hk"], [("pg", j)])
                        P.op("dve", lambda e, h=h, j=j, a=a, b=b: e.tensor_scalar(out=stp[:, h, a:b], in0=pg[j][:, 0:b - a],
                                                                                  scalar1=8.0, scalar2=None, op0=ALU.mult),
                             [("pg", j)], ["stp"])
                        if j == 0:
                            P.op("dve", lambda e, h=h, si=si: e.tensor_copy(out=C["cst"][:, si, h, 1:2], in_=pg[0][:, 0:1]),
                                 [("pg", 0)], ["cst"])
                        if j == 2:
                            P.op("dve", lambda e, h=h, si=si: e.tensor_copy(out=C["cst"][:, si, h, 0:1],
                                                                            in_=pg[2][:, GW - 1025:GW - 1024]),
                                 [("pg", 2)], ["cst"])
                P.dma("sp", self.newds("c"), S["strip" + s][:, :, :], stp[:], reads=["stp"],
                      writes=[("scr", "strip" + s)])

            self.end_phase()
        with ExitStack() as st:
            sbt = lambda name, shape, dt: st.enter_context(nc.sbuf_tensor(self.un(name), list(shape), dt))
            pst = lambda name, shape, dt: st.enter_context(nc.psum_tensor(self.un(name), list(shape), dt))
            wkv = sbt("wkv", [128, 8, 2 * D], BF16)
            mem = sbt("mem", [128, 2, D], F32)
            memT = sbt("memT", [128, 8, NMEM], F32)
            msq = sbt("msq", [128, 8, NMEM], BF16)
            mn = sbt("mn", [128, 8, NMEM], BF16)
            lnv = sbt("plnv", [128, 512], F32)
            rstd = sbt("prstd", [128, 512], F32)
            mko = sbt("mko", [128, 8, NMEM], BF16)
            mvo = sbt("mvo", [128, 2, D], BF16)
            pm = [pst("pm%d" % i, [128, 512], F32) for i in range(3)]
            pms = pst("pms", [128, 512], F32)
            k = 0
            for l in range(2):
                P.dma("pool", self.newds("wkv"), wkv[:], I["xattn_w_kv"][l].rearrange("(c p) n -> p c n", p=128),
                      writes=["wkv"])
                for s in ("A", "B"):
                    P.dma("sp", self.newds("c"), mem[:], I["mem" + s].rearrange("(j p) n -> p j n", p=128), writes=["mem"])
                    for c in range(8):
                        b = pm[k % 3]; bk = ("pm", k % 3); k += 1
                        for j in range(2):
                            P.op("pe", lambda e, b=b, c=c, j=j: e.transpose(b[:, j * 128:(j + 1) * 128],
                                                                            mem[:, j, c * 128:(c + 1) * 128], C["identf"][:]),
                                 ["mem", "identf"], [bk])
                        P.op("act", lambda e, b=b, c=c: e.copy(out=memT[:, c, :], in_=b[:, 0:NMEM]), [bk], [("memT", c)])
                        P.op("dve", lambda e, c=c: e.tensor_tensor(out=msq[:, c, :], in0=memT[:, c, :], in1=memT[:, c, :],
                                                                   op=ALU.mult), [("memT", c)], [("msq", c)])
                    self.mm(pms[:, 0:NMEM], [(C["onesD"][:], msq[:, c, :]) for c in range(8)],
                            ["onesD"] + [("msq", c) for c in range(8)], ["pms"])
                    self.rstd_from_ms(pms[:, 0:NMEM], "pms", lnv, rstd, "prstd", width=NMEM)
                    for c in range(8):
                        P.op("dve", lambda e, c=c, l=l: e.scalar_tensor_tensor(
                            out=mn[:, c, :], in0=memT[:, c, :], scalar=C["memg"][:, l * 8 + c:l * 8 + c + 1],
                            in1=rstd[:, 0:NMEM], op0=ALU.mult, op1=ALU.mult),
                            [("memT", c), "memg", "prstd"], [("mn", c)])
                    for mc in range(8):
                        b = pm[k % 3]; bk = ("pm", k % 3); k += 1
                        self.mm(b[:, 0:NMEM], [(wkv[:, c, mc * 128:(mc + 1) * 128], mn[:, c, :]) for c in range(8)],
                                ["wkv"] + [("mn", c) for c in range(8)], [bk])
                        self.copy(self.evac_eng(), mko[:, mc, :], b[:, 0:NMEM], [bk], ["mko"])
                    for tcn in range(2):
                        for hf in range(2):
                            b = pm[k % 3]; bk = ("pm", k % 3); k += 1
                            self.mm(b[:, :], [(mn[:, c, tcn * 128:(tcn + 1) * 128],
                                               wkv[:, c, D + hf * 512:D + (hf + 1) * 512]) for c in range(8)],
                                    ["wkv"] + [("mn", c) for c in range(8)], [bk])
                            self.copy(self.evac_eng(), mvo[:, tcn, hf * 512:(hf + 1) * 512], b[:, :], [bk], ["mvo"])
                    P.dma("sp", self.newds("c"), S["mK%d%s" % (l, s)][:, :, :], mko[:], reads=["mko"],
                          writes=[("scr", "mK%d%s" % (l, s))])
                    P.dma("sp", self.newds("c"), S["mV%d%s" % (l, s)][:, :, :], mvo[:], reads=["mvo"],
                          writes=[("scr", "mV%d%s" % (l, s))])
            self.end_phase()

    def emit_casts(self):
        P, I, S = self.P, self.I, self.S
        def cast(dst, src, rows):
            ncol = src.shape[1]
            for r0 in range(0, rows, 256):
                for c0 in range(0, ncol, 1024):
                    P.dma("pool", self.newds("wc"), dst[r0:r0 + 256, c0:c0 + 1024], src[r0:r0 + 256, c0:c0 + 1024],
                          writes=[("scrw", dst.tensor.name, self.nds)])
        cast(S["w_mla_o"], I["mla_w_o"], D)
        cast(S["w_diff_in"], I["diff_w_in"], D)
        for l in range(2):
            cast(S["w_xq"][l], I["xattn_w_q"][l], D)
            cast(S["w_xo"][l], I["xattn_w_o"][l], D)
            cast(S["w_1"][l], I["mlp_w1"][l], D)
            cast(S["w_2"][l], I["mlp_w2"][l], 4 * D)

    def end_phase(self, skip_casts=False):
        P = self.P
        casts = set(id(d) for d in self.dspool.get("wc", [])) if skip_casts else set()
        P.wait_all("sp", [d.last for d in P.dsems if d.queue == "sp" and d.last is not None and not d.last.flushed])
        P.wait_all("pool", [d.last for d in P.dsems if d.queue == "pool" and d.last is not None and not d.last.flushed
                            and id(d) not in casts])
        P.flush()

    def phase_l0_p1(self):
        nc, P, C, I, S = self.nc, self.P, self.C, self.I, self.S
        with ExitStack() as st:
            sbt = lambda name, shape, dt: st.enter_context(nc.sbuf_tensor(self.un(name), list(shape), dt))
            pst = lambda name, shape, dt: st.enter_context(nc.psum_tensor(self.un(name), list(shape), dt))
            w_in = sbt("w_in", [128, 8, 640], BF16)
            w_uq = sbt("w_uq", [128, 2, 1536], BF16)
            w_qrh = sbt("w_qrh", [128, 2, 512], BF16)
            w_ukv = sbt("w_ukv", [128, 2, 2048], BF16)
            P.dma("pool", self.newds("w"), w_in[:, :, 0:576], I["mla_w_in"].rearrange("(c p) n -> p c n", p=128),
                  writes=["w_in"])
            P.dma("pool", self.newds("w"), w_uq[:], I["mla_w_uq"].rearrange("(c p) n -> p c n", p=128), writes=["w_uq"])
            P.dma("pool", self.newds("w"), w_ukv[:], I["mla_w_ukv"].rearrange("(c p) n -> p c n", p=128), writes=["w_ukv"])
            self.emit_casts()
            P.op("act", lambda e: e.mul(out=w_in[:, :, 576:608], in_=w_in[:, :, 544:576], mul=-1.0),
                 ["w_in"], ["w_in"])
            P.op("act", lambda e: e.copy(out=w_in[:, :, 608:640], in_=w_in[:, :, 512:544]), ["w_in"], ["w_in"])
            uq4 = w_uq[:].rearrange("p c (h f) -> p c h f", h=8)
            rh4 = w_qrh[:].rearrange("p c (h f) -> p c h f", h=8)
            for c in range(2):
                P.op("act", lambda e, c=c: e.mul(out=rh4[:, c, :, 0:32], in_=uq4[:, c, :, 160:192], mul=-1.0),
                     ["w_uq"], ["w_qrh"])
                P.op("act", lambda e, c=c: e.copy(out=rh4[:, c, :, 32:64], in_=uq4[:, c, :, 128:160]), ["w_uq"], ["w_qrh"])
            xin = sbt("xin", [128, 4, D], F32)
            xT = sbt("xT", [128, 8, TT], F32)
            sq = [sbt("sq%d" % i, [128, TT], BF16) for i in range(2)]
            hn = sbt("hn", [128, 8, TT], BF16)
            lnv = sbt("lnv", [128, TT], F32)
            rstd = sbt("rstd", [128, TT], F32)
            cst = sbt("cstage", [128, 4, TT], F32)
            cn = sbt("cn", [128, 4, TT], BF16)
            cs = sbt("cosb", [64, TT], F32)
            sn = sbt("sinb", [64, TT], F32)
            tmp = [sbt("rt%d" % i, [64, TT], F32) for i in range(2)]
            qst = sbt("qst", [128, 8, TT], BF16)
            kst = sbt("kst", [128, 8, TT], BF16)
            qrs = sbt("qrs", [64, 8, TT], BF16)
            krs = sbt("krs", [64, TT], BF16)
            vst = sbt("vst", [128, 4, D], BF16)
            pm = [pst("p1m%d" % i, [128, 512], F32) for i in range(6)]
            pms = pst("p1s", [128, 512], F32)
            ds_x = self.newds("x"); ds_cs = self.newds("cs"); ds_sn = self.newds("sn")
            ds_o = [self.newds("o") for _ in range(7)]
            k = 0

            def bank():
                nonlocal k
                b = pm[k % 6]; bk = ("p1m", k % 6); k += 1
                return b, bk

            deferred = []

            def defer(dsem, out, in_, reads=(), writes=()):
                deferred.append((dsem, out, in_, reads, writes))

            def flush_stores():
                for (dsem, out, in_, reads, writes) in deferred:
                    P.dma("sp", dsem, out, in_, reads=reads, writes=writes)
                del deferred[:]

            for s, L, x in (("A", self.SA, I["xA"]), ("B", self.SB, I["xB"])):
                for t in range(L // TT):
                    t0 = t * TT
                    P.dma("sp", ds_x, xin[:], x[t0:t0 + TT, :].rearrange("(j p) n -> p j n", p=128), writes=["xin"])
                    P.dma("sp", ds_cs, cs[:], I["cos" + s][:, t0:t0 + TT], writes=["cs"])
                    P.dma("sp", ds_sn, sn[:], I["sin" + s][:, t0:t0 + TT], writes=["sn"])
                    flush_stores()
                    for c in range(8):
                        b, bk = bank()
                        for j in range(4):
                            P.op("pe", lambda e, b=b, c=c, j=j: e.transpose(b[:, j * 128:(j + 1) * 128],
                                                                            xin[:, j, c * 128:(c + 1) * 128], C["identf"][:]),
                                 ["xin", "identf"], [bk])
                        self.copy(self.evac_eng(), xT[:, c, :], b[:, :], [bk], [("xT", c)])
                        P.op("act", lambda e, c=c: e.activation(out=sq[c % 2][:], in_=xT[:, c, :], func=AF.Square),
                             [("xT", c)], [("sq", c % 2)])
                        P.op("pe", lambda e, c=c: e.matmul(pms[:, :], lhsT=C["onesD"][:], rhs=sq[c % 2][:], start=(c == 0),
                                                           stop=(c == 7)), ["onesD", ("sq", c % 2)], ["pms"])
                    defer(ds_o[0], S["XT" + s][:, t0:t0 + TT].rearrange("(c p) n -> p c n", p=128), xT[:],
                          reads=[("xT", c) for c in range(8)], writes=[("scr", "XT" + s)])
                    self.rstd_from_ms(pms[:, :], "pms", lnv, rstd, "rstd")
                    for c in range(8):
                        P.op("dve", lambda e, c=c: e.scalar_tensor_tensor(
                            out=hn[:, c, :], in0=xT[:, c, :], scalar=C["gcols"][:, c:c + 1], in1=rstd[:],
                            op0=ALU.mult, op1=ALU.mult), [("xT", c), "gcols", "rstd"], [("hn", c)])
                    hnr = [("hn", c) for c in range(8)]
                    for grp in range(2):
                        for m in range(2):
                            mc = grp * 2 + m
                            b, bk = bank()
                            self.mm(b[:, :], [(w_in[:, c, mc * 128:(mc + 1) * 128], hn[:, c, :]) for c in range(8)],
                                    ["w_in"] + hnr, [bk])
                            P.op("act", lambda e, b=b, mc=mc: e.copy(out=cst[:, mc, :], in_=b[:, :]), [bk], [("cst", mc)])
                            P.op("dve", lambda e, mc=mc: e.tensor_tensor(out=sq[mc % 2][:], in0=cst[:, mc, :],
                                                                         in1=cst[:, mc, :], op=ALU.mult),
                                 [("cst", mc)], [("sq", mc % 2)])
                            P.op("pe", lambda e, mc=mc, m=m: e.matmul(pms[:, :], lhsT=C["ones256"][:], rhs=sq[mc % 2][:],
                                                                      start=(m == 0), stop=(m == 1)),
                                 ["ones256", ("sq", mc % 2)], ["pms"])
                        self.rstd_from_ms(pms[:, :], "pms", lnv, rstd, "rstd")
                        for m in range(2):
                            mc = grp * 2 + m
                            P.op("dve", lambda e, mc=mc: e.scalar_tensor_tensor(
                                out=cn[:, mc, :], in0=cst[:, mc, :], scalar=C["qkvg"][:, mc:mc + 1], in1=rstd[:],
                                op0=ALU.mult, op1=ALU.mult), [("cst", mc), "qkvg", "rstd"], [("cn", mc)])

                    def rope(psr, psh, kr_, kh_, out_ap, outkey):
                        P.op("dve", lambda e: e.tensor_tensor(out=tmp[0][:], in0=psr, in1=cs[:], op=ALU.mult),
                             [kr_, "cs"], ["rt0"])
                        P.op("dve", lambda e: e.tensor_tensor(out=tmp[1][:], in0=psh, in1=sn[:], op=ALU.mult),
                             [kh_, "sn"], ["rt1"])
                        P.op("dve", lambda e: e.tensor_tensor(out=out_ap, in0=tmp[0][:], in1=tmp[1][:], op=ALU.add),
                             ["rt0", "rt1"], [outkey])

                    b1, bk1 = bank()
                    self.mm(b1[0:64, :], [(w_in[:, c, 512:576], hn[:, c, :]) for c in range(8)], ["w_in"] + hnr, [bk1])
                    b2, bk2 = bank()
                    self.mm(b2[0:64, :], [(w_in[:, c, 576:640], hn[:, c, :]) for c in range(8)], ["w_in"] + hnr, [bk2])
                    rope(b1[0:64, :], b2[0:64, :], bk1, bk2, krs[:], "krs")
                    defer(ds_o[1], S["KR" + s][:, t0:t0 + TT], krs[:], reads=["krs"], writes=[("scr", "KR" + s)])
                    cq = [("cn", 0), ("cn", 1)]
                    ckv = [("cn", 2), ("cn", 3)]
                    for h in range(8):
                        b, bk = bank()
                        self.mm(b[:, :], [(w_uq[:, c, h * 192:h * 192 + 128], cn[:, c, :]) for c in range(2)],
                                ["w_uq"] + cq, [bk])
                        self.copy(self.evac_eng(), qst[:, h, :], b[:, :], [bk], [("qst", h)])
                        b1, bk1 = bank()
                        self.mm(b1[0:64, :], [(w_uq[:, c, h * 192 + 128:h * 192 + 192], cn[:, c, :]) for c in range(2)],
                                ["w_uq"] + cq, [bk1])
                        b2, bk2 = bank()
                        self.mm(b2[0:64, :], [(w_qrh[:, c, h * 64:(h + 1) * 64], cn[:, c, :]) for c in range(2)],
                                ["w_qrh"] + cq, [bk2])
                        rope(b1[0:64, :], b2[0:64, :], bk1, bk2, qrs[:, h, :], ("qrs", h))
                        b, bk = bank()
                        self.mm(b[:, :], [(w_ukv[:, c, h * 256:h * 256 + 128], cn[:, 2 + c, :]) for c in range(2)],
                                ["w_ukv"] + ckv, [bk])
                        self.copy(self.evac_eng(), kst[:, h, :], b[:, :], [bk], [("kst", h)])
                    defer(ds_o[2], S["Q" + s][:, t0:t0 + TT].rearrange("(h p) n -> p h n", p=128), qst[:],
                          reads=[("qst", h) for h in range(8)], writes=[("scr", "Q" + s)])
                    defer(ds_o[3], S["K" + s][:, t0:t0 + TT].rearrange("(h p) n -> p h n", p=128), kst[:],
                          reads=[("kst", h) for h in range(8)], writes=[("scr", "K" + s)])
                    defer(ds_o[4], S["QR" + s][:, :, t0:t0 + TT].rearrange("h p n -> p h n"), qrs[:],
                          reads=[("qrs", h) for h in range(8)], writes=[("scr", "QR" + s)])
                    wv4 = w_ukv[:].rearrange("p c (h f) -> p c h f", h=8)
                    for j in range(4):
                        for hf in range(2):
                            b, bk = bank()
                            self.mm(b[:, :].rearrange("p (h f) -> p h f", h=4),
                                    [(cn[:, 2 + c, j * 128:(j + 1) * 128], wv4[:, c, hf * 4:(hf + 1) * 4, 128:256])
                                     for c in range(2)], ["w_ukv"] + ckv, [bk])
                            self.copy(self.evac_eng(), vst[:, j, hf * 512:(hf + 1) * 512], b[:, :], [bk], [("vst", j)])
                    defer(ds_o[5], S["V" + s][t0:t0 + TT, :].rearrange("(j p) n -> p j n", p=128), vst[:],
                          reads=[("vst", j) for j in range(4)], writes=[("scr", "V" + s)])
            flush_stores()
            self.end_phase(skip_casts=True)

    def phase_attn(self, layer):
        nc, P, C, I, S = self.nc, self.P, self.C, self.I, self.S
        mla = (layer == 0)
        nbr = 1 if mla else 2
        with ExitStack() as st:
            sbt = lambda name, shape, dt: st.enter_context(nc.sbuf_tensor(self.un(name), list(shape), dt))
            pst = lambda name, shape, dt: st.enter_context(nc.psum_tensor(self.un(name), list(shape), dt))
            Lmax = self.SA
            nkmax = Lmax // 128
            Kt = [sbt("Kt%d" % i, [128, Lmax], BF16) for i in range(2)]
            Qt = [sbt("Qt%d" % i, [128, Lmax], BF16) for i in range(2)]
            Vt = [sbt("Vt%d" % i, [128, nkmax, 129], BF16) for i in range(2)]
            if mla:
                QRt = [sbt("QRt%d" % i, [128, Lmax], BF16) for i in range(2)]
                KRt = sbt("KRt", [128, Lmax], BF16)
                nsc, nacc = 0, 2
                scp = [pst("scp%d" % i, [128, 1024], F32) for i in range(2)]
                etp = [sbt("etp%d" % i, [128, 1024], BF16) for i in range(3)]
            else:
                stp = sbt("stp", [128, 8, GW], BF16)
                nsc, nacc = 0, 1
                scd = [pst("scd%d" % i, [128, 1024], F32) for i in range(2)]
                etd = [sbt("etd%d" % i, [128, 1024], BF16) for i in range(3)]
            for i in range(2):
                P.op("pool", lambda e, i=i: e.memset(Vt[i][:, :, 128:129], 1.0), [], [("Vt1", i)])
            ne = 4
            ost = [sbt("ost%d" % i, [128, 4, 128], BF16) for i in range(2)]
            rinv = [sbt("rinv%d" % i, [128, 8], F32) for i in range(2)]
            if not mla:
                tq = [sbt("tq%d" % i, [128, 128], F32) for i in range(2)]
                accs = [[sbt("accs%d_%d" % (br, hb), [128, 512], F32) for hb in range(2)] for br in range(2)]
                od = [sbt("od%d" % i, [128, 4, 128], F32) for i in range(2)]
                junk = sbt("junk", [128, 128], F32)
                ss = [sbt("ss%d" % i, [128, 4], F32) for i in range(2)]
                ssl = [sbt("ssl%d" % i, [128, 4], F32) for i in range(2)]
                eps128 = sbt("eps128", [128, 1], F32)
                P.op("dve", lambda e: e.memset(eps128[:], EPS), [], ["eps128"])
            sc = [[pst("sc%d_%d" % (br, i), [128, 512], F32) for i in range(nsc)] for br in range(nbr)]
            acc = [[[pst("acc%d_%d_%d" % (br, a, i), [128, 512], F32) for i in range(2)] for a in range(nacc)]
                   for br in range(nbr)]
            dsl = [[self.newds("al") for _ in range(4)] for _ in range(2)]
            dso = [self.newds("ao") for _ in range(2)]
            dkr = self.newds("akr")
            dkr2 = self.newds("akr2")
            dsl2 = [self.newds("al2") for _ in range(2)]
            scale = (192 ** -0.5) if mla else (64 ** -0.5)
            hidx = 0
            qbi = 0
            for si, (s, L) in enumerate((("A", self.SA), ("B", self.SB))):
                nk = L // 128
                Lq = L if (mla or s == "A") else self.HB
                nq = Lq // 512
                if mla:
                    P.dma("sp", dkr, KRt[0:64, 0:L], S["KR" + s][:, :], reads=[("scr", "KR" + s)], writes=["KRt"])
                    P.dma("sp", dkr2, KRt[64:128, 0:L], S["KR" + s][:, :], reads=[("scr", "KR" + s)], writes=["KRt2"])
                else:
                    P.dma("sp", dkr, stp[:], S["strip" + s][:, :, :], reads=[("scr", "strip" + s)], writes=["stp"])

                def load_head(h, hp):
                    P.dma("sp", dsl[hp][0], Kt[hp][:, 0:L], S["K" + s][h * 128:(h + 1) * 128, :],
                          reads=[("scr", "K" + s)], writes=[("Kt", hp)])
                    P.dma("sp", dsl[hp][1], Qt[hp][:, 0:Lq], S["Q" + s][h * 128:(h + 1) * 128, 0:Lq],
                          reads=[("scr", "Q" + s)], writes=[("Qt", hp)])
                    P.dma("sp", dsl[hp][2], Vt[hp][:, 0:nk, 0:128],
                          S["V" + s][:, h * 128:(h + 1) * 128].rearrange("(k p) d -> p k d", p=128),
                          reads=[("scr", "V" + s)], writes=[("Vt", hp)])
                    if mla:
                        P.dma("sp", dsl[hp][3], QRt[hp][0:64, 0:L], S["QR" + s][h, :, :],
                              reads=[("scr", "QR" + s)], writes=[("QRt", hp)])
                        P.dma("sp", dsl2[hp], QRt[hp][64:128, 0:L], S["QR" + s][h, :, :],
                              reads=[("scr", "QR" + s)], writes=[("QRt2", hp)])

                load_head(0, hidx % 2)
                for h in range(8):
                    hp = hidx % 2
                    if h + 1 < 8:
                        load_head(h + 1, (hidx + 1) % 2)
                    rd_k = [("Kt", hp)] + (["KRt"] if mla else [])
                    rd_q = [("Qt", hp)] + ([("QRt", hp)] if mla else [])
                    tiles = [(qb, kc) for qb in range(nq) for kc in range(nk)]
                    pend = None
                    ti = 0

                    def emit_pv(qb, kc, slot, aset):
                        for br in range(nbr):
                            for j in range(4):
                                a = acc[br][aset][j // 2]
                                P.op("pe", lambda e, a=a, j=j, br=br, slot=slot, kc=kc, hp=hp,
                                     st_=(kc == 0 and j % 2 == 0), sp_=(kc == nk - 1 and j % 2 == 1): e.matmul(
                                    a[:, (j % 2) * 256:(j % 2) * 256 + 129],
                                    lhsT=etd[slot][:, br * 512 + j * 128:br * 512 + (j + 1) * 128], rhs=Vt[hp][:, kc, :],
                                    start=st_, stop=sp_, skip_group_check=True),
                                    [("ed", slot), ("Vt", hp), ("Vt1", hp)], [("acc", br, aset, j // 2)])

                    def emit_epilogue(qb, aset):
                        nonlocal qbi
                        op_ = qbi % 2
                        qbi += 1
                        o = ost[op_]
                        if mla:
                            for j in range(4):
                                a = acc[0][aset][j // 2]
                                c0 = (j % 2) * 256
                                P.op("dve", lambda e, a=a, c0=c0, j=j: e.reciprocal(out=rinv[op_][:, j:j + 1],
                                                                                   in_=a[:, c0 + 128:c0 + 129]),
                                     [("acc", 0, aset, j // 2)], [("rinv", op_, j)])
                                P.op("dve", lambda e, a=a, c0=c0, j=j: e.tensor_scalar(
                                    out=o[:, j, :], in0=a[:, c0:c0 + 128], scalar1=rinv[op_][:, j:j + 1], scalar2=None,
                                    op0=ALU.mult), [("acc", 0, aset, j // 2), ("rinv", op_, j)], [("ost", op_)])
                        else:
                            for br in range(2):
                                for hb in range(2):
                                    P.op("dve", lambda e, br=br, hb=hb: e.tensor_copy(out=accs[br][hb][:, 0:385],
                                                                                      in_=acc[br][aset][hb][:, 0:385]),
                                         [("acc", br, aset, hb)], [("accs", br, hb)])
                            P.op("dve", lambda e: e.memset(ss[op_][:], 0.0), [], [("ss", op_, j) for j in range(4)])
                            for j in range(4):
                                a0 = accs[0][j // 2]
                                a1 = accs[1][j // 2]
                                c0 = (j % 2) * 256
                                P.op("dve", lambda e, a0=a0, c0=c0, j=j: e.reciprocal(out=rinv[op_][:, j:j + 1],
                                                                                     in_=a0[:, c0 + 128:c0 + 129]),
                                     [("accs", 0, j // 2)], [("rinv", op_, j)])
                                P.op("dve", lambda e, a1=a1, c0=c0, j=j: e.reciprocal(out=rinv[op_][:, 4 + j:5 + j],
                                                                                     in_=a1[:, c0 + 128:c0 + 129]),
                                     [("accs", 1, j // 2)], [("rinv", op_, 4 + j)])
                                P.op("dve", lambda e, j=j: e.tensor_tensor(out=rinv[op_][:, 4 + j:5 + j],
                                                                           in0=rinv[op_][:, 4 + j:5 + j], in1=C["nlam"][:],
                                                                           op=ALU.mult),
                                     [("rinv", op_, 4 + j), "nlam"], [("rinv", op_, 4 + j)])
                                P.op("dve", lambda e, a0=a0, c0=c0, j=j: e.tensor_scalar(
                                    out=tq[j % 2][:], in0=a0[:, c0:c0 + 128], scalar1=rinv[op_][:, j:j + 1], scalar2=None,
                                    op0=ALU.mult), [("accs", 0, j // 2), ("rinv", op_, j)], [("tq", j % 2)])
                                P.op("dve", lambda e, a1=a1, c0=c0, j=j: e.scalar_tensor_tensor(
                                    out=od[op_][:, j, :], in0=a1[:, c0:c0 + 128], scalar=rinv[op_][:, 4 + j:5 + j],
                                    in1=tq[j % 2][:], op0=ALU.mult, op1=ALU.add),
                                    [("accs", 1, j // 2), ("rinv", op_, 4 + j), ("tq", j % 2)], [("od", op_, j)])
                                P.op("dve", lambda e, j=j: e.scalar_tensor_tensor(
                                    out=junk[:], in0=od[op_][:, j, :], scalar=1.0 / 128, in1=od[op_][:, j, :],
                                    op0=ALU.mult, op1=ALU.mult, accum_out=ss[op_][:, j:j + 1]),
                                    [("od", op_, j), ("ss", op_, j)], ["junk", ("ss", op_, j)])
                            def part2(op_=op_, o=o, qb=qb):
                                ssk = [("ss", op_, j) for j in range(4)]
                                P.op("act", lambda e: e.activation(out=ssl[op_][:], in_=ss[op_][:], func=AF.Ln,
                                                                   bias=eps128[:], scale=1.0), ssk + ["eps128"],
                                     [("ssl", op_)])
                                P.op("act", lambda e: e.activation(out=ssl[op_][:], in_=ssl[op_][:], func=AF.Exp,
                                                                   scale=-0.5), [("ssl", op_)], [("ssl", op_)])
                                for j in range(4):
                                    P.op("dve", lambda e, j=j: e.tensor_scalar(out=o[:, j, :], in0=od[op_][:, j, :],
                                                                               scalar1=ssl[op_][:, j:j + 1], scalar2=None,
                                                                               op0=ALU.mult),
                                         [("od", op_, j), ("ssl", op_)], [("ost", op_)])
                                q0 = qb * 512
                                P.dma("pool", dso[op_],
                                      S["O" + s][q0:q0 + 512, h * 128:(h + 1) * 128].rearrange("(j p) d -> p j d", p=128),
                                      o[:], reads=[("ost", op_)], writes=[("scr", "O" + s)])
                            pending2.append([12, part2])
                            return
                        q0 = qb * 512
                        P.dma("pool", dso[op_],
                              S["O" + s][q0:q0 + 512, h * 128:(h + 1) * 128].rearrange("(j p) d -> p j d", p=128), o[:],
                              reads=[("ost", op_)], writes=[("scr", "O" + s)])

                    pending2 = []

                    def tick_pending(force=False):
                        for p in list(pending2):
                            p[0] -= 1
                            if force or p[0] <= 0:
                                p[1]()
                                pending2.remove(p)

                    if mla:
                        pairs = [(qb, kp) for qb in range(nq) for kp in range(nk // 2)]
                        ppend = None

                        def emit_pv_pair(qb, kp, slot, aset):
                            for half in range(2):
                                kc = 2 * kp + half
                                for j in range(4):
                                    a = acc[0][aset][j // 2]
                                    P.op("pe", lambda e, a=a, j=j, slot=slot, kc=kc, hp=hp, half=half,
                                         st_=(kc == 0 and j % 2 == 0), sp_=(kc == nk - 1 and j % 2 == 1): e.matmul(
                                        a[:, (j % 2) * 256:(j % 2) * 256 + 129],
                                        lhsT=etp[slot][:, half * 512 + j * 128:half * 512 + (j + 1) * 128],
                                        rhs=Vt[hp][:, kc, :], start=st_, stop=sp_, skip_group_check=True),
                                        [("ep", slot), ("Vt", hp), ("Vt1", hp)], [("acc", 0, aset, j // 2)])

                        for pi_, (qb, kp) in enumerate(pairs):
                            slot_s = pi_ % 2
                            slot_e = pi_ % 3
                            aset = qb % nacc
                            q0 = qb * 512
                            scb = scp[slot_s]
                            for half in range(2):
                                kc = 2 * kp + half
                                P.op("pe", lambda e, scb=scb, kc=kc, q0=q0, hp=hp, half=half: e.matmul(
                                    scb[:, half * 512:(half + 1) * 512], lhsT=Kt[hp][:, kc * 128:(kc + 1) * 128],
                                    rhs=Qt[hp][:, q0:q0 + 512], start=True, stop=False),
                                    [("Kt", hp), ("Qt", hp)], [("scp", slot_s)])
                            for half in range(2):
                                kc = 2 * kp + half
                                r0 = half * 64
                                P.op("pe", lambda e, scb=scb, kc=kc, q0=q0, hp=hp, half=half, r0=r0: e.matmul(
                                    scb[:, half * 512:(half + 1) * 512], lhsT=KRt[r0:r0 + 64, kc * 128:(kc + 1) * 128],
                                    rhs=QRt[hp][r0:r0 + 64, q0:q0 + 512], start=False, stop=True),
                                    ["KRt", "KRt2", ("QRt", hp), ("QRt2", hp)], [("scp", slot_s)])
                            eb = etp[slot_e]
                            P.op("act", lambda e, eb=eb, scb=scb: e.activation(out=eb[:], in_=scb[:, :], func=AF.Exp,
                                                                               scale=float(scale)),
                                 [("scp", slot_s)], [("ep", slot_e)])
                            if ppend is not None:
                                emit_pv_pair(*ppend)
                                if ppend[1] == nk // 2 - 1:
                                    emit_epilogue(ppend[0], ppend[3])
                            ppend = (qb, kp, slot_e, aset)
                        emit_pv_pair(*ppend)
                        emit_epilogue(ppend[0], ppend[3])
                        hidx += 1
                        continue
                    for (qb, kc) in tiles:
                        slot_s = ti % 2
                        slot_e = ti % 3
                        aset = 0
                        q0 = qb * 512
                        m = kc - 4 * qb
                        scb = scd[slot_s]
                        near = (-1 <= m <= 4)
                        for br in range(2):
                            r0 = br * 64
                            P.op("pe", lambda e, scb=scb, kc=kc, q0=q0, r0=r0, near=near, hp=hp, br=br: e.matmul(
                                scb[:, br * 512:(br + 1) * 512], lhsT=Kt[hp][r0:r0 + 64, kc * 128:(kc + 1) * 128],
                                rhs=Qt[hp][r0:r0 + 64, q0:q0 + 512], start=True, stop=(not near)),
                                rd_k + rd_q, [("scd", slot_s)])
                        if near:
                            off = 512 - 128 * m
                            for br in range(2):
                                P.op("pe", lambda e, scb=scb, off=off, h=h, br=br: e.matmul(
                                    scb[:, br * 512:(br + 1) * 512], lhsT=C["identb"][:], rhs=stp[:, h, off:off + 512],
                                    start=False, stop=True), ["identb", "stp"], [("scd", slot_s)])
                            bias_ap = None
                        else:
                            bias_ap = C["cst"][:, si, h, 0:1] if m < -1 else C["cst"][:, si, h, 1:2]
                        eb = etd[slot_e]
                        if bias_ap is None:
                            P.op("act", lambda e, eb=eb, scb=scb: e.activation(out=eb[:], in_=scb[:, :], func=AF.Exp,
                                                                               scale=float(scale)),
                                 [("scd", slot_s)], [("ed", slot_e)])
                        else:
                            P.op("act", lambda e, eb=eb, scb=scb, bias_ap=bias_ap: e.activation(
                                out=eb[:], in_=scb[:, :], func=AF.Exp, bias=bias_ap, scale=float(scale)),
                                [("scd", slot_s), "cst"], [("ed", slot_e)])
                        if pend is not None:
                            emit_pv(*pend)
                            if pend[1] == nk - 1:
                                emit_epilogue(pend[0], pend[3])
                        pend = (qb, kc, slot_e, aset)
                        ti += 1
                        tick_pending()
                    emit_pv(*pend)
                    emit_epilogue(pend[0], pend[3])
                    tick_pending(force=True)
                    hidx += 1
            self.end_phase()

    def phase_tail(self, layer):
        nc, P, C, I, S = self.nc, self.P, self.C, self.I, self.S
        last = (layer == 1)
        with ExitStack() as st:
            sbt = lambda name, shape, dt: st.enter_context(nc.sbuf_tensor(self.un(name), list(shape), dt))
            pst = lambda name, shape, dt: st.enter_context(nc.psum_tensor(self.un(name), list(shape), dt))
            NSLOT = 3
            ring = [sbt("ring%d" % i, [128, 8, D], BF16) for i in range(NSLOT)]
            dsr = [self.newds("r") for _ in range(NSLOT)]
            xT = [sbt("xT%d" % i, [128, 8, TT], F32) for i in range(2)]
            oin = sbt("oin", [128, 4, D], BF16)
            hn = sbt("hn", [128, 8, TT], BF16)
            ys = sbt("ys", [128, 8, TT], F32)
            hid = sbt("hid", [128, 32, TT], BF16)
            sq8 = sbt("sq8", [128, 8, TT], BF16)
            rl = [sbt("rl%d" % i, [128, TT], F32) for i in range(4)]
            lnv = sbt("lnv", [128, TT], F32)
            lnx = sbt("lnx", [128, TT], F32)
            rstd = sbt("rstd", [128, TT], F32)
            rinv = sbt("rinv", [128, TT], F32)
            ex = [sbt("ex%d" % i, [128, 2, TT], BF16) for i in range(2)]
            mK = sbt("mK", [128, 8, NMEM], BF16)
            mV = sbt("mV", [128, 2, D], BF16)
            yout = ys[:].rearrange("p (j a) n -> p j (a n)", j=4)
            pm = [pst("ptm%d" % i, [128, 512], F32) for i in range(5)]
            pms = pst("pts", [128, 512], F32)
            ptrs = [pst("ptr%d" % i, [128, 1024], BF16) for i in range(2)]
            pmx = pms
            ds_x = [self.newds("tx") for _ in range(2)]
            ds_oi = self.newds("toi")
            ds_m = [self.newds("tm") for _ in range(2)]
            ds_st = [self.newds("ts") for _ in range(5)]
            k = 0

            def bank():
                nonlocal k
                b = pm[k % 5]; bk = ("ptm", k % 5); k += 1
                return b, bk

            g0 = layer * 48
            slices = []
            w_o = S["w_mla_o"] if layer == 0 else S["w_diff_o"]
            slices.append(("wo", w_o))
            slices.append(("xq", S["w_xq"][layer]))
            slices.append(("xo", S["w_xo"][layer]))
            for i in range(4):
                slices.append(("w1_%d" % i, S["w_1"][layer][:, i * D:(i + 1) * D]))
            for i in range(4):
                slices.append(("w2_%d" % i, S["w_2"][layer][i * D:(i + 1) * D, :]))
            if not last:
                for i in range(3):
                    slices.append(("win_%d" % i, S["w_diff_in"][:, i * D:(i + 1) * D]))
            wname = {"wo": ("scr", w_o.tensor.name), "xq": ("scr", "w_xq"), "xo": ("scr", "w_xo")}
            self._ring_seq = []
            state = {"issued": 0, "used": 0}
            plan = []

            for s, L in (("A", self.SA), ("B", self.SB)):
                Lq = L if (not last or s == "A") else self.HB
                for t in range(Lq // TT):
                    own = (s == "A") or (t * TT < self.HB)
                    sl = list(slices)
                    if (not last) and (not own):
                        sl = [x for x in sl if x[0] != "win_0"]
                    plan.append((s, t, own, sl))
            flat = [(pi, sname, ap) for pi, (_, _, _, sl) in enumerate(plan) for (sname, ap) in sl]

            def issue_next():
                i = state["issued"]
                if i >= len(flat):
                    return
                _, sname, ap = flat[i]
                slot = i % NSLOT
                P.dma("sp", dsr[slot], ring[slot][:], ap.rearrange("(c p) n -> p c n", p=128),
                      reads=[("scr", ap.tensor.name)], writes=[("ring", slot)])
                state["issued"] += 1

            def next_slice(expect):
                i = state["used"]
                assert flat[i][1] == expect, (flat[i][1], expect)
                while state["issued"] < min(len(flat), i + NSLOT):
                    issue_next()
                state["used"] += 1
                slot = i % NSLOT
                return ring[slot], ("ring", slot)

            def load_tile_inputs(pi):
                s, t, own, _ = plan[pi]
                par = pi % 2
                t0 = t * TT
                P.dma("sp", ds_x[par], xT[par][:], S["XT" + s][:, t0:t0 + TT].rearrange("(c p) n -> p c n", p=128),
                      reads=[("scr", "XT" + s, t)], writes=[("xT", par, c) for c in range(8)])

            def sq_op(c, src_ap, srckey):
                if c % 2 == 0:
                    P.op("act", lambda e, c=c: e.activation(out=sq8[:, c, :], in_=src_ap, func=AF.Square),
                         [srckey], [("sq8", c)])
                else:
                    P.op("dve", lambda e, c=c: e.tensor_tensor(out=sq8[:, c, :], in0=src_ap, in1=src_ap, op=ALU.mult),
                         [srckey], [("sq8", c)])

            def ones_mm(c):
                P.op("pe", lambda e, c=c: e.matmul(pms[:, :], lhsT=C["onesD"][:], rhs=sq8[:, c, :], start=(c == 0),
                                                   stop=(c == 7)), ["onesD", ("sq8", c)], ["pms"])

            def stats_mm():
                for c in range(8):
                    ones_mm(c)
                self.rstd_from_ms(pms[:, :], "pms", lnv, rstd, "rstd")

            def pre_norm(par, gbase):
                for c in range(8):
                    sq_op(c, xT[par][:, c, :], ("xT", par, c))
                stats_mm()
                for c in range(8):
                    P.op("dve", lambda e, c=c: e.scalar_tensor_tensor(
                        out=hn[:, c, :], in0=xT[par][:, c, :], scalar=C["gcols"][:, gbase + c:gbase + c + 1], in1=rstd[:],
                        op0=ALU.mult, op1=ALU.mult), [("xT", par, c), "gcols", "rstd"], [("hn", c)])

            def pre_scale(par, gbase):
                for c in range(8):
                    P.op("dve", lambda e, c=c: e.tensor_scalar(
                        out=hn[:, c, :], in0=xT[par][:, c, :], scalar1=C["gcols"][:, gbase + c:gbase + c + 1],
                        scalar2=None, op0=ALU.mult), [("xT", par, c), "gcols"], [("hn", c)])
                    P.op("act", lambda e, c=c: e.activation(out=sq8[:, c, :], in_=xT[par][:, c, :], func=AF.Square),
                         [("xT", par, c)], [("sq8", c)])

            def out_proj_post(par, w, wkey, src, srckeys, gbase):
                for mc in range(8):
                    b, bk = bank()
                    self.mm(b[:, :], [(w[:, c, mc * 128:(mc + 1) * 128], src[:, c, :]) for c in range(8)],
                            [wkey] + srckeys, [bk])
                    if mc >= 1:
                        ones_mm(mc - 1)
                    P.op("act", lambda e, b=b, mc=mc: e.copy(out=ys[:, mc, :], in_=b[:, :]), [bk], [("ys", mc)])
                    P.op("dve", lambda e, mc=mc: e.tensor_tensor(out=sq8[:, mc, :], in0=ys[:, mc, :], in1=ys[:, mc, :],
                                                                 op=ALU.mult), [("ys", mc)], [("sq8", mc)])
                ones_mm(7)
                post(par, gbase)

            def post(par, gbase):
                self.rstd_from_ms(pms[:, :], "pms", lnv, rstd, "rstd")
                for c in range(8):
                    P.op("dve", lambda e, c=c: e.scalar_tensor_tensor(
                        out=ys[:, c, :], in0=ys[:, c, :], scalar=C["gcols"][:, gbase + c:gbase + c + 1], in1=rstd[:],
                        op0=ALU.mult, op1=ALU.mult), [("ys", c), "gcols", "rstd"], [("ys", c)])
                    P.op("pool", lambda e, c=c: e.tensor_tensor(out=xT[par][:, c, :], in0=xT[par][:, c, :],
                                                                in1=ys[:, c, :], op=ALU.add),
                         [("xT", par, c), ("ys", c)], [("xT", par, c)])

            if layer == 0:
                P.wait_all("sp", [d.last for d in self.dspool.get("wc", []) if d.last is not None])
            cur_seq = None
            load_tile_inputs(0)
            for pi, (s, t, own, sl) in enumerate(plan):
                par = pi % 2
                t0 = t * TT
                if s != cur_seq:
                    cur_seq = s
                    P.dma("sp", ds_m[0], mK[:], S["mK%d%s" % (layer, s)][:, :, :],
                          reads=[("scr", "mK%d%s" % (layer, s))], writes=["mK"])
                    P.dma("sp", ds_m[1], mV[:], S["mV%d%s" % (layer, s)][:, :, :],
                          reads=[("scr", "mV%d%s" % (layer, s))], writes=["mV"])
                P.dma("sp", ds_oi, oin[:], S["O" + s][t0:t0 + TT, :].rearrange("(j p) n -> p j n", p=128),
                      reads=[("scr", "O" + s)], writes=["oin"])
                if pi + 1 < len(plan):
                    load_tile_inputs(pi + 1)
                for c in range(8):
                    for j in range(4):
                        P.op("pe", lambda e, c=c, j=j: e.transpose(ptrs[c % 2][:, j * 128:(j + 1) * 128],
                                                                   oin[:, j, c * 128:(c + 1) * 128], C["identb"][:]),
                             ["oin", "identb"], [("ptr", c % 2)])
                    self.copy(self.evac_eng(), hn[:, c, :], ptrs[c % 2][:, 0:512], [("ptr", c % 2)],
                              [("hn", c)])
                hnk = [("hn", c) for c in range(8)]
                w, wk = next_slice("wo")
                out_proj_post(par, w, wk, hn, hnk, g0 + 8)
                pre_scale(par, g0 + 16)
                w, wk = next_slice("xq")
                for mc in range(8):
                    b, bk = bank()
                    self.mm(b[:, :], [(w[:, c, mc * 128:(mc + 1) * 128], hn[:, c, :]) for c in range(8)], [wk] + hnk, [bk])
                    if mc == 0:
                        stats_mm()
                    P.op("dve", lambda e, b=b, mc=mc: e.tensor_tensor(out=hid[:, mc, :], in0=b[:, :], in1=rstd[:],
                                                                      op=ALU.mult), [bk, "rstd"], [("hid", mc)])
                xs = 256 ** -0.5

                def xscores(h):
                    xe = ex[h % 2]
                    for mcn in range(2):
                        b, bk = bank()
                        self.mm(b[:, :], [(mK[:, 2 * h + fc, mcn * 128:(mcn + 1) * 128], hid[:, 2 * h + fc, :])
                                          for fc in range(2)], ["mK", ("hid", 2 * h), ("hid", 2 * h + 1)], [bk])
                        P.op("act", lambda e, b=b, xe=xe, mcn=mcn: e.activation(out=xe[:, mcn, :], in_=b[:, :], func=AF.Exp,
                                                                                scale=float(xs)),
                             [bk], [("ex", h % 2, mcn)])

                def xfinish(h):
                    xe = ex[h % 2]
                    exk = [("ex", h % 2, 0), ("ex", h % 2, 1)]
                    self.mm(pmx[:, :], [(C["ones1"][:], xe[:, mcn, :]) for mcn in range(2)], ["ones1"] + exk, ["pms"])
                    P.op("act", lambda e: e.activation(out=lnx[:], in_=pmx[:, :], func=AF.Ln), ["pms"], ["lnx"])
                    P.op("act", lambda e: e.activation(out=rinv[:], in_=lnx[:], func=AF.Exp, scale=-1.0),
                         ["lnx"], ["rinv"])
                    for dc in range(2):
                        b, bk = bank()
                        self.mm(b[:, :], [(mV[:, mcn, h * 256 + dc * 128:h * 256 + (dc + 1) * 128], xe[:, mcn, :])
                                          for mcn in range(2)], ["mV"] + exk, [bk])
                        P.op("dve", lambda e, b=b, h=h, dc=dc: e.tensor_tensor(out=hn[:, 2 * h + dc, :], in0=b[:, :],
                                                                               in1=rinv[:], op=ALU.mult),
                             [bk, "rinv"], [("hn", 2 * h + dc)])

                for h in range(5):
                    if h < 4:
                        xscores(h)
                    if h >= 1:
                        xfinish(h - 1)
                w, wk = next_slice("xo")
                out_proj_post(par, w, wk, hn, hnk, g0 + 24)
                pre_scale(par, g0 + 32)
                for i in range(4):
                    w, wk = next_slice("w1_%d" % i)
                    for mc in range(8):
                        b, bk = bank()
                        self.mm(b[:, :], [(w[:, c, mc * 128:(mc + 1) * 128], hn[:, c, :]) for c in range(8)], [wk] + hnk,
                                [bk])
                        if i == 0 and mc == 0:
                            stats_mm()
                        ri = (i * 8 + mc) % 4
                        r = rl[ri]
                        P.op("dve", lambda e, b=b, r=r: e.scalar_tensor_tensor(out=r[:], in0=b[:, :], scalar=0.0, in1=rstd[:],
                                                                               op0=ALU.max, op1=ALU.mult),
                             [bk, "rstd"], [("rl", ri)])
                        P.op("pool", lambda e, r=r, i=i, mc=mc: e.tensor_tensor(out=hid[:, i * 8 + mc, :], in0=r[:], in1=r[:],
                                                                                 op=ALU.mult),
                             [("rl", ri)], [("hid", i * 8 + mc)])
                for i in range(4):
                    w, wk = next_slice("w2_%d" % i)
                    for mc in range(8):
                        b, bk = bank()
                        self.mm(b[:, :], [(w[:, c, mc * 128:(mc + 1) * 128], hid[:, i * 8 + c, :]) for c in range(8)],
                                [wk] + [("hid", i * 8 + c) for c in range(8)], [bk])
                        if i == 0:
                            P.op("act", lambda e, b=b, mc=mc: e.copy(out=ys[:, mc, :], in_=b[:, :]), [bk], [("ys", mc)])
                        else:
                            P.op("dve", lambda e, b=b, mc=mc: e.tensor_tensor(out=ys[:, mc, :], in0=ys[:, mc, :], in1=b[:, :],
                                                                              op=ALU.add), [bk, ("ys", mc)], [("ys", mc)])
                        if i == 3:
                            sq_op(mc, ys[:, mc, :], ("ys", mc))
                            if mc >= 1:
                                ones_mm(mc - 1)
                ones_mm(7)
                post(par, g0 + 40)
                xk = [("xT", par, c) for c in range(8)]
                if not last:
                    if own:
                        P.dma("pool", ds_st[0], S["XT" + s][:, t0:t0 + TT].rearrange("(c p) n -> p c n", p=128), xT[par][:],
                              reads=xk, writes=[("scr", "XT" + s, t)])
                    pre_norm(par, 48)
                    for i in range(3):
                        if i == 0 and not own:
                            continue
                        w, wk = next_slice("win_%d" % i)
                        if i < 2:
                            for mc in range(8):
                                b, bk = bank()
                                self.mm(b[:, :], [(w[:, c, mc * 128:(mc + 1) * 128], hn[:, c, :]) for c in range(8)],
                                        [wk] + hnk, [bk])
                                self.copy(self.evac_eng(), hid[:, i * 8 + mc, :], b[:, :], [bk], [("hid", i * 8 + mc)])
                            dst = S[("Q" if i == 0 else "K") + s]
                            P.dma("pool", ds_st[1 + i], dst[:, t0:t0 + TT].rearrange("(c p) n -> p c n", p=128),
                                  hid[:, i * 8:(i + 1) * 8, :], reads=[("hid", i * 8 + c) for c in range(8)],
                                  writes=[("scr", dst.tensor.name)])
                        else:
                            vv = hid[:, 16:24, :].rearrange("p (j a) n -> p j (a n)", j=4)
                            for j in range(4):
                                for hf in range(2):
                                    b, bk = bank()
                                    self.mm(b[:, :], [(hn[:, c, j * 128:(j + 1) * 128], w[:, c, hf * 512:(hf + 1) * 512])
                                                      for c in range(8)], [wk] + hnk, [bk])
                                    self.copy(self.evac_eng(), vv[:, j, hf * 512:(hf + 1) * 512], b[:, :], [bk],
                                              [("hid", 16 + 2 * j + hf)])
                            P.dma("pool", ds_st[3], S["V" + s][t0:t0 + TT, :].rearrange("(j p) n -> p j n", p=128), vv,
                                  reads=[("hid", 16 + c) for c in range(8)], writes=[("scr", "V" + s)])
                else:
                    for j in range(4):
                        for hf in range(2):
                            b, bk = bank()
                            for cc in range(4):
                                c = hf * 4 + cc
                                P.op("pe", lambda e, b=b, c=c, cc=cc, j=j, par=par: e.transpose(
                                    b[:, cc * 128:(cc + 1) * 128], xT[par][:, c, j * 128:(j + 1) * 128], C["identf"][:]),
                                    [("xT", par, c), "identf"], [bk])
                            self.copy(self.evac_eng(), yout[:, j, hf * 512:(hf + 1) * 512], b[:, :], [bk], [("ys", 2 * j + hf)])
                    ydst = self.yA if s == "A" else self.yB
                    P.dma("pool", ds_st[4], ydst[t0:t0 + TT, :].rearrange("(j p) n -> p j n", p=128), yout,
                          reads=[("ys", c) for c in range(8)], writes=[("out", s)])
            assert state["used"] == len(flat)
            self.end_phase()


def _t5_bucket(rel):
    nb = 16
    max_exact = 8
    rel = np.asarray(rel, dtype=np.int64)
    ret = (rel > 0).astype(np.int64) * nb
    n = np.abs(rel)
    nf = np.maximum(n, 1).astype(np.float32)
    val = np.log(nf / np.float32(max_exact)) / np.float32(math.log(128 / max_exact)) * np.float32(nb - max_exact)
    large = max_exact + val.astype(np.float32).astype(np.int32).astype(np.int64)
    large = np.minimum(large, nb - 1)
    return ret + np.where(n < max_exact, n, large)


def _onehot(sign):
    m = np.arange(GL)
    rel = sign * (639 - m)
    b = _t5_bucket(rel)
    oh = np.zeros((32, GL), np.float32)
    oh[b, m] = 1.0
    return oh


def _rope_tables(pos):
    half = 32
    freqs = (np.float32(10000.0) ** (-np.arange(half, dtype=np.float32) / np.float32(half))).astype(np.float32)
    ang = (pos.astype(np.float32)[None, :] * freqs[:, None]).astype(np.float32)
    c = np.cos(ang).astype(np.float32)
    s = np.sin(ang).astype(np.float32)
    return np.ascontiguousarray(np.concatenate([c, c], 0)), np.ascontiguousarray(np.concatenate([s, s], 0))


_CACHE = {}
_DBG = None


def kernel(x_prompt, x_sample, mem_prompt, mem_sample, norm_gains, rel_bias_table,
           mla_w_in, mla_q_norm, mla_kv_norm, mla_w_uq, mla_w_ukv, mla_w_o,
           diff_w_in, diff_lambda, diff_subln, diff_w_o,
           xattn_mem_norm, xattn_w_q, xattn_w_kv, xattn_w_o, mlp_w1, mlp_w2):
    f = lambda a: np.ascontiguousarray(np.asarray(a, dtype=np.float32))
    x_prompt, x_sample, mem_prompt, mem_sample = f(x_prompt), f(x_sample), f(mem_prompt), f(mem_sample)
    NB, SB, _ = x_prompt.shape
    NA, SA, _ = x_sample.shape
    assert NA == 8 and NB == 4
    HB = SB // 2
    key = (SA, SB)
    if key not in _CACHE:
        _CACHE[key] = Builder(SA, SB).build()
    nc = _CACHE[key]

    def cols(v):
        v = f(v).reshape(-1, 128)
        return np.ascontiguousarray(v.T)

    shared = {
        "gcols": cols(f(norm_gains).reshape(-1)),
        "memg": cols(f(xattn_mem_norm).reshape(-1)),
        "qkvg": np.ascontiguousarray(np.concatenate([cols(f(mla_q_norm)[0]), cols(f(mla_kv_norm)[0])], 1)),
        "subg": cols(f(diff_subln)[0]),
        "lam": f(diff_lambda)[0].reshape(1, 256),
        "table": f(rel_bias_table),
        "ohA": _onehot(1),
        "ident": np.eye(128, dtype=np.float32),
        "anti": np.ascontiguousarray(np.eye(128, dtype=np.float32)[::-1]),
        "mla_w_in": f(mla_w_in)[0], "mla_w_uq": f(mla_w_uq)[0], "mla_w_ukv": f(mla_w_ukv)[0], "mla_w_o": f(mla_w_o)[0],
        "diff_w_in": f(diff_w_in)[0], "diff_w_o": f(diff_w_o)[0],
        "xattn_w_q": f(xattn_w_q), "xattn_w_kv": f(xattn_w_kv), "xattn_w_o": f(xattn_w_o),
        "mlp_w1": f(mlp_w1), "mlp_w2": f(mlp_w2),
    }
    cosA, sinA = _rope_tables(np.arange(SA))
    shared["cosA"], shared["sinA"] = cosA, sinA
    tabs = {}
    for p in range(2):
        pos = np.arange(SB) if p == 0 else (SB - 1 - np.arange(SB))
        tabs[p] = _rope_tables(pos) + (_onehot(1 if p == 0 else -1),)
    in_maps = []
    for c in range(8):
        b, p = c // 2, c % 2
        m = dict(shared)
        m["xA"] = x_sample[c]
        m["xB"] = x_prompt[b] if p == 0 else np.ascontiguousarray(x_prompt[b][::-1])
        m["memA"] = mem_sample[c]
        m["memB"] = mem_prompt[b]
        m["cosB"], m["sinB"], m["ohB"] = tabs[p]
        in_maps.append(m)
    res = run_bass_kernel_spmd(nc, in_maps, core_ids=list(range(8)))
    global _DBG
    _DBG = res.results
    y_sample = np.stack([np.asarray(res.results[c]["yA"], dtype=np.float32) for c in range(8)], 0)
    y_prompt = np.empty((NB, SB, D), np.float32)
    for c in range(8):
        b, p = c // 2, c % 2
        yb = np.asarray(res.results[c]["yB"], dtype=np.float32)
        if p == 0:
            y_prompt[b, :HB] = yb
        else:
            y_prompt[b, HB:] = yb[::-1]
    return (y_prompt, y_sample)
```

```python
import math
from contextlib import ExitStack

import numpy as np
import concourse.bass as bass
import concourse.mybir as mybir
from concourse.bass_utils import run_bass_kernel_spmd

F32 = mybir.dt.float32
BF16 = mybir.dt.bfloat16
AF = mybir.ActivationFunctionType
ALU = mybir.AluOpType

import os as _os
SAME_ENGINE_SYNC = set(_os.environ.get('SES', 'act,dve,pool').split(','))
ENGS = ("pe", "act", "dve", "pool", "sp")

D = 1024
NC8 = 8
TT = 512
NMEM = 256
EPS = 1e-6
GW = 1152
GL = 1280


class Ins:
    __slots__ = ("eng", "fn", "deps", "needed", "val", "dsem", "flushed")

    def __init__(self, eng, fn, dsem=None):
        self.eng = eng
        self.fn = fn
        self.deps = ()
        self.needed = False
        self.val = 0
        self.dsem = dsem
        self.flushed = False


class DSem:
    def __init__(self, handle, name):
        self.handle = handle
        self.name = name
        self.count = 0
        self.last = None
        self.queue = None


class BufState:
    __slots__ = ("w", "r")

    def __init__(self):
        self.w = None
        self.r = {}


class Prog:
    def __init__(self, nc, stack):
        self.nc = nc
        self.stack = stack
        self.esem = {}
        self.ecnt = {e: 0 for e in ENGS}
        for e in ("pe", "act", "dve", "pool"):
            self.esem[e] = stack.enter_context(nc.semaphore("es_" + e))
        self.bufs = {}
        self.streams = {e: [] for e in ENGS}
        self.waited = {e: {} for e in ENGS}
        self.dsems = []
        self.nins = 0
        self.nwait = 0

    def dsem(self, name):
        h = self.stack.enter_context(self.nc.semaphore("ds_" + name))
        d = DSem(h, name)
        self.dsems.append(d)
        return d

    def _record(self, ins, reads, writes):
        deps = {}
        for b in reads:
            st = self.bufs.get(b)
            if st is not None and st.w is not None:
                deps[id(st.w)] = st.w
        for b in writes:
            st = self.bufs.get(b)
            if st is not None:
                if st.w is not None:
                    deps[id(st.w)] = st.w
                for r in st.r.values():
                    deps[id(r)] = r
        if ins.dsem is not None and ins.dsem.last is not None:
            deps[id(ins.dsem.last)] = ins.dsem.last
        out = []
        for d in deps.values():
            if d is ins:
                continue
            if d.flushed and d.dsem is None:
                continue
            if d.dsem is None and ins.dsem is None and d.eng == ins.eng:
                if d.eng == "pe" or d.eng not in SAME_ENGINE_SYNC:
                    continue
            d.needed = True
            out.append(d)
        ins.deps = out
        rk = ins.eng if ins.dsem is None else ("d", id(ins.dsem))
        for b in reads:
            st = self.bufs.get(b)
            if st is None:
                st = self.bufs[b] = BufState()
            st.r[rk] = ins
        for b in writes:
            st = self.bufs.get(b)
            if st is None:
                st = self.bufs[b] = BufState()
            st.w = ins
            st.r = {}
        if ins.dsem is not None:
            ins.dsem.last = ins
        self.streams[ins.eng].append(ins)
        return ins

    def op(self, eng, fn, reads=(), writes=()):
        return self._record(Ins(eng, fn), reads, writes)

    def dma(self, queue, dsem, out, in_, reads=(), writes=(), **kw):
        assert dsem.queue in (None, queue)
        dsem.queue = queue
        fn = lambda e: e.dma_start(out=out, in_=in_, **kw)
        return self._record(Ins(queue, fn, dsem=dsem), reads, writes)

    def wait_all(self, eng, inss):
        ins = Ins(eng, None)
        out = []
        for d in inss:
            if d is None:
                continue
            d.needed = True
            out.append(d)
        ins.deps = out
        self.streams[eng].append(ins)
        return ins

    def flush(self):
        nc = self.nc
        for e in ENGS:
            for ins in self.streams[e]:
                if ins.dsem is not None:
                    ins.dsem.count += 1
                    ins.val = 16 * ins.dsem.count
                elif ins.needed and ins.fn is not None:
                    self.ecnt[e] += 1
                    ins.val = self.ecnt[e]

        def replay(ename, eng):
            waited = self.waited[ename]
            for ins in self.streams[ename]:
                for d in ins.deps:
                    if d.dsem is not None:
                        key = ("d", id(d.dsem))
                        h = d.dsem.handle
                    else:
                        key = d.eng
                        h = self.esem[d.eng]
                    assert d.val > 0, (ename, d.eng)
                    if waited.get(key, 0) < d.val:
                        eng.wait_ge(h, d.val)
                        waited[key] = d.val
                        self.nwait += 1
                if ins.fn is None:
                    continue
                i = ins.fn(eng)
                self.nins += 1
                if ins.dsem is not None:
                    i.then_inc(ins.dsem.handle, 16)
                elif ins.needed:
                    i.then_inc(self.esem[ename], 1)

        with nc.Block() as block:
            @block.tensor
            def _(t):
                replay("pe", t)

            @block.scalar
            def _(s):
                replay("act", s)

            @block.vector
            def _(v):
                replay("dve", v)

            @block.gpsimd
            def _(g):
                replay("pool", g)

            @block.sync
            def _(s):
                replay("sp", s)

        for e in ENGS:
            for ins in self.streams[e]:
                ins.flushed = True
        self.streams = {e: [] for e in ENGS}


class Builder:
    def __init__(self, SA, SB):
        self.SA, self.SB = SA, SB
        self.HB = SB // 2
        self.nc = bass.Bass("TRN2", target_bir_lowering=False)
        self.rr = 0

    def un(self, name):
        self.uid = getattr(self, "uid", 0) + 1
        return "%s_u%d" % (name, self.uid)

    def din(self, name, shape, dt=F32):
        return self.nc.dram_tensor(name, list(shape), dt, kind="ExternalInput").ap()

    def dout(self, name, shape, dt=F32):
        return self.nc.dram_tensor(name, list(shape), dt, kind="ExternalOutput").ap()

    def dscr(self, name, shape, dt):
        import os
        kind = "ExternalOutput" if os.environ.get("KDEBUG") else "Internal"
        return self.nc.dram_tensor(name, list(shape), dt, kind=kind).ap()

    def build(self):
        nc = self.nc
        SA, SB, HB = self.SA, self.SB, self.HB
        I = self.I = {}
        I["xA"] = self.din("xA", [SA, D])
        I["xB"] = self.din("xB", [SB, D])
        I["memA"] = self.din("memA", [NMEM, D])
        I["memB"] = self.din("memB", [NMEM, D])
        I["gcols"] = self.din("gcols", [128, 96])
        I["memg"] = self.din("memg", [128, 16])
        I["qkvg"] = self.din("qkvg", [128, 4])
        I["subg"] = self.din("subg", [128, 1])
        I["lam"] = self.din("lam", [1, 256])
        I["table"] = self.din("table", [32, 8])
        I["ohA"] = self.din("ohA", [32, GL])
        I["ohB"] = self.din("ohB", [32, GL])
        I["ident"] = self.din("ident", [128, 128])
        I["anti"] = self.din("anti", [128, 128])
        I["cosA"] = self.din("cosA", [64, SA])
        I["sinA"] = self.din("sinA", [64, SA])
        I["cosB"] = self.din("cosB", [64, SB])
        I["sinB"] = self.din("sinB", [64, SB])
        I["mla_w_in"] = self.din("mla_w_in", [D, 576])
        I["mla_w_uq"] = self.din("mla_w_uq", [256, 1536])
        I["mla_w_ukv"] = self.din("mla_w_ukv", [256, 2048])
        I["mla_w_o"] = self.din("mla_w_o", [D, D])
        I["diff_w_in"] = self.din("diff_w_in", [D, 3 * D])
        I["diff_w_o"] = self.din("diff_w_o", [D, D])
        I["xattn_w_q"] = self.din("xattn_w_q", [2, D, D])
        I["xattn_w_kv"] = self.din("xattn_w_kv", [2, D, 2 * D])
        I["xattn_w_o"] = self.din("xattn_w_o", [2, D, D])
        I["mlp_w1"] = self.din("mlp_w1", [2, D, 4 * D])
        I["mlp_w2"] = self.din("mlp_w2", [2, 4 * D, D])
        self.yA = self.dout("yA", [SA, D])
        self.yB = self.dout("yB", [HB, D])
        S = self.S = {}
        for s, L in (("A", SA), ("B", SB)):
            S["XT" + s] = self.dscr("XT" + s, [D, L], F32)
            S["Q" + s] = self.dscr("Q" + s, [D, L], BF16)
            S["K" + s] = self.dscr("K" + s, [D, L], BF16)
            S["QR" + s] = self.dscr("QR" + s, [8, 64, L], BF16)
            S["KR" + s] = self.dscr("KR" + s, [64, L], BF16)
            S["V" + s] = self.dscr("V" + s, [L, D], BF16)
            S["O" + s] = self.dscr("O" + s, [L, D], BF16)
            S["strip" + s] = self.dscr("strip" + s, [128, 8, GW], BF16)
            S["gv" + s] = self.dscr("gv" + s, [8, GL], F32)
            for l in range(2):
                S["mK%d%s" % (l, s)] = self.dscr("mK%d%s" % (l, s), [128, 8, NMEM], BF16)
                S["mV%d%s" % (l, s)] = self.dscr("mV%d%s" % (l, s), [128, 2, D], BF16)
        S["w_mla_o"] = self.dscr("w_mla_o", [D, D], BF16)
        S["w_diff_o"] = self.dscr("w_diff_o", [D, D], BF16)
        S["w_diff_in"] = self.dscr("w_diff_in", [D, 3 * D], BF16)
        S["w_xq"] = self.dscr("w_xq", [2, D, D], BF16)
        S["w_xo"] = self.dscr("w_xo", [2, D, D], BF16)
        S["w_1"] = self.dscr("w_1", [2, D, 4 * D], BF16)
        S["w_2"] = self.dscr("w_2", [2, 4 * D, D], BF16)

        with ExitStack() as top:
            self.P = P = Prog(nc, top)
            sbt = lambda name, shape, dt: top.enter_context(nc.sbuf_tensor(self.un(name), list(shape), dt))
            C = self.C = {}
            C["identf"] = sbt("identf", [128, 128], F32)
            C["identb"] = sbt("identb", [128, 128], BF16)
            C["antif"] = sbt("antif", [128, 128], F32)
            C["ones1"] = sbt("ones1", [128, 128], BF16)
            C["onesD"] = sbt("onesD", [128, 128], BF16)
            C["ones256"] = sbt("ones256", [128, 128], BF16)
            C["onesf"] = sbt("onesf", [128, 128], F32)
            C["gcols"] = sbt("gcols_s", [128, 96], F32)
            C["memg"] = sbt("memg_s", [128, 16], F32)
            C["qkvg"] = sbt("qkvg_s", [128, 4], F32)
            C["subs"] = sbt("subs_s", [128, 1], F32)
            C["nlam"] = sbt("nlam_s", [128, 1], F32)
            C["epsc"] = sbt("epsc", [128, 1], F32)
            C["cst"] = sbt("cst_s", [128, 2, 8, 2], F32)
            self.nds = 0
            self.dspool = {}
            self.dsrr = {}

            import os
            stop = int(os.environ.get("STOP_AFTER", "99"))
            phases = [self.phase_prologue, self.phase_l0_p1, lambda: self.phase_attn(0), lambda: self.phase_tail(0),
                      lambda: self.phase_attn(1), lambda: self.phase_tail(1)]
            for i, ph in enumerate(phases):
                if i > stop:
                    break
                ph()
            print("program: instrs", self.P.nins, "waits", self.P.nwait, "dsems", len(self.P.dsems), flush=True)
        return nc

    def newds(self, name):
        self.nds += 1
        if name in ("c", "wc"):
            pool = self.dspool.setdefault(name, [])
            if len(pool) < 4:
                pool.append(self.P.dsem("%s%d" % (name, self.nds)))
                return pool[-1]
            self.dsrr[name] = self.dsrr.get(name, 0) + 1
            return pool[self.dsrr[name] % 4]
        return self.P.dsem("%s%d" % (name, self.nds))

    def evac_eng(self):
        self.rr += 1
        return "act" if (self.rr & 1) else "dve"

    def copy(self, eng, out, in_, reads, writes):
        if eng == "act":
            return self.P.op("act", lambda e: e.copy(out=out, in_=in_), reads, writes)
        return self.P.op(eng, lambda e: e.tensor_copy(out=out, in_=in_), reads, writes)

    def mm(self, out, pairs, reads, writes):
        n = len(pairs)
        last = None
        for i, (l, r) in enumerate(pairs):
            last = self.P.op("pe", (lambda e, l=l, r=r, a=(i == 0), b=(i == n - 1):
                                    e.matmul(out, lhsT=l, rhs=r, start=a, stop=b)), reads, writes)
        return last

    def rstd_from_ms(self, ms_ps, ms_key, lnv, rstd, key, width=TT):
        P, C = self.P, self.C
        P.op("act", lambda e: e.activation(out=lnv[:, 0:width], in_=ms_ps, func=AF.Ln, bias=C["epsc"][:], scale=1.0),
             [ms_key, "epsc"], [key + "_ln"])
        P.op("act", lambda e: e.activation(out=rstd[:, 0:width], in_=lnv[:, 0:width], func=AF.Exp, scale=-0.5),
             [key + "_ln"], [key])

    def phase_prologue(self):
        nc, P, C, I, S = self.nc, self.P, self.C, self.I, self.S
        with ExitStack() as st:
            sbt = lambda name, shape, dt: st.enter_context(nc.sbuf_tensor(self.un(name), list(shape), dt))
            pst = lambda name, shape, dt: st.enter_context(nc.psum_tensor(self.un(name), list(shape), dt))
            ld = self.newds("c")
            P.dma("sp", ld, C["identf"][:], I["ident"][:, :], writes=["identf"])
            ld2 = self.newds("c")
            P.dma("sp", ld2, C["antif"][:], I["anti"][:, :], writes=["antif"])
            for nm in ("gcols", "memg", "qkvg"):
                P.dma("sp", self.newds("c"), C[nm][:], I[nm][:, :], writes=[nm])
            P.op("act", lambda e: e.copy(out=C["identb"][:], in_=C["identf"][:]), ["identf"], ["identb"])
            P.op("dve", lambda e: e.memset(C["ones1"][:], 1.0), [], ["ones1"])
            P.op("dve", lambda e: e.memset(C["onesD"][:], 1.0 / D), [], ["onesD"])
            P.op("dve", lambda e: e.memset(C["ones256"][:], 1.0 / 256), [], ["ones256"])
            P.op("dve", lambda e: e.memset(C["onesf"][:], 1.0), [], ["onesf"])
            P.op("dve", lambda e: e.memset(C["epsc"][:], EPS), [], ["epsc"])
            lam_init = 0.8 - 0.6 * math.exp(-0.3 * 1)
            subg = sbt("subg", [128, 1], F32)
            P.dma("sp", self.newds("c"), subg[:], I["subg"][:, :], writes=["subg"])
            P.op("dve", lambda e: e.tensor_scalar(out=C["subs"][:], in0=subg[:], scalar1=float(1.0 - lam_init),
                                                   scalar2=None, op0=ALU.mult), ["subg"], ["subs"])
            lam = sbt("lam", [1, 256], F32)
            lj = sbt("lamj", [1, 64], F32)
            ls = sbt("lams", [1, 4], F32)
            P.dma("sp", self.newds("c"), lam[:], I["lam"][:, :], writes=["lam"])
            P.op("dve", lambda e: e.memset(ls[:], 0.0), [], ["ls"])
            P.op("dve", lambda e: e.scalar_tensor_tensor(out=lj[:], in0=lam[:, 0:64], scalar=1.0, in1=lam[:, 64:128],
                                                          op0=ALU.mult, op1=ALU.mult, accum_out=ls[:, 0:1]),
                 ["lam", "ls"], ["lj", "ls"])
            P.op("dve", lambda e: e.scalar_tensor_tensor(out=lj[:], in0=lam[:, 128:192], scalar=1.0, in1=lam[:, 192:256],
                                                          op0=ALU.mult, op1=ALU.mult, accum_out=ls[:, 1:2]),
                 ["lam", "ls", "lj"], ["lj", "ls"])
            P.op("act", lambda e: e.activation(out=ls[:, 2:4], in_=ls[:, 0:2], func=AF.Exp), ["ls"], ["ls"])
            P.op("dve", lambda e: e.tensor_tensor(out=ls[:, 0:1], in0=ls[:, 3:4], in1=ls[:, 2:3], op=ALU.subtract),
                 ["ls"], ["ls"])
            P.op("dve", lambda e: e.tensor_scalar(out=ls[:, 1:2], in0=ls[:, 0:1], scalar1=float(-lam_init),
                                                   scalar2=None, op0=ALU.add), ["ls"], ["ls"])
            pb = pst("pb", [128, 512], F32)
            P.op("pe", lambda e: e.matmul(pb[:, 0:1], lhsT=C["onesf"][0:1, :], rhs=ls[0:1, 1:2], start=True, stop=True),
                 ["onesf", "ls"], ["pb"])
            P.op("dve", lambda e: e.tensor_copy(out=C["nlam"][:], in_=pb[:, 0:1]), ["pb"], ["nlam"])

            wdo = sbt("wdo", [128, 8, D], F32)
            wdb = sbt("wdb", [128, 8, D], BF16)
            P.dma("sp", self.newds("c"), wdo[:], I["diff_w_o"].rearrange("(c p) n -> p c n", p=128), writes=["wdo"])
            for c in range(8):
                P.op("dve", lambda e, c=c: e.tensor_scalar(out=wdb[:, c, :], in0=wdo[:, c, :], scalar1=C["subs"][:],
                                                            scalar2=None, op0=ALU.mult), ["wdo", "subs"], ["wdb"])
            P.dma("sp", self.newds("c"), S["w_diff_o"].rearrange("(c p) n -> p c n", p=128), wdb[:],
                  reads=["wdb"], writes=[("scr", "w_diff_o")])

            tab = sbt("tab", [32, 8], F32)
            P.dma("sp", self.newds("c"), tab[:], I["table"][:, :], writes=["tab"])
            oh = sbt("oh", [32, GL], F32)
            gvs = sbt("gvs", [8, GL], F32)
            hk = sbt("hk", [128, 8, GW], F32)
            stp = sbt("stp", [128, 8, GW], BF16)
            pg = [pst("pg%d" % i, [128, 512], F32) for i in range(3)]
            for si, s in enumerate(("A", "B")):
                P.dma("sp", self.newds("c"), oh[:], I["oh" + s][:, :], writes=["oh"])
                for j, (a, b) in enumerate(((0, 512), (512, 1024), (1024, GL))):
                    P.op("pe", lambda e, j=j, a=a, b=b: e.matmul(pg[j][0:8, 0:b - a], lhsT=tab[:, :], rhs=oh[:, a:b],
                                                                 start=True, stop=True), ["tab", "oh"], [("pg", j)])
                    P.op("dve", lambda e, j=j, a=a, b=b: e.tensor_copy(out=gvs[:, a:b], in_=pg[j][0:8, 0:b - a]),
                         [("pg", j)], ["gvs"])
                P.dma("sp", self.newds("c"), S["gv" + s][:, :], gvs[:], reads=["gvs"], writes=[("scr", "gv" + s)])
                src = bass.AP(S["gv" + s].tensor, 0, [[1, 128], [GL, 8], [1, GW]])
                P.dma("sp", self.newds("c"), hk[:], src, reads=[("scr", "gv" + s)], writes=["hk"])
                for h in range(8):
                    for j, (a, b) in enumerate(((0, 512), (512, 1024), (1024, GW))):
                        P.op("pe", lambda e, h=h, j=j, a=a, b=b: e.matmul(pg[j][:, 0:b - a], lhsT=C["antif"][:],
                                                                          rhs=hk[:, h, a:b], start=True, stop=True),
                             ["antif", "hk"], [("pg", j)])
                        P.op("dve", lambda e, h=h, j=j, a=a, b=b: e.tensor_scalar(out=stp[:, h, a:b], in0=pg[j][:, 0:b - a],
                                                                                  scalar1=8.0, scalar2=None, op0=ALU.mult),
                             [("pg", j)], ["stp"])
                        if j == 0:
                            P.op("dve", lambda e, h=h, si=si: e.tensor_copy(out=C["cst"][:, si, h, 1:2], in_=pg[0][:, 0:1]),
                                 [("pg", 0)], ["cst"])
                        if j == 2:
                            P.op("dve", lambda e, h=h, si=si: e.tensor_copy(out=C["cst"][:, si, h, 0:1],
                                                                            in_=pg[2][:, GW - 1025:GW - 1024]),
                                 [("pg", 2)], ["cst"])
                P.dma("sp", self.newds("c"), S["strip" + s][:, :, :], stp[:], reads=["stp"],
                      writes=[("scr", "strip" + s)])

            self.end_phase()
        with ExitStack() as st:
            sbt = lambda name, shape, dt: st.enter_context(nc.sbuf_tensor(self.un(name), list(shape), dt))
            pst = lambda name, shape, dt: st.enter_context(nc.psum_tensor(self.un(name), list(shape), dt))
            wkv = sbt("wkv", [128, 8, 2 * D], BF16)
            mem = sbt("mem", [128, 2, D], F32)
            memT = sbt("memT", [128, 8, NMEM], F32)
            msq = sbt("msq", [128, 8, NMEM], BF16)
            mn = sbt("mn", [128, 8, NMEM], BF16)
            lnv = sbt("plnv", [128, 512], F32)
            rstd = sbt("prstd", [128, 512], F32)
            mko = sbt("mko", [128, 8, NMEM], BF16)
            mvo = sbt("mvo", [128, 2, D], BF16)
            pm = [pst("pm%d" % i, [128, 512], F32) for i in range(3)]
            pms = pst("pms", [128, 512], F32)
            k = 0
            for l in range(2):
                P.dma("pool", self.newds("wkv"), wkv[:], I["xattn_w_kv"][l].rearrange("(c p) n -> p c n", p=128),
                      writes=["wkv"])
                for s in ("A", "B"):
                    P.dma("sp", self.newds("c"), mem[:], I["mem" + s].rearrange("(j p) n -> p j n", p=128), writes=["mem"])
                    for c in range(8):
                        b = pm[k % 3]; bk = ("pm", k % 3); k += 1
                        for j in range(2):
                            P.op("pe", lambda e, b=b, c=c, j=j: e.transpose(b[:, j * 128:(j + 1) * 128],
                                                                            mem[:, j, c * 128:(c + 1) * 128], C["identf"][:]),
                                 ["mem", "identf"], [bk])
                        P.op("act", lambda e, b=b, c=c: e.copy(out=memT[:, c, :], in_=b[:, 0:NMEM]), [bk], [("memT", c)])
                        P.op("dve", lambda e, c=c: e.tensor_tensor(out=msq[:, c, :], in0=memT[:, c, :], in1=memT[:, c, :],
                                                                   op=ALU.mult), [("memT", c)], [("msq", c)])
                    self.mm(pms[:, 0:NMEM], [(C["onesD"][:], msq[:, c, :]) for c in range(8)],
                            ["onesD"] + [("msq", c) for c in range(8)], ["pms"])
                    self.rstd_from_ms(pms[:, 0:NMEM], "pms", lnv, rstd, "prstd", width=NMEM)
                    for c in range(8):
                        P.op("dve", lambda e, c=c, l=l: e.scalar_tensor_tensor(
                            out=mn[:, c, :], in0=memT[:, c, :], scalar=C["memg"][:, l * 8 + c:l * 8 + c + 1],
                            in1=rstd[:, 0:NMEM], op0=ALU.mult, op1=ALU.mult),
                            [("memT", c), "memg", "prstd"], [("mn", c)])
                    for mc in range(8):
                        b = pm[k % 3]; bk = ("pm", k % 3); k += 1
                        self.mm(b[:, 0:NMEM], [(wkv[:, c, mc * 128:(mc + 1) * 128], mn[:, c, :]) for c in range(8)],
                                ["wkv"] + [("mn", c) for c in range(8)], [bk])
                        self.copy(self.evac_eng(), mko[:, mc, :], b[:, 0:NMEM], [bk], ["mko"])
                    for tcn in range(2):
                        for hf in range(2):
                            b = pm[k % 3]; bk = ("pm", k % 3); k += 1
                            self.mm(b[:, :], [(mn[:, c, tcn * 128:(tcn + 1) * 128],
                                               wkv[:, c, D + hf * 512:D + (hf + 1) * 512]) for c in range(8)],
                                    ["wkv"] + [("mn", c) for c in range(8)], [bk])
                            self.copy(self.evac_eng(), mvo[:, tcn, hf * 512:(hf + 1) * 512], b[:, :], [bk], ["mvo"])
                    P.dma("sp", self.newds("c"), S["mK%d%s" % (l, s)][:, :, :], mko[:], reads=["mko"],
                          writes=[("scr", "mK%d%s" % (l, s))])
                    P.dma("sp", self.newds("c"), S["mV%d%s" % (l, s)][:, :, :], mvo[:], reads=["mvo"],
                          writes=[("scr", "mV%d%s" % (l, s))])
            self.end_phase()

    def emit_casts(self):
        P, I, S = self.P, self.I, self.S
        def cast(dst, src, rows):
            ncol = src.shape[1]
            for r0 in range(0, rows, 256):
                for c0 in range(0, ncol, 1024):
                    P.dma("pool", self.newds("wc"), dst[r0:r0 + 256, c0:c0 + 1024], src[r0:r0 + 256, c0:c0 + 1024],
                          writes=[("scrw", dst.tensor.name, self.nds)])
        cast(S["w_mla_o"], I["mla_w_o"], D)
        cast(S["w_diff_in"], I["diff_w_in"], D)
        for l in range(2):
            cast(S["w_xq"][l], I["xattn_w_q"][l], D)
            cast(S["w_xo"][l], I["xattn_w_o"][l], D)
            cast(S["w_1"][l], I["mlp_w1"][l], D)
            cast(S["w_2"][l], I["mlp_w2"][l], 4 * D)

    def end_phase(self, skip_casts=False):
        P = self.P
        casts = set(id(d) for d in self.dspool.get("wc", [])) if skip_casts else set()
        P.wait_all("sp", [d.last for d in P.dsems if d.queue == "sp" and d.last is not None and not d.last.flushed])
        P.wait_all("pool", [d.last for d in P.dsems if d.queue == "pool" and d.last is not None and not d.last.flushed
                            and id(d) not in casts])
        P.flush()

    def phase_l0_p1(self):
        nc, P, C, I, S = self.nc, self.P, self.C, self.I, self.S
        with ExitStack() as st:
            sbt = lambda name, shape, dt: st.enter_context(nc.sbuf_tensor(self.un(name), list(shape), dt))
            pst = lambda name, shape, dt: st.enter_context(nc.psum_tensor(self.un(name), list(shape), dt))
            w_in = sbt("w_in", [128, 8, 640], BF16)
            w_uq = sbt("w_uq", [128, 2, 1536], BF16)
            w_qrh = sbt("w_qrh", [128, 2, 512], BF16)
            w_ukv = sbt("w_ukv", [128, 2, 2048], BF16)
            P.dma("pool", self.newds("w"), w_in[:, :, 0:576], I["mla_w_in"].rearrange("(c p) n -> p c n", p=128),
                  writes=["w_in"])
            P.dma("pool", self.newds("w"), w_uq[:], I["mla_w_uq"].rearrange("(c p) n -> p c n", p=128), writes=["w_uq"])
            P.dma("pool", self.newds("w"), w_ukv[:], I["mla_w_ukv"].rearrange("(c p) n -> p c n", p=128), writes=["w_ukv"])
            self.emit_casts()
            P.op("act", lambda e: e.mul(out=w_in[:, :, 576:608], in_=w_in[:, :, 544:576], mul=-1.0),
                 ["w_in"], ["w_in"])
            P.op("act", lambda e: e.copy(out=w_in[:, :, 608:640], in_=w_in[:, :, 512:544]), ["w_in"], ["w_in"])
            uq4 = w_uq[:].rearrange("p c (h f) -> p c h f", h=8)
            rh4 = w_qrh[:].rearrange("p c (h f) -> p c h f", h=8)
            for c in range(2):
                P.op("act", lambda e, c=c: e.mul(out=rh4[:, c, :, 0:32], in_=uq4[:, c, :, 160:192], mul=-1.0),
                     ["w_uq"], ["w_qrh"])
                P.op("act", lambda e, c=c: e.copy(out=rh4[:, c, :, 32:64], in_=uq4[:, c, :, 128:160]), ["w_uq"], ["w_qrh"])
            xin = sbt("xin", [128, 4, D], F32)
            xT = sbt("xT", [128, 8, TT], F32)
            sq = [sbt("sq%d" % i, [128, TT], BF16) for i in range(2)]
            hn = sbt("hn", [128, 8, TT], BF16)
            lnv = sbt("lnv", [128, TT], F32)
            rstd = sbt("rstd", [128, TT], F32)
            cst = sbt("cstage", [128, 4, TT], F32)
            cn = sbt("cn", [128, 4, TT], BF16)
            cs = sbt("cosb", [64, TT], F32)
            sn = sbt("sinb", [64, TT], F32)
            tmp = [sbt("rt%d" % i, [64, TT], F32) for i in range(2)]
            qst = sbt("qst", [128, 8, TT], BF16)
            kst = sbt("kst", [128, 8, TT], BF16)
            qrs = sbt("qrs", [64, 8, TT], BF16)
            krs = sbt("krs", [64, TT], BF16)
            vst = sbt("vst", [128, 4, D], BF16)
            pm = [pst("p1m%d" % i, [128, 512], F32) for i in range(6)]
            pms = pst("p1s", [128, 512], F32)
            ds_x = self.newds("x"); ds_cs = self.newds("cs"); ds_sn = self.newds("sn")
            ds_o = [self.newds("o") for _ in range(7)]
            k = 0

            def bank():
                nonlocal k
                b = pm[k % 6]; bk = ("p1m", k % 6); k += 1
                return b, bk

            deferred = []

            def defer(dsem, out, in_, reads=(), writes=()):
                deferred.append((dsem, out, in_, reads, writes))

            def flush_stores():
                for (dsem, out, in_, reads, writes) in deferred:
                    P.dma("sp", dsem, out, in_, reads=reads, writes=writes)
                del deferred[:]

            for s, L, x in (("A", self.SA, I["xA"]), ("B", self.SB, I["xB"])):
                for t in range(L // TT):
                    t0 = t * TT
                    P.dma("sp", ds_x, xin[:], x[t0:t0 + TT, :].rearrange("(j p) n -> p j n", p=128), writes=["xin"])
                    P.dma("sp", ds_cs, cs[:], I["cos" + s][:, t0:t0 + TT], writes=["cs"])
                    P.dma("sp", ds_sn, sn[:], I["sin" + s][:, t0:t0 + TT], writes=["sn"])
                    flush_stores()
                    for c in range(8):
                        b, bk = bank()
                        for j in range(4):
                            P.op("pe", lambda e, b=b, c=c, j=j: e.transpose(b[:, j * 128:(j + 1) * 128],
                                                                            xin[:, j, c * 128:(c + 1) * 128], C["identf"][:]),
                                 ["xin", "identf"], [bk])
                        self.copy(self.evac_eng(), xT[:, c, :], b[:, :], [bk], [("xT", c)])
                        P.op("act", lambda e, c=c: e.activation(out=sq[c % 2][:], in_=xT[:, c, :], func=AF.Square),
                             [("xT", c)], [("sq", c % 2)])
                        P.op("pe", lambda e, c=c: e.matmul(pms[:, :], lhsT=C["onesD"][:], rhs=sq[c % 2][:], start=(c == 0),
                                                           stop=(c == 7)), ["onesD", ("sq", c % 2)], ["pms"])
                    defer(ds_o[0], S["XT" + s][:, t0:t0 + TT].rearrange("(c p) n -> p c n", p=128), xT[:],
                          reads=[("xT", c) for c in range(8)], writes=[("scr", "XT" + s)])
                    self.rstd_from_ms(pms[:, :], "pms", lnv, rstd, "rstd")
                    for c in range(8):
                        P.op("dve", lambda e, c=c: e.scalar_tensor_tensor(
                            out=hn[:, c, :], in0=xT[:, c, :], scalar=C["gcols"][:, c:c + 1], in1=rstd[:],
                            op0=ALU.mult, op1=ALU.mult), [("xT", c), "gcols", "rstd"], [("hn", c)])
                    hnr = [("hn", c) for c in range(8)]
                    for grp in range(2):
                        for m in range(2):
                            mc = grp * 2 + m
                            b, bk = bank()
                            self.mm(b[:, :], [(w_in[:, c, mc * 128:(mc + 1) * 128], hn[:, c, :]) for c in range(8)],
                                    ["w_in"] + hnr, [bk])
                            P.op("act", lambda e, b=b, mc=mc: e.copy(out=cst[:, mc, :], in_=b[:, :]), [bk], [("cst", mc)])
                            P.op("dve", lambda e, mc=mc: e.tensor_tensor(out=sq[mc % 2][:], in0=cst[:, mc, :],
                                                                         in1=cst[:, mc, :], op=ALU.mult),
                                 [("cst", mc)], [("sq", mc % 2)])
                            P.op("pe", lambda e, mc=mc, m=m: e.matmul(pms[:, :], lhsT=C["ones256"][:], rhs=sq[mc % 2][:],
                                                                      start=(m == 0), stop=(m == 1)),
                                 ["ones256", ("sq", mc % 2)], ["pms"])
                        self.rstd_from_ms(pms[:, :], "pms", lnv, rstd, "rstd")
                        for m in range(2):
                            mc = grp * 2 + m
                            P.op("dve", lambda e, mc=mc: e.scalar_tensor_tensor(
                                out=cn[:, mc, :], in0=cst[:, mc, :], scalar=C["qkvg"][:, mc:mc + 1], in1=rstd[:],
                                op0=ALU.mult, op1=ALU.mult), [("cst", mc), "qkvg", "rstd"], [("cn", mc)])

                    def rope(psr, psh, kr_, kh_, out_ap, outkey):
                        P.op("dve", lambda e: e.tensor_tensor(out=tmp[0][:], in0=psr, in1=cs[:], op=ALU.mult),
                             [kr_, "cs"], ["rt0"])
                        P.op("dve", lambda e: e.tensor_tensor(out=tmp[1][:], in0=psh, in1=sn[:], op=ALU.mult),
                             [kh_, "sn"], ["rt1"])
                        P.op("dve", lambda e: e.tensor_tensor(out=out_ap, in0=tmp[0][:], in1=tmp[1][:], op=ALU.add),
                             ["rt0", "rt1"], [outkey])

                    b1, bk1 = bank()
                    self.mm(b1[0:64, :], [(w_in[:, c, 512:576], hn[:, c, :]) for c in range(8)], ["w_in"] + hnr, [bk1])
                    b2, bk2 = bank()
                    self.mm(b2[0:64, :], [(w_in[:, c, 576:640], hn[:, c, :]) for c in range(8)], ["w_in"] + hnr, [bk2])
                    rope(b1[0:64, :], b2[0:64, :], bk1, bk2, krs[:], "krs")
                    defer(ds_o[1], S["KR" + s][:, t0:t0 + TT], krs[:], reads=["krs"], writes=[("scr", "KR" + s)])
                    cq = [("cn", 0), ("cn", 1)]
                    ckv = [("cn", 2), ("cn", 3)]
                    for h in range(8):
                        b, bk = bank()
                        self.mm(b[:, :], [(w_uq[:, c, h * 192:h * 192 + 128], cn[:, c, :]) for c in range(2)],
                                ["w_uq"] + cq, [bk])
                        self.copy(self.evac_eng(), qst[:, h, :], b[:, :], [bk], [("qst", h)])
                        b1, bk1 = bank()
                        self.mm(b1[0:64, :], [(w_uq[:, c, h * 192 + 128:h * 192 + 192], cn[:, c, :]) for c in range(2)],
                                ["w_uq"] + cq, [bk1])
                        b2, bk2 = bank()
                        self.mm(b2[0:64, :], [(w_qrh[:, c, h * 64:(h + 1) * 64], cn[:, c, :]) for c in range(2)],
                                ["w_qrh"] + cq, [bk2])
                        rope(b1[0:64, :], b2[0:64, :], bk1, bk2, qrs[:, h, :], ("qrs", h))
                        b, bk = bank()
                        self.mm(b[:, :], [(w_ukv[:, c, h * 256:h * 256 + 128], cn[:, 2 + c, :]) for c in range(2)],
                                ["w_ukv"] + ckv, [bk])
                        self.copy(self.evac_eng(), kst[:, h, :], b[:, :], [bk], [("kst", h)])
                    defer(ds_o[2], S["Q" + s][:, t0:t0 + TT].rearrange("(h p) n -> p h n", p=128), qst[:],
                          reads=[("qst", h) for h in range(8)], writes=[("scr", "Q" + s)])
                    defer(ds_o[3], S["K" + s][:, t0:t0 + TT].rearrange("(h p) n -> p h n", p=128), kst[:],
                          reads=[("kst", h) for h in range(8)], writes=[("scr", "K" + s)])
                    defer(ds_o[4], S["QR" + s][:, :, t0:t0 + TT].rearrange("h p n -> p h n"), qrs[:],
                          reads=[("qrs", h) for h in range(8)], writes=[("scr", "QR" + s)])
                    wv4 = w_ukv[:].rearrange("p c (h f) -> p c h f", h=8)
                    for j in range(4):
                        for hf in range(2):
                            b, bk = bank()
                            self.mm(b[:, :].rearrange("p (h f) -> p h f", h=4),
                                    [(cn[:, 2 + c, j * 128:(j + 1) * 128], wv4[:, c, hf * 4:(hf + 1) * 4, 128:256])
                                     for c in range(2)], ["w_ukv"] + ckv, [bk])
                            self.copy(self.evac_eng(), vst[:, j, hf * 512:(hf + 1) * 512], b[:, :], [bk], [("vst", j)])
                    defer(ds_o[5], S["V" + s][t0:t0 + TT, :].rearrange("(j p) n -> p j n", p=128), vst[:],
                          reads=[("vst", j) for j in range(4)], writes=[("scr", "V" + s)])
            flush_stores()
            self.end_phase(skip_casts=True)

    def phase_attn(self, layer):
        nc, P, C, I, S = self.nc, self.P, self.C, self.I, self.S
        mla = (layer == 0)
        nbr = 1 if mla else 2
        with ExitStack() as st:
            sbt = lambda name, shape, dt: st.enter_context(nc.sbuf_tensor(self.un(name), list(shape), dt))
            pst = lambda name, shape, dt: st.enter_context(nc.psum_tensor(self.un(name), list(shape), dt))
            Lmax = self.SA
            nkmax = Lmax // 128
            Kt = [sbt("Kt%d" % i, [128, Lmax], BF16) for i in range(2)]
            Qt = [sbt("Qt%d" % i, [128, Lmax], BF16) for i in range(2)]
            Vt = [sbt("Vt%d" % i, [128, nkmax, 129], BF16) for i in range(2)]
            if mla:
                QRt = [sbt("QRt%d" % i, [128, Lmax], BF16) for i in range(2)]
                KRt = sbt("KRt", [128, Lmax], BF16)
                nsc, nacc = 0, 2
                scp = [pst("scp%d" % i, [128, 1024], F32) for i in range(2)]
                etp = [sbt("etp%d" % i, [128, 1024], BF16) for i in range(3)]
            else:
                stp = sbt("stp", [128, 8, GW], BF16)
                nsc, nacc = 0, 1
                scd = [pst("scd%d" % i, [128, 1024], F32) for i in range(2)]
                etd = [sbt("etd%d" % i, [128, 1024], BF16) for i in range(3)]
            for i in range(2):
                P.op("pool", lambda e, i=i: e.memset(Vt[i][:, :, 128:129], 1.0), [], [("Vt1", i)])
            ne = 4
            ost = [sbt("ost%d" % i, [128, 4, 128], BF16) for i in range(2)]
            rinv = [sbt("rinv%d" % i, [128, 8], F32) for i in range(2)]
            if not mla:
                tq = [sbt("tq%d" % i, [128, 128], F32) for i in range(2)]
                accs = [[sbt("accs%d_%d" % (br, hb), [128, 512], F32) for hb in range(2)] for br in range(2)]
                od = [sbt("od%d" % i, [128, 4, 128], F32) for i in range(2)]
                junk = sbt("junk", [128, 128], F32)
                ss = [sbt("ss%d" % i, [128, 4], F32) for i in range(2)]
                ssl = [sbt("ssl%d" % i, [128, 4], F32) for i in range(2)]
                eps128 = sbt("eps128", [128, 1], F32)
                P.op("dve", lambda e: e.memset(eps128[:], EPS), [], ["eps128"])
            sc = [[pst("sc%d_%d" % (br, i), [128, 512], F32) for i in range(nsc)] for br in range(nbr)]
            acc = [[[pst("acc%d_%d_%d" % (br, a, i), [128, 512], F32) for i in range(2)] for a in range(nacc)]
                   for br in range(nbr)]
            dsl = [[self.newds("al") for _ in range(4)] for _ in range(2)]
            dso = [self.newds("ao") for _ in range(2)]
            dkr = self.newds("akr")
            dkr2 = self.newds("akr2")
            dsl2 = [self.newds("al2") for _ in range(2)]
            scale = (192 ** -0.5) if mla else (64 ** -0.5)
            hidx = 0
            qbi = 0
            for si, (s, L) in enumerate((("A", self.SA), ("B", self.SB))):
                nk = L // 128
                Lq = L if (mla or s == "A") else self.HB
                nq = Lq // 512
                if mla:
                    P.dma("sp", dkr, KRt[0:64, 0:L], S["KR" + s][:, :], reads=[("scr", "KR" + s)], writes=["KRt"])
                    P.dma("sp", dkr2, KRt[64:128, 0:L], S["KR" + s][:, :], reads=[("scr", "KR" + s)], writes=["KRt2"])
                else:
                    P.dma("sp", dkr, stp[:], S["strip" + s][:, :, :], reads=[("scr", "strip" + s)], writes=["stp"])

                def load_head(h, hp):
                    P.dma("sp", dsl[hp][0], Kt[hp][:, 0:L], S["K" + s][h * 128:(h + 1) * 128, :],
                          reads=[("scr", "K" + s)], writes=[("Kt", hp)])
                    P.dma("sp", dsl[hp][1], Qt[hp][:, 0:Lq], S["Q" + s][h * 128:(h + 1) * 128, 0:Lq],
                          reads=[("scr", "Q" + s)], writes=[("Qt", hp)])
                    P.dma("sp", dsl[hp][2], Vt[hp][:, 0:nk, 0:128],
                          S["V" + s][:, h * 128:(h + 1) * 128].rearrange("(k p) d -> p k d", p=128),
                          reads=[("scr", "V" + s)], writes=[("Vt", hp)])
                    if mla:
                        P.dma("sp", dsl[hp][3], QRt[hp][0:64, 0:L], S["QR" + s][h, :, :],
                              reads=[("scr", "QR" + s)], writes=[("QRt", hp)])
                        P.dma("sp", dsl2[hp], QRt[hp][64:128, 0:L], S["QR" + s][h, :, :],
                              reads=[("scr", "QR" + s)], writes=[("QRt2", hp)])

                load_head(0, hidx % 2)
                for h in range(8):
                    hp = hidx % 2
                    if h + 1 < 8:
                        load_head(h + 1, (hidx + 1) % 2)
                    rd_k = [("Kt", hp)] + (["KRt"] if mla else [])
                    rd_q = [("Qt", hp)] + ([("QRt", hp)] if mla else [])
                    tiles = []
                    for qb in range(nq):
                        near_l = [kc for kc in range(nk) if -1 <= kc - 4 * qb <= 4]
                        far_l = [kc for kc in range(nk) if not (-1 <= kc - 4 * qb <= 4)]
                        order = []
                        fi = 0
                        for n_ in near_l:
                            order += far_l[fi:fi + 2]
                            fi += 2
                            order.append(n_)
                        order += far_l[fi:]
                        for idx, kc in enumerate(order):
                            tiles.append((qb, kc, idx == 0, idx == len(order) - 1))
                    pend = None
                    ti = 0

                    def emit_pv(qb, kc, slot, aset, first, last):
                        for br in range(nbr):
                            for j in range(4):
                                a = acc[br][aset][j // 2]
                                P.op("pe", lambda e, a=a, j=j, br=br, slot=slot, kc=kc, hp=hp,
                                     st_=(first and j % 2 == 0), sp_=(last and j % 2 == 1): e.matmul(
                                    a[:, (j % 2) * 256:(j % 2) * 256 + 129],
                                    lhsT=etd[slot][:, br * 512 + j * 128:br * 512 + (j + 1) * 128], rhs=Vt[hp][:, kc, :],
                                    start=st_, stop=sp_, skip_group_check=True),
                                    [("ed", slot), ("Vt", hp), ("Vt1", hp)], [("acc", br, aset, j // 2)])

                    def emit_epilogue(qb, aset):
                        nonlocal qbi
                        op_ = qbi % 2
                        qbi += 1
                        o = ost[op_]
                        if mla:
                            for j in range(4):
                                a = acc[0][aset][j // 2]
                                c0 = (j % 2) * 256
                                P.op("dve", lambda e, a=a, c0=c0, j=j: e.reciprocal(out=rinv[op_][:, j:j + 1],
                                                                                   in_=a[:, c0 + 128:c0 + 129]),
                                     [("acc", 0, aset, j // 2)], [("rinv", op_, j)])
                                P.op("dve", lambda e, a=a, c0=c0, j=j: e.tensor_scalar(
                                    out=o[:, j, :], in0=a[:, c0:c0 + 128], scalar1=rinv[op_][:, j:j + 1], scalar2=None,
                                    op0=ALU.mult), [("acc", 0, aset, j // 2), ("rinv", op_, j)], [("ost", op_)])
                        else:
                            for br in range(2):
                                for hb in range(2):
                                    P.op("dve", lambda e, br=br, hb=hb: e.tensor_copy(out=accs[br][hb][:, 0:385],
                                                                                      in_=acc[br][aset][hb][:, 0:385]),
                                         [("acc", br, aset, hb)], [("accs", br, hb)])
                            P.op("dve", lambda e: e.memset(ss[op_][:], 0.0), [], [("ss", op_, j) for j in range(4)])
                            for j in range(4):
                                a0 = accs[0][j // 2]
                                a1 = accs[1][j // 2]
                                c0 = (j % 2) * 256
                                P.op("dve", lambda e, a0=a0, c0=c0, j=j: e.reciprocal(out=rinv[op_][:, j:j + 1],
                                                                                     in_=a0[:, c0 + 128:c0 + 129]),
                                     [("accs", 0, j // 2)], [("rinv", op_, j)])
                                P.op("dve", lambda e, a1=a1, c0=c0, j=j: e.reciprocal(out=rinv[op_][:, 4 + j:5 + j],
                                                                                     in_=a1[:, c0 + 128:c0 + 129]),
                                     [("accs", 1, j // 2)], [("rinv", op_, 4 + j)])
                                P.op("dve", lambda e, j=j: e.tensor_tensor(out=rinv[op_][:, 4 + j:5 + j],
                                                                           in0=rinv[op_][:, 4 + j:5 + j], in1=C["nlam"][:],
                                                                           op=ALU.mult),
                                     [("rinv", op_, 4 + j), "nlam"], [("rinv", op_, 4 + j)])
                                P.op("dve", lambda e, a0=a0, c0=c0, j=j: e.tensor_scalar(
                                    out=tq[j % 2][:], in0=a0[:, c0:c0 + 128], scalar1=rinv[op_][:, j:j + 1], scalar2=None,
                                    op0=ALU.mult), [("accs", 0, j // 2), ("rinv", op_, j)], [("tq", j % 2)])
                                P.op("dve", lambda e, a1=a1, c0=c0, j=j: e.scalar_tensor_tensor(
                                    out=od[op_][:, j, :], in0=a1[:, c0:c0 + 128], scalar=rinv[op_][:, 4 + j:5 + j],
                                    in1=tq[j % 2][:], op0=ALU.mult, op1=ALU.add),
                                    [("accs", 1, j // 2), ("rinv", op_, 4 + j), ("tq", j % 2)], [("od", op_, j)])
                                P.op("dve", lambda e, j=j: e.scalar_tensor_tensor(
                                    out=junk[:], in0=od[op_][:, j, :], scalar=1.0 / 128, in1=od[op_][:, j, :],
                                    op0=ALU.mult, op1=ALU.mult, accum_out=ss[op_][:, j:j + 1]),
                                    [("od", op_, j), ("ss", op_, j)], ["junk", ("ss", op_, j)])
                            def part2(op_=op_, o=o, qb=qb):
                                ssk = [("ss", op_, j) for j in range(4)]
                                P.op("act", lambda e: e.activation(out=ssl[op_][:], in_=ss[op_][:], func=AF.Ln,
                                                                   bias=eps128[:], scale=1.0), ssk + ["eps128"],
                                     [("ssl", op_)])
                                P.op("act", lambda e: e.activation(out=ssl[op_][:], in_=ssl[op_][:], func=AF.Exp,
                                                                   scale=-0.5), [("ssl", op_)], [("ssl", op_)])
                                for j in range(4):
                                    P.op("dve", lambda e, j=j: e.tensor_scalar(out=o[:, j, :], in0=od[op_][:, j, :],
                                                                               scalar1=ssl[op_][:, j:j + 1], scalar2=None,
                                                                               op0=ALU.mult),
                                         [("od", op_, j), ("ssl", op_)], [("ost", op_)])
                                q0 = qb * 512
                                P.dma("pool", dso[op_],
                                      S["O" + s][q0:q0 + 512, h * 128:(h + 1) * 128].rearrange("(j p) d -> p j d", p=128),
                                      o[:], reads=[("ost", op_)], writes=[("scr", "O" + s)])
                            pending2.append([12, part2])
                            return
                        q0 = qb * 512
                        P.dma("pool", dso[op_],
                              S["O" + s][q0:q0 + 512, h * 128:(h + 1) * 128].rearrange("(j p) d -> p j d", p=128), o[:],
                              reads=[("ost", op_)], writes=[("scr", "O" + s)])

                    pending2 = []

                    def tick_pending(force=False):
                        for p in list(pending2):
                            p[0] -= 1
                            if force or p[0] <= 0:
                                p[1]()
                                pending2.remove(p)

                    if mla:
                        pairs = [(qb, kp) for qb in range(nq) for kp in range(nk // 2)]
                        ppend = None

                        def emit_pv_pair(qb, kp, slot, aset):
                            for half in range(2):
                                kc = 2 * kp + half
                                for j in range(4):
                                    a = acc[0][aset][j // 2]
                                    P.op("pe", lambda e, a=a, j=j, slot=slot, kc=kc, hp=hp, half=half,
                                         st_=(kc == 0 and j % 2 == 0), sp_=(kc == nk - 1 and j % 2 == 1): e.matmul(
                                        a[:, (j % 2) * 256:(j % 2) * 256 + 129],
                                        lhsT=etp[slot][:, half * 512 + j * 128:half * 512 + (j + 1) * 128],
                                        rhs=Vt[hp][:, kc, :], start=st_, stop=sp_, skip_group_check=True),
                                        [("ep", slot), ("Vt", hp), ("Vt1", hp)], [("acc", 0, aset, j // 2)])

                        for pi_, (qb, kp) in enumerate(pairs):
                            slot_s = pi_ % 2
                            slot_e = pi_ % 3
                            aset = qb % nacc
                            q0 = qb * 512
                            scb = scp[slot_s]
                            for half in range(2):
                                kc = 2 * kp + half
                                P.op("pe", lambda e, scb=scb, kc=kc, q0=q0, hp=hp, half=half: e.matmul(
                                    scb[:, half * 512:(half + 1) * 512], lhsT=Kt[hp][:, kc * 128:(kc + 1) * 128],
                                    rhs=Qt[hp][:, q0:q0 + 512], start=True, stop=False),
                                    [("Kt", hp), ("Qt", hp)], [("scp", slot_s)])
                            for half in range(2):
                                kc = 2 * kp + half
                                r0 = half * 64
                                P.op("pe", lambda e, scb=scb, kc=kc, q0=q0, hp=hp, half=half, r0=r0: e.matmul(
                                    scb[:, half * 512:(half + 1) * 512], lhsT=KRt[r0:r0 + 64, kc * 128:(kc + 1) * 128],
                                    rhs=QRt[hp][r0:r0 + 64, q0:q0 + 512], start=False, stop=True),
                                    ["KRt", "KRt2", ("QRt", hp), ("QRt2", hp)], [("scp", slot_s)])
                            eb = etp[slot_e]
                            P.op("act", lambda e, eb=eb, scb=scb: e.activation(out=eb[:], in_=scb[:, :], func=AF.Exp,
                                                                               scale=float(scale)),
                                 [("scp", slot_s)], [("ep", slot_e)])
                            if ppend is not None:
                                emit_pv_pair(*ppend)
                                if ppend[1] == nk // 2 - 1:
                                    emit_epilogue(ppend[0], ppend[3])
                            ppend = (qb, kp, slot_e, aset)
                        emit_pv_pair(*ppend)
                        emit_epilogue(ppend[0], ppend[3])
                        hidx += 1
                        continue
                    for (qb, kc, first_, last_) in tiles:
                        slot_s = ti % 2
                        slot_e = ti % 3
                        aset = 0
                        q0 = qb * 512
                        m = kc - 4 * qb
                        scb = scd[slot_s]
                        near = (-1 <= m <= 4)
                        for br in range(2):
                            r0 = br * 64
                            P.op("pe", lambda e, scb=scb, kc=kc, q0=q0, r0=r0, near=near, hp=hp, br=br: e.matmul(
                                scb[:, br * 512:(br + 1) * 512], lhsT=Kt[hp][r0:r0 + 64, kc * 128:(kc + 1) * 128],
                                rhs=Qt[hp][r0:r0 + 64, q0:q0 + 512], start=True, stop=(not near)),
                                rd_k + rd_q, [("scd", slot_s)])
                        if near:
                            off = 512 - 128 * m
                            for br in range(2):
                                P.op("pe", lambda e, scb=scb, off=off, h=h, br=br: e.matmul(
                                    scb[:, br * 512:(br + 1) * 512], lhsT=C["identb"][:], rhs=stp[:, h, off:off + 512],
                                    start=False, stop=True), ["identb", "stp"], [("scd", slot_s)])
                            bias_ap = None
                        else:
                            bias_ap = C["cst"][:, si, h, 0:1] if m < -1 else C["cst"][:, si, h, 1:2]
                        eb = etd[slot_e]
                        if bias_ap is None:
                            P.op("act", lambda e, eb=eb, scb=scb: e.activation(out=eb[:], in_=scb[:, :], func=AF.Exp,
                                                                               scale=float(scale)),
                                 [("scd", slot_s)], [("ed", slot_e)])
                        else:
                            P.op("act", lambda e, eb=eb, scb=scb, bias_ap=bias_ap: e.activation(
                                out=eb[:], in_=scb[:, :], func=AF.Exp, bias=bias_ap, scale=float(scale)),
                                [("scd", slot_s), "cst"], [("ed", slot_e)])
                        if pend is not None:
                            emit_pv(*pend)
                            if pend[5]:
                                emit_epilogue(pend[0], pend[3])
                        pend = (qb, kc, slot_e, aset, first_, last_)
                        ti += 1
                        tick_pending()
                    emit_pv(*pend)
                    emit_epilogue(pend[0], pend[3])
                    tick_pending(force=True)
                    hidx += 1
            self.end_phase()

    def phase_tail(self, layer):
        nc, P, C, I, S = self.nc, self.P, self.C, self.I, self.S
        last = (layer == 1)
        with ExitStack() as st:
            sbt = lambda name, shape, dt: st.enter_context(nc.sbuf_tensor(self.un(name), list(shape), dt))
            pst = lambda name, shape, dt: st.enter_context(nc.psum_tensor(self.un(name), list(shape), dt))
            NSLOT = 3
            ring = [sbt("ring%d" % i, [128, 8, D], BF16) for i in range(NSLOT)]
            dsr = [self.newds("r") for _ in range(NSLOT)]
            xT = [sbt("xT%d" % i, [128, 8, TT], F32) for i in range(2)]
            oin = sbt("oin", [128, 4, D], BF16)
            hn = sbt("hn", [128, 8, TT], BF16)
            ys = sbt("ys", [128, 8, TT], F32)
            hid = sbt("hid", [128, 32, TT], BF16)
            sq8 = sbt("sq8", [128, 8, TT], BF16)
            rl = [sbt("rl%d" % i, [128, TT], F32) for i in range(4)]
            lnv = sbt("lnv", [128, TT], F32)
            lnx = sbt("lnx", [128, TT], F32)
            rstd = sbt("rstd", [128, TT], F32)
            rinv = sbt("rinv", [128, TT], F32)
            ex = [sbt("ex%d" % i, [128, 2, TT], BF16) for i in range(2)]
            mK = sbt("mK", [128, 8, NMEM], BF16)
            mV = sbt("mV", [128, 2, D], BF16)
            yout = ys[:].rearrange("p (j a) n -> p j (a n)", j=4)
            pm = [pst("ptm%d" % i, [128, 512], F32) for i in range(5)]
            pms = pst("pts", [128, 512], F32)
            ptrs = [pst("ptr%d" % i, [128, 1024], BF16) for i in range(2)]
            pmx = pms
            ds_x = [self.newds("tx") for _ in range(2)]
            ds_oi = self.newds("toi")
            ds_m = [self.newds("tm") for _ in range(2)]
            ds_st = [self.newds("ts") for _ in range(5)]
            k = 0

            def bank():
                nonlocal k
                b = pm[k % 5]; bk = ("ptm", k % 5); k += 1
                return b, bk

            g0 = layer * 48
            slices = []
            w_o = S["w_mla_o"] if layer == 0 else S["w_diff_o"]
            slices.append(("wo", w_o))
            slices.append(("xq", S["w_xq"][layer]))
            slices.append(("xo", S["w_xo"][layer]))
            for i in range(4):
                slices.append(("w1_%d" % i, S["w_1"][layer][:, i * D:(i + 1) * D]))
            for i in range(4):
                slices.append(("w2_%d" % i, S["w_2"][layer][i * D:(i + 1) * D, :]))
            if not last:
                for i in range(3):
                    slices.append(("win_%d" % i, S["w_diff_in"][:, i * D:(i + 1) * D]))
            wname = {"wo": ("scr", w_o.tensor.name), "xq": ("scr", "w_xq"), "xo": ("scr", "w_xo")}
            self._ring_seq = []
            state = {"issued": 0, "used": 0}
            plan = []

            for s, L in (("A", self.SA), ("B", self.SB)):
                Lq = L if (not last or s == "A") else self.HB
                for t in range(Lq // TT):
                    own = (s == "A") or (t * TT < self.HB)
                    sl = list(slices)
                    if (not last) and (not own):
                        sl = [x for x in sl if x[0] != "win_0"]
                    plan.append((s, t, own, sl))
            flat = [(pi, sname, ap) for pi, (_, _, _, sl) in enumerate(plan) for (sname, ap) in sl]

            def issue_next():
                i = state["issued"]
                if i >= len(flat):
                    return
                _, sname, ap = flat[i]
                slot = i % NSLOT
                P.dma("sp", dsr[slot], ring[slot][:], ap.rearrange("(c p) n -> p c n", p=128),
                      reads=[("scr", ap.tensor.name)], writes=[("ring", slot)])
                state["issued"] += 1

            def next_slice(expect):
                i = state["used"]
                assert flat[i][1] == expect, (flat[i][1], expect)
                while state["issued"] < min(len(flat), i + NSLOT):
                    issue_next()
                state["used"] += 1
                slot = i % NSLOT
                return ring[slot], ("ring", slot)

            def load_tile_inputs(pi):
                s, t, own, _ = plan[pi]
                par = pi % 2
                t0 = t * TT
                P.dma("sp", ds_x[par], xT[par][:], S["XT" + s][:, t0:t0 + TT].rearrange("(c p) n -> p c n", p=128),
                      reads=[("scr", "XT" + s, t)], writes=[("xT", par, c) for c in range(8)])

            def sq_op(c, src_ap, srckey):
                if c % 2 == 0:
                    P.op("act", lambda e, c=c: e.activation(out=sq8[:, c, :], in_=src_ap, func=AF.Square),
                         [srckey], [("sq8", c)])
                else:
                    P.op("dve", lambda e, c=c: e.tensor_tensor(out=sq8[:, c, :], in0=src_ap, in1=src_ap, op=ALU.mult),
                         [srckey], [("sq8", c)])

            def ones_mm(c):
                P.op("pe", lambda e, c=c: e.matmul(pms[:, :], lhsT=C["onesD"][:], rhs=sq8[:, c, :], start=(c == 0),
                                                   stop=(c == 7)), ["onesD", ("sq8", c)], ["pms"])

            def stats_mm():
                for c in range(8):
                    ones_mm(c)
                self.rstd_from_ms(pms[:, :], "pms", lnv, rstd, "rstd")

            def pre_norm(par, gbase):
                for c in range(8):
                    sq_op(c, xT[par][:, c, :], ("xT", par, c))
                stats_mm()
                for c in range(8):
                    P.op("dve", lambda e, c=c: e.scalar_tensor_tensor(
                        out=hn[:, c, :], in0=xT[par][:, c, :], scalar=C["gcols"][:, gbase + c:gbase + c + 1], in1=rstd[:],
                        op0=ALU.mult, op1=ALU.mult), [("xT", par, c), "gcols", "rstd"], [("hn", c)])

            def pre_scale(par, gbase):
                for c in range(8):
                    P.op("dve", lambda e, c=c: e.tensor_scalar(
                        out=hn[:, c, :], in0=xT[par][:, c, :], scalar1=C["gcols"][:, gbase + c:gbase + c + 1],
                        scalar2=None, op0=ALU.mult), [("xT", par, c), "gcols"], [("hn", c)])
                    P.op("act", lambda e, c=c: e.activation(out=sq8[:, c, :], in_=xT[par][:, c, :], func=AF.Square),
                         [("xT", par, c)], [("sq8", c)])

            def out_proj_post(par, w, wkey, src, srckeys, gbase):
                for mc in range(8):
                    b, bk = bank()
                    self.mm(b[:, :], [(w[:, c, mc * 128:(mc + 1) * 128], src[:, c, :]) for c in range(8)],
                            [wkey] + srckeys, [bk])
                    if mc >= 1:
                        ones_mm(mc - 1)
                    P.op("act", lambda e, b=b, mc=mc: e.copy(out=ys[:, mc, :], in_=b[:, :]), [bk], [("ys", mc)])
                    P.op("dve", lambda e, mc=mc: e.tensor_tensor(out=sq8[:, mc, :], in0=ys[:, mc, :], in1=ys[:, mc, :],
                                                                 op=ALU.mult), [("ys", mc)], [("sq8", mc)])
                ones_mm(7)
                post(par, gbase)

            def post(par, gbase):
                self.rstd_from_ms(pms[:, :], "pms", lnv, rstd, "rstd")
                for c in range(8):
                    P.op("dve", lambda e, c=c: e.scalar_tensor_tensor(
                        out=ys[:, c, :], in0=ys[:, c, :], scalar=C["gcols"][:, gbase + c:gbase + c + 1], in1=rstd[:],
                        op0=ALU.mult, op1=ALU.mult), [("ys", c), "gcols", "rstd"], [("ys", c)])
                    P.op("pool" if c % 2 else "dve", lambda e, c=c: e.tensor_tensor(
                        out=xT[par][:, c, :], in0=xT[par][:, c, :], in1=ys[:, c, :], op=ALU.add),
                         [("xT", par, c), ("ys", c)], [("xT", par, c)])

            if layer == 0:
                P.wait_all("sp", [d.last for d in self.dspool.get("wc", []) if d.last is not None])
            cur_seq = None
            load_tile_inputs(0)
            for pi, (s, t, own, sl) in enumerate(plan):
                par = pi % 2
                t0 = t * TT
                if s != cur_seq:
                    cur_seq = s
                    P.dma("sp", ds_m[0], mK[:], S["mK%d%s" % (layer, s)][:, :, :],
                          reads=[("scr", "mK%d%s" % (layer, s))], writes=["mK"])
                    P.dma("sp", ds_m[1], mV[:], S["mV%d%s" % (layer, s)][:, :, :],
                          reads=[("scr", "mV%d%s" % (layer, s))], writes=["mV"])
                P.dma("sp", ds_oi, oin[:], S["O" + s][t0:t0 + TT, :].rearrange("(j p) n -> p j n", p=128),
                      reads=[("scr", "O" + s)], writes=["oin"])
                if pi + 1 < len(plan):
                    load_tile_inputs(pi + 1)
                for c in range(8):
                    for j in range(4):
                        P.op("pe", lambda e, c=c, j=j: e.transpose(ptrs[c % 2][:, j * 128:(j + 1) * 128],
                                                                   oin[:, j, c * 128:(c + 1) * 128], C["identb"][:]),
                             ["oin", "identb"], [("ptr", c % 2)])
                    self.copy(self.evac_eng(), hn[:, c, :], ptrs[c % 2][:, 0:512], [("ptr", c % 2)],
                              [("hn", c)])
                hnk = [("hn", c) for c in range(8)]
                w, wk = next_slice("wo")
                out_proj_post(par, w, wk, hn, hnk, g0 + 8)
                pre_scale(par, g0 + 16)
                w, wk = next_slice("xq")
                for mc in range(8):
                    b, bk = bank()
                    self.mm(b[:, :], [(w[:, c, mc * 128:(mc + 1) * 128], hn[:, c, :]) for c in range(8)], [wk] + hnk, [bk])
                    if mc == 0:
                        stats_mm()
                    P.op("dve", lambda e, b=b, mc=mc: e.tensor_tensor(out=hid[:, mc, :], in0=b[:, :], in1=rstd[:],
                                                                      op=ALU.mult), [bk, "rstd"], [("hid", mc)])
                xs = 256 ** -0.5

                def xscores(h):
                    xe = ex[h % 2]
                    for mcn in range(2):
                        b, bk = bank()
                        self.mm(b[:, :], [(mK[:, 2 * h + fc, mcn * 128:(mcn + 1) * 128], hid[:, 2 * h + fc, :])
                                          for fc in range(2)], ["mK", ("hid", 2 * h), ("hid", 2 * h + 1)], [bk])
                        P.op("act", lambda e, b=b, xe=xe, mcn=mcn: e.activation(out=xe[:, mcn, :], in_=b[:, :], func=AF.Exp,
                                                                                scale=float(xs)),
                             [bk], [("ex", h % 2, mcn)])

                def xfinish(h):
                    xe = ex[h % 2]
                    exk = [("ex", h % 2, 0), ("ex", h % 2, 1)]
                    self.mm(pmx[:, :], [(C["ones1"][:], xe[:, mcn, :]) for mcn in range(2)], ["ones1"] + exk, ["pms"])
                    P.op("act", lambda e: e.activation(out=lnx[:], in_=pmx[:, :], func=AF.Ln), ["pms"], ["lnx"])
                    P.op("act", lambda e: e.activation(out=rinv[:], in_=lnx[:], func=AF.Exp, scale=-1.0),
                         ["lnx"], ["rinv"])
                    for dc in range(2):
                        b, bk = bank()
                        self.mm(b[:, :], [(mV[:, mcn, h * 256 + dc * 128:h * 256 + (dc + 1) * 128], xe[:, mcn, :])
                                          for mcn in range(2)], ["mV"] + exk, [bk])
                        P.op("dve", lambda e, b=b, h=h, dc=dc: e.tensor_tensor(out=hn[:, 2 * h + dc, :], in0=b[:, :],
                                                                               in1=rinv[:], op=ALU.mult),
                             [bk, "rinv"], [("hn", 2 * h + dc)])

                for h in range(5):
                    if h < 4:
                        xscores(h)
                    if h >= 1:
                        xfinish(h - 1)
                w, wk = next_slice("xo")
                out_proj_post(par, w, wk, hn, hnk, g0 + 24)
                pre_scale(par, g0 + 32)
                for i in range(4):
                    w, wk = next_slice("w1_%d" % i)
                    for mc in range(8):
                        b, bk = bank()
                        self.mm(b[:, :], [(w[:, c, mc * 128:(mc + 1) * 128], hn[:, c, :]) for c in range(8)], [wk] + hnk,
                                [bk])
                        if i == 0 and mc == 0:
                            stats_mm()
                        ri = (i * 8 + mc) % 4
                        r = rl[ri]
                        P.op("dve", lambda e, b=b, r=r: e.scalar_tensor_tensor(out=r[:], in0=b[:, :], scalar=0.0, in1=rstd[:],
                                                                               op0=ALU.max, op1=ALU.mult),
                             [bk, "rstd"], [("rl", ri)])
                        P.op("act", lambda e, r=r, i=i, mc=mc: e.activation(out=hid[:, i * 8 + mc, :], in_=r[:],
                                                                            func=AF.Square),
                             [("rl", ri)], [("hid", i * 8 + mc)])
                for i in range(4):
                    w, wk = next_slice("w2_%d" % i)
                    for mc in range(8):
                        b, bk = bank()
                        self.mm(b[:, :], [(w[:, c, mc * 128:(mc + 1) * 128], hid[:, i * 8 + c, :]) for c in range(8)],
                                [wk] + [("hid", i * 8 + c) for c in range(8)], [bk])
                        if i == 0:
                            P.op("act", lambda e, b=b, mc=mc: e.copy(out=ys[:, mc, :], in_=b[:, :]), [bk], [("ys", mc)])
                        else:
                            P.op("dve", lambda e, b=b, mc=mc: e.tensor_tensor(out=ys[:, mc, :], in0=ys[:, mc, :], in1=b[:, :],
                                                                              op=ALU.add), [bk, ("ys", mc)], [("ys", mc)])
                        if i == 3:
                            sq_op(mc, ys[:, mc, :], ("ys", mc))
                            if mc >= 1:
                                ones_mm(mc - 1)
                ones_mm(7)
                post(par, g0 + 40)
                xk = [("xT", par, c) for c in range(8)]
                if not last:
                    if own:
                        P.dma("pool", ds_st[0], S["XT" + s][:, t0:t0 + TT].rearrange("(c p) n -> p c n", p=128), xT[par][:],
                              reads=xk, writes=[("scr", "XT" + s, t)])
                    pre_norm(par, 48)
                    for i in range(3):
                        if i == 0 and not own:
                            continue
                        w, wk = next_slice("win_%d" % i)
                        if i < 2:
                            for mc in range(8):
                                b, bk = bank()
                                self.mm(b[:, :], [(w[:, c, mc * 128:(mc + 1) * 128], hn[:, c, :]) for c in range(8)],
                                        [wk] + hnk, [bk])
                                self.copy(self.evac_eng(), hid[:, i * 8 + mc, :], b[:, :], [bk], [("hid", i * 8 + mc)])
                            dst = S[("Q" if i == 0 else "K") + s]
                            P.dma("pool", ds_st[1 + i], dst[:, t0:t0 + TT].rearrange("(c p) n -> p c n", p=128),
                                  hid[:, i * 8:(i + 1) * 8, :], reads=[("hid", i * 8 + c) for c in range(8)],
                                  writes=[("scr", dst.tensor.name)])
                        else:
                            vv = hid[:, 16:24, :].rearrange("p (j a) n -> p j (a n)", j=4)
                            for j in range(4):
                                for hf in range(2):
                                    b, bk = bank()
                                    self.mm(b[:, :], [(hn[:, c, j * 128:(j + 1) * 128], w[:, c, hf * 512:(hf + 1) * 512])
                                                      for c in range(8)], [wk] + hnk, [bk])
                                    self.copy(self.evac_eng(), vv[:, j, hf * 512:(hf + 1) * 512], b[:, :], [bk],
                                              [("hid", 16 + 2 * j + hf)])
                            P.dma("pool", ds_st[3], S["V" + s][t0:t0 + TT, :].rearrange("(j p) n -> p j n", p=128), vv,
                                  reads=[("hid", 16 + c) for c in range(8)], writes=[("scr", "V" + s)])
                else:
                    for j in range(4):
                        for hf in range(2):
                            b, bk = bank()
                            for cc in range(4):
                                c = hf * 4 + cc
                                P.op("pe", lambda e, b=b, c=c, cc=cc, j=j, par=par: e.transpose(
                                    b[:, cc * 128:(cc + 1) * 128], xT[par][:, c, j * 128:(j + 1) * 128], C["identf"][:]),
                                    [("xT", par, c), "identf"], [bk])
                            self.copy(self.evac_eng(), yout[:, j, hf * 512:(hf + 1) * 512], b[:, :], [bk], [("ys", 2 * j + hf)])
                    ydst = self.yA if s == "A" else self.yB
                    P.dma("pool", ds_st[4], ydst[t0:t0 + TT, :].rearrange("(j p) n -> p j n", p=128), yout,
                          reads=[("ys", c) for c in range(8)], writes=[("out", s)])
            assert state["used"] == len(flat)
            self.end_phase()


def _t5_bucket(rel):
    nb = 16
    max_exact = 8
    rel = np.asarray(rel, dtype=np.int64)
    ret = (rel > 0).astype(np.int64) * nb
    n = np.abs(rel)
    nf = np.maximum(n, 1).astype(np.float32)
    val = np.log(nf / np.float32(max_exact)) / np.float32(math.log(128 / max_exact)) * np.float32(nb - max_exact)
    large = max_exact + val.astype(np.float32).astype(np.int32).astype(np.int64)
    large = np.minimum(large, nb - 1)
    return ret + np.where(n < max_exact, n, large)


def _onehot(sign):
    m = np.arange(GL)
    rel = sign * (639 - m)
    b = _t5_bucket(rel)
    oh = np.zeros((32, GL), np.float32)
    oh[b, m] = 1.0
    return oh


def _rope_tables(pos):
    half = 32
    freqs = (np.float32(10000.0) ** (-np.arange(half, dtype=np.float32) / np.float32(half))).astype(np.float32)
    ang = (pos.astype(np.float32)[None, :] * freqs[:, None]).astype(np.float32)
    c = np.cos(ang).astype(np.float32)
    s = np.sin(ang).astype(np.float32)
    return np.ascontiguousarray(np.concatenate([c, c], 0)), np.ascontiguousarray(np.concatenate([s, s], 0))


_CACHE = {}
_DBG = None


def kernel(x_prompt, x_sample, mem_prompt, mem_sample, norm_gains, rel_bias_table,
           mla_w_in, mla_q_norm, mla_kv_norm, mla_w_uq, mla_w_ukv, mla_w_o,
           diff_w_in, diff_lambda, diff_subln, diff_w_o,
           xattn_mem_norm, xattn_w_q, xattn_w_kv, xattn_w_o, mlp_w1, mlp_w2):
    f = lambda a: np.ascontiguousarray(np.asarray(a, dtype=np.float32))
    x_prompt, x_sample, mem_prompt, mem_sample = f(x_prompt), f(x_sample), f(mem_prompt), f(mem_sample)
    NB, SB, _ = x_prompt.shape
    NA, SA, _ = x_sample.shape
    assert NA == 8 and NB == 4
    HB = SB // 2
    key = (SA, SB)
    if key not in _CACHE:
        _CACHE[key] = Builder(SA, SB).build()
    nc = _CACHE[key]

    def cols(v):
        v = f(v).reshape(-1, 128)
        return np.ascontiguousarray(v.T)

    shared = {
        "gcols": cols(f(norm_gains).reshape(-1)),
        "memg": cols(f(xattn_mem_norm).reshape(-1)),
        "qkvg": np.ascontiguousarray(np.concatenate([cols(f(mla_q_norm)[0]), cols(f(mla_kv_norm)[0])], 1)),
        "subg": cols(f(diff_subln)[0]),
        "lam": f(diff_lambda)[0].reshape(1, 256),
        "table": f(rel_bias_table),
        "ohA": _onehot(1),
        "ident": np.eye(128, dtype=np.float32),
        "anti": np.ascontiguousarray(np.eye(128, dtype=np.float32)[::-1]),
        "mla_w_in": f(mla_w_in)[0], "mla_w_uq": f(mla_w_uq)[0], "mla_w_ukv": f(mla_w_ukv)[0], "mla_w_o": f(mla_w_o)[0],
        "diff_w_in": f(diff_w_in)[0], "diff_w_o": f(diff_w_o)[0],
        "xattn_w_q": f(xattn_w_q), "xattn_w_kv": f(xattn_w_kv), "xattn_w_o": f(xattn_w_o),
        "mlp_w1": f(mlp_w1), "mlp_w2": f(mlp_w2),
    }
    cosA, sinA = _rope_tables(np.arange(SA))
    shared["cosA"], shared["sinA"] = cosA, sinA
    tabs = {}
    for p in range(2):
        pos = np.arange(SB) if p == 0 else (SB - 1 - np.arange(SB))
        tabs[p] = _rope_tables(pos) + (_onehot(1 if p == 0 else -1),)
    in_maps = []
    for c in range(8):
        b, p = c // 2, c % 2
        m = dict(shared)
        m["xA"] = x_sample[c]
        m["xB"] = x_prompt[b] if p == 0 else np.ascontiguousarray(x_prompt[b][::-1])
        m["memA"] = mem_sample[c]
        m["memB"] = mem_prompt[b]
        m["cosB"], m["sinB"], m["ohB"] = tabs[p]
        in_maps.append(m)
    res = run_bass_kernel_spmd(nc, in_maps, core_ids=list(range(8)))
    global _DBG
    _DBG = res.results
    y_sample = np.stack([np.asarray(res.results[c]["yA"], dtype=np.float32) for c in range(8)], 0)
    y_prompt = np.empty((NB, SB, D), np.float32)
    for c in range(8):
        b, p = c // 2, c % 2
        yb = np.asarray(res.results[c]["yB"], dtype=np.float32)
        if p == 0:
            y_prompt[b, :HB] = yb
        else:
            y_prompt[b, HB:] = yb[::-1]
    return (y_prompt, y_sample)
```
